# Optimizing a Trainium2 kernel written in Bass

```python
import math
import jax
import jax.numpy as jnp
from jax import lax
import numpy as np

D_MODEL = 1024
BATCH = 16
SEQ = 2048
DEPTH = 4
DEC_BATCH = 128
DEC_SEQ = 8
PAST_LEN = 8192
PAGE_SIZE = 128

N_BRANCH = 4
BRANCH_W = D_MODEL // 2
HEAD_DIM = 64
SC_W = BRANCH_W
SC_K = 3
MLA_HEADS = BRANCH_W // HEAD_DIM
MLA_NOPE = HEAD_DIM
MLA_ROPE = HEAD_DIM // 2
MLA_V = HEAD_DIM
MLA_QLORA = D_MODEL // 4
MLA_KVLORA = D_MODEL // 8
ROPE_THETA = 10000.0
Q_BLOCK = 128
GDN_HEADS = BRANCH_W // HEAD_DIM
GDN_DK = HEAD_DIM
GDN_DV = HEAD_DIM
GDN_K = 4
GDN_CHUNK = 64
GDN_QKV = GDN_HEADS * (2 * GDN_DK + GDN_DV)
LRU_W = BRANCH_W
LRU_BLOCKS = BRANCH_W // HEAD_DIM
LRU_BD = LRU_W // LRU_BLOCKS
LRU_K = 4
LRU_C = 8.0
FFN_HIDDEN = -(-8 * D_MODEL // (3 * 256)) * 256
IN_SIZES = (3 * SC_W, MLA_QLORA, MLA_KVLORA + MLA_ROPE, GDN_QKV, GDN_HEADS * GDN_DV, GDN_HEADS, GDN_HEADS, LRU_W, LRU_W)
IN_WIDTH = sum(IN_SIZES)
EPS = 1e-6
STATE_KEYS = ('ckv', 'kpe', 'sconv', 'gdn_conv', 'gdn', 'lru_conv', 'lru')

kernel_name = 'hybrid_conditioned_decoder_step'


def rmsnorm(x, g):
    xf = x.astype(jnp.float32)
    y = xf * lax.rsqrt(jnp.mean(xf * xf, axis=-1, keepdims=True) + EPS)
    return (y * g.astype(jnp.float32)).astype(x.dtype)


def l2norm(x):
    return x * lax.rsqrt(jnp.sum(x * x, axis=-1, keepdims=True) + EPS)


def causal_dwconv(u, buf, w, b=None):
    k_w = w.shape[0]
    t = u.shape[1]
    ext = jnp.concatenate([buf.astype(u.dtype), u], axis=1)
    out = w[0] * ext[:, 0:t]
    for j in range(1, k_w):
        out = out + w[j] * ext[:, j:j + t]
    if b is not None:
        out = out + b
    return out, ext[:, t:]


def rope(x, pos):
    half = x.shape[-1] // 2
    inv = ROPE_THETA ** (-jnp.arange(half, dtype=jnp.float32) / half)
    ang = pos.astype(jnp.float32)[:, None] * inv[None, :]
    shape = (1, pos.shape[0]) + (1,) * (x.ndim - 3) + (half,)
    cos = jnp.cos(ang).reshape(shape)
    sin = jnp.sin(ang).reshape(shape)
    xf = x.astype(jnp.float32)
    x1, x2 = xf[..., :half], xf[..., half:]
    return jnp.concatenate([x1 * cos - x2 * sin, x1 * sin + x2 * cos], axis=-1).astype(x.dtype)


def mla_attend_prompt(q_abs, q_pe, ckv, kpe):
    b, t, h, c = q_abs.shape
    nb = t // Q_BLOCK
    scale = (MLA_NOPE + MLA_ROPE) ** -0.5
    qa = q_abs.reshape(b, nb, Q_BLOCK, h, c).swapaxes(0, 1)
    qp = q_pe.reshape(b, nb, Q_BLOCK, h, MLA_ROPE).swapaxes(0, 1)
    kpos = jnp.arange(t)

    def block(args):
        i, qa_i, qp_i = args
        s = jnp.einsum('bqhc,bkc->bhqk', qa_i, ckv) + jnp.einsum('bqhr,bkr->bhqk', qp_i, kpe)
        s = s.astype(jnp.float32) * scale
        qpos = i * Q_BLOCK + jnp.arange(Q_BLOCK)
        s = jnp.where(kpos[None, :] <= qpos[:, None], s, -jnp.inf)
        p = jax.nn.softmax(s, axis=-1).astype(ckv.dtype)
        return jnp.einsum('bhqk,bkc->bqhc', p, ckv)

    o = lax.map(block, (jnp.arange(nb), qa, qp))
    return o.swapaxes(0, 1).reshape(b, t, h, c)


def mla_attend_sample(q_abs, q_pe, ckv_past, kpe_past, ckv_new, kpe_new):
    t = q_abs.shape[1]
    p_len = ckv_past.shape[1]
    scale = (MLA_NOPE + MLA_ROPE) ** -0.5
    s_past = jnp.einsum('bqhc,bkc->bhqk', q_abs, ckv_past) + jnp.einsum('bqhr,bkr->bhqk', q_pe, kpe_past)
    s_new = jnp.einsum('bqhc,bkc->bhqk', q_abs, ckv_new) + jnp.einsum('bqhr,bkr->bhqk', q_pe, kpe_new)
    causal = jnp.tril(jnp.ones((t, t), dtype=bool))
    s_new = jnp.where(causal, s_new.astype(jnp.float32) * scale, -jnp.inf)
    s = jnp.concatenate([s_past.astype(jnp.float32) * scale, s_new], axis=-1)
    p = jax.nn.softmax(s, axis=-1).astype(ckv_new.dtype)
    return (jnp.einsum('bhqk,bkc->bqhc', p[..., :p_len], ckv_past)
            + jnp.einsum('bhqk,bkc->bqhc', p[..., p_len:], ckv_new))


def gated_delta_rule(q, k, v, g, beta, s0):
    b, t, h, dk = q.shape
    dv = v.shape[-1]
    c = GDN_CHUNK
    f32 = jnp.float32
    q = l2norm(q.astype(f32)) * (dk ** -0.5)
    k = l2norm(k.astype(f32))
    v = v.astype(f32)
    pad = (-t) % c
    n = (t + pad) // c

    def to_chunks(x):
        x = jnp.pad(x, [(0, 0), (0, pad)] + [(0, 0)] * (x.ndim - 2))
        x = x.reshape((b, n, c) + x.shape[2:])
        return jnp.moveaxis(x, 1, 0).swapaxes(2, 3)

    qc, kc, vc = to_chunks(q), to_chunks(k), to_chunks(v)
    gc = jnp.cumsum(to_chunks(g.astype(f32)), axis=-1)
    bc = to_chunks(beta.astype(f32))
    tril = jnp.tril(jnp.ones((c, c), dtype=bool))
    strict = jnp.tril(jnp.ones((c, c), dtype=bool), -1)
    diff = gc[..., :, None] - gc[..., None, :]
    decay = jnp.where(tril, jnp.exp(jnp.where(tril, diff, 0.0)), 0.0)
    kb = kc * bc[..., None]
    a_mat = jnp.where(strict, jnp.einsum('nbhid,nbhjd->nbhij', kb, kc) * decay, 0.0)
    rhs = jnp.concatenate([vc * bc[..., None], kb * jnp.exp(gc)[..., None]], axis=-1)
    sol = lax.linalg.triangular_solve(jnp.eye(c, dtype=f32) + a_mat, rhs,
                                      left_side=True, lower=True, unit_diagonal=True)
    u, w = sol[..., :dv], sol[..., dv:]
    intra = jnp.where(tril, jnp.einsum('nbhid,nbhjd->nbhij', qc, kc) * decay, 0.0)

    def step(s_mat, xs):
        q_i, k_i, u_i, w_i, g_i, att_i = xs
        v_new = u_i - jnp.einsum('bhcd,bhde->bhce', w_i, s_mat)
        o_i = (jnp.einsum('bhcd,bhde->bhce', q_i * jnp.exp(g_i)[..., None], s_mat)
               + jnp.einsum('bhij,bhje->bhie', att_i, v_new))
        g_last = g_i[..., -1:]
        s_mat = (s_mat * jnp.exp(g_last)[..., None]
                 + jnp.einsum('bhcd,bhce->bhde', k_i * jnp.exp(g_last - g_i)[..., None], v_new))
        return s_mat, o_i

    s_fin, o = lax.scan(step, s0.astype(f32), (qc, kc, u, w, gc, intra))
    o = jnp.moveaxis(o.swapaxes(2, 3), 0, 1).reshape(b, n * c, h, dv)[:, :t]
    return o, s_fin


def linear_recurrence(a, bx, h0):
    bx = bx.at[:, 0].add(a[:, 0] * h0)

    def comb(l, r):
        return (l[0] * r[0], r[0] * l[1] + r[1])

    _, hs = lax.associative_scan(comb, (a, bx), axis=1)
    return hs, hs[:, -1]


def mixer_block(h, pos, st, p, past):
    b, t, _ = h.shape
    f32 = jnp.float32
    points = [int(v) for v in np.cumsum(IN_SIZES)[:-1]]
    sc_in, q_a, kv_a, gdn_qkv, gdn_z, gdn_a, gdn_b, lru_x, lru_g = jnp.split(h @ p['w_in'], points, axis=-1)

    b_gate, c_gate, x_t = jnp.split(sc_in, 3, axis=-1)
    conv_a, sc_buf = causal_dwconv(c_gate * x_t, st['sconv'], p['w_sc_conv'])
    y_a = b_gate * conv_a

    cq = rmsnorm(q_a, p['g_q_norm'])
    q = (cq @ p['w_qb']).reshape(b, t, MLA_HEADS, MLA_NOPE + MLA_ROPE)
    q_nope = q[..., :MLA_NOPE]
    q_pe = rope(q[..., MLA_NOPE:], pos)
    ckv = rmsnorm(kv_a[..., :MLA_KVLORA], p['g_kv_norm'])
    kpe = rope(kv_a[..., MLA_KVLORA:], pos)
    w_kb = p['w_kvb'][..., :MLA_NOPE]
    w_vb = p['w_kvb'][..., MLA_NOPE:]
    q_abs = jnp.einsum('bthd,chd->bthc', q_nope, w_kb)
    if past is None:
        o_lat = mla_attend_prompt(q_abs, q_pe, ckv, kpe)
    else:
        o_lat = mla_attend_sample(q_abs, q_pe, past[0], past[1], ckv, kpe)
    y_b = jnp.einsum('bthc,chd->bthd', o_lat, w_vb).reshape(b, t, MLA_HEADS * MLA_V)

    qkv_c, gdn_buf = causal_dwconv(gdn_qkv, st['gdn_conv'], p['w_gdn_conv'])
    qkv_c = jax.nn.silu(qkv_c)
    q_g, k_g, v_g = jnp.split(qkv_c, [GDN_HEADS * GDN_DK, 2 * GDN_HEADS * GDN_DK], axis=-1)
    q_g = q_g.reshape(b, t, GDN_HEADS, GDN_DK)
    k_g = k_g.reshape(b, t, GDN_HEADS, GDN_DK)
    v_g = v_g.reshape(b, t, GDN_HEADS, GDN_DV)
    beta = jax.nn.sigmoid(gdn_b.astype(f32))
    g_log = -jnp.exp(p['gdn_a_log'].astype(f32)) * jax.nn.softplus(gdn_a.astype(f32) + p['gdn_dt_bias'].astype(f32))
    o_c, s_gdn = gated_delta_rule(q_g, k_g, v_g, g_log, beta, st['gdn'])
    o_c = rmsnorm(o_c.astype(h.dtype), p['g_gdn_norm']) * jax.nn.silu(gdn_z.reshape(b, t, GDN_HEADS, GDN_DV))
    y_c = o_c.reshape(b, t, GDN_HEADS * GDN_DV)

    u, lru_buf = causal_dwconv(lru_x, st['lru_conv'], p['w_lru_conv'], p['b_lru_conv'])
    ub = u.reshape(b, t, LRU_BLOCKS, LRU_BD)
    r = jax.nn.sigmoid(jnp.einsum('btni,nij->btnj', ub, p['w_lru_gate_a']).reshape(b, t, LRU_W) + p['b_lru_gate_a'])
    ig = jax.nn.sigmoid(jnp.einsum('btni,nij->btnj', ub, p['w_lru_gate_x']).reshape(b, t, LRU_W) + p['b_lru_gate_x'])
    log_a = -LRU_C * r.astype(f32) * jax.nn.softplus(-p['lru_lambda'].astype(f32))
    a = jnp.exp(log_a)
    mult = jnp.sqrt(1.0 - jnp.exp(2.0 * log_a))
    mult = jnp.where((pos == 0)[None, :, None], 1.0, mult)
    hs, h_last = linear_recurrence(a, mult * (ig * u).astype(f32), st['lru'].astype(f32))
    y_d = hs.astype(h.dtype) * jax.nn.gelu(lru_g)

    ys = jnp.stack([y_a, y_b, y_c, y_d], axis=2)
    proj = jnp.einsum('btnc,ncd->btnd', ys, p['w_branch_out'])
    gates = jax.nn.sigmoid((h @ p['w_merge_gate']).reshape(b, t, N_BRANCH, D_MODEL))
    mix = jnp.einsum('btnd,btnd->btd', gates, proj) @ p['w_mix_out']
    new_st = {'ckv': ckv, 'kpe': kpe, 'sconv': sc_buf, 'gdn_conv': gdn_buf,
              'gdn': s_gdn.astype(h.dtype), 'lru_conv': lru_buf, 'lru': h_last.astype(h.dtype)}
    return mix, new_st


def trunk_layer(x, c, pos, st, p, past):
    mod = (jax.nn.silu(c) @ p['w_ada'] + p['b_ada'])[:, None, :]
    sh1, sc1, gt1, sh2, sc2, gt2 = jnp.split(mod, 6, axis=-1)
    hmix = rmsnorm(x, p['g_norm_mix']) * (1.0 + sc1) + sh1
    mix, new_st = mixer_block(hmix, pos, st, p, past)
    x = x + gt1 * mix
    hffn = rmsnorm(x, p['g_norm_ffn']) * (1.0 + sc2) + sh2
    gate, up = jnp.split(hffn @ p['w_ffn_in'], 2, axis=-1)
    x = x + gt2 * ((jax.nn.silu(gate) * up) @ p['w_ffn_out'])
    return x, new_st


def setup_inputs(seed: int = 0) -> dict:
    key = jax.random.key(seed)
    keys = jax.random.split(key, 48)
    f32 = jnp.float32
    cnt = [0]

    def nxt():
        k = keys[cnt[0]]
        cnt[0] += 1
        return k

    def nrm(shape, scale):
        return jax.random.normal(nxt(), shape, f32) * scale

    def gain(shape):
        return 1.0 + nrm(shape, 0.02)

    def unif(shape, lo, hi):
        return jax.random.uniform(nxt(), shape, f32, lo, hi)

    n_pages = PAST_LEN // PAGE_SIZE
    n_used = DEC_BATCH * n_pages
    n_pool = (5 * n_used) // 4
    page_table = jax.random.permutation(nxt(), n_pool)[:n_used].reshape(DEC_BATCH, n_pages).astype(jnp.int32)
    a0 = unif((DEPTH, LRU_W), 0.9, 0.999)
    s0 = a0 ** (1.0 / LRU_C)
    dt0 = unif((DEPTH, GDN_HEADS), 0.001, 0.1)
    return {
        'x_prompt': nrm((BATCH, SEQ, D_MODEL), 1.0),
        'x_sample': nrm((DEC_BATCH, DEC_SEQ, D_MODEL), 1.0),
        'c_prompt': nrm((BATCH, D_MODEL), 1.0),
        'c_sample': nrm((DEC_BATCH, D_MODEL), 1.0),
        'cache_mla_ckv': nrm((DEPTH, n_pool, PAGE_SIZE, MLA_KVLORA), 1.0),
        'cache_mla_kpe': nrm((DEPTH, n_pool, PAGE_SIZE, MLA_ROPE), 1.0),
        'page_table': page_table,
        'state_sconv': nrm((DEPTH, DEC_BATCH, SC_K - 1, SC_W), 1.0),
        'state_gdn_conv': nrm((DEPTH, DEC_BATCH, GDN_K - 1, GDN_QKV), 1.0),
        'state_gdn': nrm((DEPTH, DEC_BATCH, GDN_HEADS, GDN_DK, GDN_DV), 0.1),
        'state_lru_conv': nrm((DEPTH, DEC_BATCH, LRU_K - 1, LRU_W), 1.0),
        'state_lru': nrm((DEPTH, DEC_BATCH, LRU_W), 0.5),
        'w_ada': nrm((DEPTH, D_MODEL, 6 * D_MODEL), D_MODEL ** -0.5),
        'b_ada': nrm((DEPTH, 6 * D_MODEL), 0.02),
        'g_norm_mix': gain((DEPTH, D_MODEL)),
        'g_norm_ffn': gain((DEPTH, D_MODEL)),
        'w_in': nrm((DEPTH, D_MODEL, IN_WIDTH), D_MODEL ** -0.5),
        'w_sc_conv': nrm((DEPTH, SC_K, SC_W), SC_K ** -0.5),
        'g_q_norm': gain((DEPTH, MLA_QLORA)),
        'w_qb': nrm((DEPTH, MLA_QLORA, MLA_HEADS * (MLA_NOPE + MLA_ROPE)), MLA_QLORA ** -0.5),
        'g_kv_norm': gain((DEPTH, MLA_KVLORA)),
        'w_kvb': nrm((DEPTH, MLA_KVLORA, MLA_HEADS, MLA_NOPE + MLA_V), MLA_KVLORA ** -0.5),
        'w_gdn_conv': nrm((DEPTH, GDN_K, GDN_QKV), GDN_K ** -0.5),
        'gdn_a_log': jnp.log(unif((DEPTH, GDN_HEADS), 1.0, 16.0)),
        'gdn_dt_bias': dt0 + jnp.log(-jnp.expm1(-dt0)),
        'g_gdn_norm': gain((DEPTH, GDN_DV)),
        'w_lru_conv': nrm((DEPTH, LRU_K, LRU_W), LRU_K ** -0.5),
        'b_lru_conv': nrm((DEPTH, LRU_W), 0.02),
        'w_lru_gate_a': nrm((DEPTH, LRU_BLOCKS, LRU_BD, LRU_BD), LRU_BD ** -0.5),
        'b_lru_gate_a': nrm((DEPTH, LRU_W), 0.02),
        'w_lru_gate_x': nrm((DEPTH, LRU_BLOCKS, LRU_BD, LRU_BD), LRU_BD ** -0.5),
        'b_lru_gate_x': nrm((DEPTH, LRU_W), 0.02),
        'lru_lambda': jnp.log(s0) - jnp.log1p(-s0),
        'w_branch_out': nrm((DEPTH, N_BRANCH, BRANCH_W, D_MODEL), BRANCH_W ** -0.5),
        'w_merge_gate': nrm((DEPTH, D_MODEL, N_BRANCH * D_MODEL), D_MODEL ** -0.5),
        'w_mix_out': nrm((DEPTH, D_MODEL, D_MODEL), D_MODEL ** -0.5),
        'w_ffn_in': nrm((DEPTH, D_MODEL, 2 * FFN_HIDDEN), D_MODEL ** -0.5),
        'w_ffn_out': nrm((DEPTH, FFN_HIDDEN, D_MODEL), FFN_HIDDEN ** -0.5),
        'g_final': gain((D_MODEL,)),
    }


def reference(x_prompt, x_sample, c_prompt, c_sample, cache_mla_ckv, cache_mla_kpe, page_table,
              state_sconv, state_gdn_conv, state_gdn, state_lru_conv, state_lru,
              w_ada, b_ada, g_norm_mix, g_norm_ffn, w_in, w_sc_conv, g_q_norm, w_qb, g_kv_norm, w_kvb,
              w_gdn_conv, gdn_a_log, gdn_dt_bias, g_gdn_norm, w_lru_conv, b_lru_conv,
              w_lru_gate_a, b_lru_gate_a, w_lru_gate_x, b_lru_gate_x, lru_lambda,
              w_branch_out, w_merge_gate, w_mix_out, w_ffn_in, w_ffn_out, g_final):
    params = {'w_ada': w_ada, 'b_ada': b_ada, 'g_norm_mix': g_norm_mix, 'g_norm_ffn': g_norm_ffn,
              'w_in': w_in, 'w_sc_conv': w_sc_conv, 'g_q_norm': g_q_norm, 'w_qb': w_qb,
              'g_kv_norm': g_kv_norm, 'w_kvb': w_kvb, 'w_gdn_conv': w_gdn_conv, 'gdn_a_log': gdn_a_log,
              'gdn_dt_bias': gdn_dt_bias, 'g_gdn_norm': g_gdn_norm, 'w_lru_conv': w_lru_conv,
              'b_lru_conv': b_lru_conv, 'w_lru_gate_a': w_lru_gate_a, 'b_lru_gate_a': b_lru_gate_a,
              'w_lru_gate_x': w_lru_gate_x, 'b_lru_gate_x': b_lru_gate_x, 'lru_lambda': lru_lambda,
              'w_branch_out': w_branch_out, 'w_merge_gate': w_merge_gate, 'w_mix_out': w_mix_out,
              'w_ffn_in': w_ffn_in, 'w_ffn_out': w_ffn_out}
    b_p, s_p, _ = x_prompt.shape
    b_s, t_s, _ = x_sample.shape
    n_pages = page_table.shape[1]
    past_len = n_pages * cache_mla_ckv.shape[2]
    dt = x_prompt.dtype
    pos_p = jnp.arange(s_p, dtype=jnp.int32)
    pos_s = past_len + jnp.arange(t_s, dtype=jnp.int32)
    xp, xs = x_prompt, x_sample
    out_p = {k: [] for k in STATE_KEYS}
    out_s = {k: [] for k in STATE_KEYS}
    for l in range(DEPTH):
        pl = {k: v[l] for k, v in params.items()}
        st_p = {'sconv': jnp.zeros((b_p, SC_K - 1, SC_W), dt),
                'gdn_conv': jnp.zeros((b_p, GDN_K - 1, GDN_QKV), dt),
                'gdn': jnp.zeros((b_p, GDN_HEADS, GDN_DK, GDN_DV), dt),
                'lru_conv': jnp.zeros((b_p, LRU_K - 1, LRU_W), dt),
                'lru': jnp.zeros((b_p, LRU_W), dt)}
        xp, nst_p = trunk_layer(xp, c_prompt, pos_p, st_p, pl, None)
        st_s = {'sconv': state_sconv[l], 'gdn_conv': state_gdn_conv[l], 'gdn': state_gdn[l],
                'lru_conv': state_lru_conv[l], 'lru': state_lru[l]}
        past = (cache_mla_ckv[l][page_table].reshape(b_s, past_len, MLA_KVLORA),
                cache_mla_kpe[l][page_table].reshape(b_s, past_len, MLA_ROPE))
        xs, nst_s = trunk_layer(xs, c_sample, pos_s, st_s, pl, past)
        for k in STATE_KEYS:
            out_p[k].append(nst_p[k])
            out_s[k].append(nst_s[k])
    y_prompt = rmsnorm(xp, g_final)
    y_sample = rmsnorm(xs, g_final)
    p_ckv = jnp.stack(out_p['ckv'])
    p_kpe = jnp.stack(out_p['kpe'])
    p_sconv = jnp.stack(out_p['sconv'])
    p_gdn_conv = jnp.stack(out_p['gdn_conv'])
    p_gdn = jnp.stack(out_p['gdn'])
    p_lru_conv = jnp.stack(out_p['lru_conv'])
    p_lru = jnp.stack(out_p['lru'])
    s_ckv = jnp.stack(out_s['ckv'])
    s_kpe = jnp.stack(out_s['kpe'])
    s_sconv = jnp.stack(out_s['sconv'])
    s_gdn_conv = jnp.stack(out_s['gdn_conv'])
    s_gdn = jnp.stack(out_s['gdn'])
    s_lru_conv = jnp.stack(out_s['lru_conv'])
    s_lru = jnp.stack(out_s['lru'])
    return (y_prompt, y_sample, p_ckv, p_kpe, p_sconv, p_gdn_conv, p_gdn, p_lru_conv, p_lru,
            s_ckv, s_kpe, s_sconv, s_gdn_conv, s_gdn, s_lru_conv, s_lru)
```

```python
import os
import numpy as np
from contextlib import ExitStack
import concourse.bass as bass
import concourse.mybir as mybir
from concourse.bass_utils import run_bass_kernel_spmd

F32 = mybir.dt.float32
BF16 = mybir.dt.bfloat16
I32 = mybir.dt.int32
AF = mybir.ActivationFunctionType
ALU = mybir.AluOpType

EPOCH = int(os.environ.get("MK_EPOCH", "24000"))


class Dep:
    __slots__ = ("w", "r", "chan")

    def __init__(self):
        self.w = None
        self.r = {}
        self.chan = None


class Chan:
    def __init__(self, sem):
        self.sem = sem
        self.count = 0


class Sched:
    ENGS = ("pe", "act", "dve", "pool", "sp")

    def __init__(self, nc, es):
        self.nc = nc
        self.es = es
        self.streams = {e: [] for e in self.ENGS}
        self.count = {e: 0 for e in self.ENGS}
        self.known = {e: {} for e in self.ENGS}
        self.esems = {}
        self.nsem = 0
        self.targets = {e: set() for e in self.ENGS}

    def new_sem(self, name):
        self.nsem += 1
        return self.es.enter_context(self.nc.semaphore(name))

    def chan(self, name):
        return Chan(self.new_sem("c_" + name))

    def _esem(self, eng, epoch):
        k = (eng, epoch)
        if k not in self.esems:
            self.esems[k] = self.new_sem("e_%s_%d" % k)
        return self.esems[k]

    def _key(self, ev):
        if ev[0] == "D":
            return ("D", id(ev[1]))
        return ("E", ev[1])

    def emit(self, eng, fn, reads=(), writes=(), chan=None):
        need = {}

        def add(ev, kind):
            if ev is None:
                return
            if ev[0] == "E" and ev[1] == eng:
                if eng == "pe" or kind != "raw":
                    return
            key = self._key(ev)
            val = ev[2]
            if self.known[eng].get(key, 0) >= val:
                return
            if key not in need or need[key][2] < val:
                need[key] = ev

        for d in reads:
            add(d.w, "raw")
        for d in writes:
            add(d.w, "waw")
            for ev in d.r.values():
                add(ev, "war")
        for key, ev in need.items():
            self.known[eng][key] = ev[2]
            if ev[0] == "E":
                self.targets[ev[1]].add(ev[2])
        if chan is not None:
            chan.count += 16
            ev = ("D", chan, chan.count)
            rk = ("D", id(chan))
        else:
            self.count[eng] += 1
            ev = ("E", eng, self.count[eng])
            rk = ("E", eng)
        for d in reads:
            d.r[rk] = ev
        for d in writes:
            d.w = ev
            d.r = {}
        self.streams[eng].append((list(need.values()), fn, ev))
        return ev

    def replay(self, final_waits):
        nc = self.nc
        rank = {}
        for e in self.ENGS:
            rank[e] = {c: i + 1 for i, c in enumerate(sorted(self.targets[e]))}
            for ep in range((max(len(rank[e]), 1) - 1) // EPOCH + 1):
                self._esem(e, ep)
        S = self

        def semval(ev):
            if ev[0] == "D":
                return ev[1].sem, ev[2]
            c = rank[ev[1]][ev[2]]
            ep = (c - 1) // EPOCH
            return S._esem(ev[1], ep), c - ep * EPOCH

        block = self.es.enter_context(nc.Block())

        def run(engname, engobj):
            for waits, fn, ev in S.streams[engname]:
                for wev in waits:
                    sem, val = semval(wev)
                    engobj.wait_ge(sem, val)
                ins = fn(engobj)
                if ev[0] == "D":
                    ins.then_inc(ev[1].sem, 16)
                elif ev[2] in rank[engname]:
                    sem, _ = semval(ev)
                    ins.then_inc(sem, 1)
            if engname == "sp":
                for ev in final_waits:
                    sem, val = semval(ev)
                    engobj.wait_ge(sem, val)

        @block.sync
        def _(e):
            run("sp", e)

        @block.gpsimd
        def _(e):
            run("pool", e)

        @block.vector
        def _(e):
            run("dve", e)

        @block.scalar
        def _(e):
            run("act", e)

        @block.tensor
        def _(e):
            run("pe", e)


D = 1024
NCH = 8
DEPTH_FULL = 4
H = 8
FFN = 2816
NPAR = 176
P_BADA, P_GMIX, P_GFFN, P_SCW, P_GQ, P_GKV, P_GDNW, P_LRUW, P_LRUB, P_BA, P_BX, P_LAM, P_GGDN, P_HM, P_DTB, P_ALOG = (
    0, 48, 56, 64, 76, 78, 79, 127, 143, 147, 151, 155, 159, 160, 168, 169)
INCH = {"sc": (0, 12, 128), "qa": (12, 2, 128), "ckv": (14, 1, 128), "kpeA": (15, 1, 128), "kpeB": (16, 1, 128),
        "gqkv": (17, 12, 128), "gz": (29, 4, 128), "ga": (33, 1, 8), "gb": (34, 1, 8), "lx": (35, 4, 128), "lg": (39, 4, 128)}
NINCH = 43


class Cfg:
    def __init__(self, depth=4, seq=2048, nsp=2, nss=16, ts=8, npages=64, npool=10240, T=512):
        self.depth, self.seq, self.nsp, self.nss, self.ts, self.npages, self.npool, self.T = depth, seq, nsp, nss, ts, npages, npool, T
        self.ntokp = nsp * seq
        self.ntoks = nss * ts
        self.nseq = nsp + nss
        self.past = npages * 128


class KB:
    def __init__(self, nc, es):
        self.nc, self.es = nc, es
        self.S = Sched(nc, es)
        self.psr = 0

    def sb(self, name, shape, dt=F32):
        return self.es.enter_context(self.nc.sbuf_tensor("s_" + name, list(shape), dt))

    def mm(self, out, lhsT, rhs, start=True, stop=True, r=(), w=()):
        return self.S.emit("pe", lambda e: e.matmul(out, lhsT=lhsT, rhs=rhs, start=start, stop=stop), r, w)

    def act(self, out, in_, func, r=(), w=(), bias=None, scale=None):
        kw = {}
        if bias is not None:
            kw["bias"] = bias
        if scale is not None:
            kw["scale"] = scale
        return self.S.emit("act", lambda e: e.activation(out=out, in_=in_, func=func, **kw), r, w)

    def tt(self, out, in0, in1, op, r=(), w=(), eng="dve"):
        return self.S.emit(eng, lambda e: e.tensor_tensor(out=out, in0=in0, in1=in1, op=op), r, w)

    def ts(self, out, in0, s1, op0, s2=None, op1=None, r=(), w=(), eng="dve"):
        if op1 is None:
            return self.S.emit(eng, lambda e: e.tensor_scalar(out=out, in0=in0, scalar1=s1, scalar2=None, op0=op0), r, w)
        return self.S.emit(eng, lambda e: e.tensor_scalar(out=out, in0=in0, scalar1=s1, scalar2=s2, op0=op0, op1=op1), r, w)

    def stt(self, out, in0, scalar, in1, op0, op1, r=(), w=()):
        return self.S.emit("dve", lambda e: e.scalar_tensor_tensor(out=out, in0=in0, scalar=scalar, in1=in1, op0=op0, op1=op1), r, w)

    def cp(self, out, in_, r=(), w=(), eng="dve"):
        if eng == "act":
            return self.S.emit("act", lambda e: e.activation(out=out, in_=in_, func=AF.Copy), r, w)
        return self.S.emit(eng, lambda e: e.tensor_copy(out=out, in_=in_), r, w)

    def recip(self, out, in_, r=(), w=()):
        return self.S.emit("dve", lambda e: e.reciprocal(out=out, in_=in_), r, w)

    def memset(self, ap, val, w=(), eng="pool"):
        return self.S.emit(eng, lambda e: e.memset(ap, val), (), w)

    def scan(self, out, d0, d1, init, r=(), w=()):
        return self.S.emit("dve", lambda e: e.tensor_tensor_scan(out=out, data0=d0, data1=d1, initial=init, op0=ALU.mult, op1=ALU.add), r, w)

    def dma(self, q, out, in_, owner, r=(), w=()):
        if owner.chan is None:
            owner.chan = self.S.chan("d%d" % self.S.nsem)
        chan = owner.chan
        return self.S.emit(q, lambda e: e.dma_start(out=out, in_=in_, allow_slow_non_contiguous=True), r, w, chan=chan)

    def gather(self, out, in_, idx_ap, owner, r=(), w=()):
        if owner.chan is None:
            owner.chan = self.S.chan("g%d" % self.S.nsem)
        chan = owner.chan
        return self.S.emit("pool", lambda e: e.indirect_dma_start(out=out, out_offset=None, in_=in_,
                                                                   in_offset=bass.IndirectOffsetOnAxis(ap=idx_ap, axis=0)), r, w, chan=chan)


C_ID, C_BONES, C_NBS, C_NBT, C_OFFD, C_TRIU, C_EEXP, C_I8, C_IOTA, NCONST = 0, 128, 256, 384, 512, 640, 768, 1280, 1288, 1296
M_WQ, M_WKBT, M_WVB, M_LRUG = 0, 2048, 2560, 3072


def build_program(cfg):
    nc = bass.Bass("TRN2", target_bir_lowering=False)
    L, T, NSEQ, NSP, NSS, TS = cfg.depth, cfg.T, cfg.nseq, cfg.nsp, cfg.nss, cfg.ts
    NTOK = cfg.ntokp + cfg.ntoks
    NPG = cfg.npages

    def din(name, shape, dt=F32):
        return nc.dram_tensor(name, list(shape), dt, kind="ExternalInput").ap()

    def dout(name, shape, dt=F32):
        return nc.dram_tensor(name, list(shape), dt, kind="ExternalOutput").ap()

    xin = din("xin", [8, 128, NTOK])
    cT_d = din("cT", [128, 8, NSEQ])
    par_d = din("par", [128, L, NPAR])
    const_d = din("const", [128, NCONST])
    rope_d = din("rope", [128, 2, NTOK])
    wada_d = din("wada", [L, 12, 128, 4096])
    win_d = din("win", [L, 11, 128, 4096])
    wmg_d = din("wmg", [L, 8, 128, 4096])
    wbo_d = din("wbo", [L, 4, 128, 4096])
    wmo_d = din("wmo", [L, 2, 128, 4096])
    wfi_d = din("wfi", [L, 11, 128, 4096])
    wfo_d = din("wfo", [L, 8, 128, 22 * 128])
    wmisc_d = din("wmisc", [L, 128, 4096])
    gfin_d = din("gfin", [128, 8])
    gmask_d = din("gmask", [64, 12, 64])
    st_sconv_d = din("st_sconv", [L, 128, 4, NSS, 2])
    st_gconv_d = din("st_gconv", [L, 128, 12, NSS, 3])
    st_lconv_d = din("st_lconv", [L, 128, 4, NSS, 3])
    st_lru_d = din("st_lru", [L, 128, 4, NSS])
    st_gdn_d = din("st_gdn", [L, NSS, 128, 4, 64])
    pt_d = din("pt", [NSS, NPG], I32)
    cckv_d = din("cckv", [L, cfg.npool * 128, 128])
    ckpe_d = din("ckpe", [L, cfg.npool * 128, 32])

    y_d = dout("y", [8, 128, NTOK])
    o_ckv_d = dout("o_ckv", [L, 128, NTOK])
    o_kpe_d = dout("o_kpe", [L, 32, NTOK])
    o_sconv_d = dout("o_sconv", [L, 128, 4, NSEQ, 2])
    o_gconv_d = dout("o_gconv", [L, 128, 12, NSEQ, 3])
    o_lconv_d = dout("o_lconv", [L, 128, 4, NSEQ, 3])
    o_lru_d = dout("o_lru", [L, 128, 4, NSEQ])
    o_gdn_d = dout("o_gdn", [L, NSEQ, 128, 4, 64])
    xs_d = nc.dram_tensor("xs", [8, 128, NTOK], F32, kind="Internal").ap()

    es = ExitStack()
    with es:
        K = KB(nc, es)
        S = K.S
        sb = K.sb
        out_deps = []

        def dma_out(dst, src, r):
            K.dma("sp", dst, src, r[0], r=r)
            if r[0] not in out_deps:
                out_deps.append(r[0])

        const = sb("const", [128, NCONST]); d_const = Dep()
        par = sb("par", [128, L, NPAR]); d_par = Dep()
        cTs = sb("cTs", [128, 8, NSEQ]); d_cT = Dep()
        gfin = sb("gfin", [128, 8]); d_gfin = Dep()
        K.dma("sp", const[:], const_d, d_const, w=[d_const])
        K.dma("sp", par[:], par_d, d_par, w=[d_par])
        K.dma("sp", cTs[:], cT_d, d_cT, w=[d_cT])
        K.dma("sp", gfin[:], gfin_d, d_gfin, w=[d_gfin])
        identf = const[:, C_ID:C_ID + 128]
        ident_b = sb("ident_b", [128, 128], BF16)
        bones_b = sb("bones_b", [128, 128], BF16)
        ones_b = sb("ones_b", [128, 128], BF16)
        ones_f = sb("ones_f", [128, 128], F32)
        d_cb = Dep()
        K.cp(ident_b[:], identf, r=[d_const], w=[d_cb])
        K.cp(bones_b[:], const[:, C_BONES:C_BONES + 128], r=[d_const], w=[d_cb])
        K.memset(ones_b[:], 1.0, w=[d_cb])
        K.memset(ones_f[:], 1.0, w=[d_cb])
        nbs = const[:, C_NBS:C_NBS + 128]
        nbt = const[:, C_NBT:C_NBT + 128]
        offd = const[:, C_OFFD:C_OFFD + 128]
        triu = const[:, C_TRIU:C_TRIU + 128]
        eexp = const[0:8, C_EEXP:C_EEXP + 512].rearrange("p (a b) -> p a b", a=4)
        i8 = const[0:8, C_I8:C_I8 + 8]
        iota_f = const[:, C_IOTA:C_IOTA + 1]
        csil = sb("csil", [128, 8, NSEQ], BF16); d_csil = Dep()
        K.act(csil[:], cTs[:], AF.Silu, r=[d_cT], w=[d_csil])

        psb = [es.enter_context(nc.psum_tensor("ps%d" % i, [128, 512], F32)) for i in range(8)]
        d_ps = [Dep() for _ in range(8)]

        def ps_next():
            i = K.psr % 4
            K.psr += 1
            return psb[i], d_ps[i]

        NSLOT = 2
        wring = [sb("wr%d" % i, [128, 4096], BF16) for i in range(NSLOT)]
        d_wr = [Dep() for _ in range(NSLOT)]
        wstate = {"i": 0}

        def wload(src_ap, nel=4096, kview=None):
            i = wstate["i"] % NSLOT
            wstate["i"] += 1
            dst = wring[i][:, 0:nel] if kview is None else wring[i][:, 0:nel].rearrange("p (k n) -> p k n", k=kview)
            K.dma("pool", dst, src_ap, d_wr[i], w=[d_wr[i]])
            return wring[i], d_wr[i]

        modT = sb("modT", [128, 48, NSEQ]); d_mod = Dep()
        A1 = sb("A1", [128, 8, NSEQ]); A2 = sb("A2", [128, 8, NSEQ])
        negA = sb("negA", [8, 1]); lcl = sb("lcl", [128, 4]); lcl2 = sb("lcl2", [128, 4]); d_lp = Dep()
        hm_dummy = None

        def layer_setup(l):
            for b in range(12):
                wt, dw = wload(wada_d[l, b])
                wv = wt[:, :].rearrange("p (k n) -> p k n", k=8)
                for j4 in range(4):
                    j = b * 4 + j4
                    ps, dp = ps_next()
                    for kc in range(8):
                        K.mm(ps[:, 0:NSEQ], wv[:, kc, j4 * 128:(j4 + 1) * 128], csil[:, kc, :], start=(kc == 0), stop=(kc == 7),
                             r=[dw, d_csil], w=[dp])
                    K.act(modT[:, j, :], ps[:, 0:NSEQ], AF.Identity, bias=par[:, l, P_BADA + j:P_BADA + j + 1], r=[dp, d_par], w=[d_mod])
            for oc in range(8):
                K.ts(A1[:, oc, :], modT[:, 8 + oc, :], 1.0, ALU.add, par[:, l, P_GMIX + oc:P_GMIX + oc + 1], ALU.mult, r=[d_mod, d_par], w=[d_mod])
                K.ts(A2[:, oc, :], modT[:, 32 + oc, :], 1.0, ALU.add, par[:, l, P_GFFN + oc:P_GFFN + oc + 1], ALU.mult, r=[d_mod, d_par], w=[d_mod])
            K.act(negA[:], par[0:8, l, P_ALOG:P_ALOG + 1], AF.Exp, r=[d_par], w=[d_lp])
            K.ts(negA[:], negA[:], -1.0, ALU.mult, r=[d_lp], w=[d_lp])
            K.act(lcl[:], par[:, l, P_LAM:P_LAM + 4], AF.Exp, scale=-1.0, r=[d_par], w=[d_lp])
            K.act(lcl[:], lcl[:], AF.Ln, bias=1.0, r=[d_lp], w=[d_lp])
            K.ts(lcl2[:], lcl[:], -16.0, ALU.mult, r=[d_lp], w=[d_lp])
            K.ts(lcl[:], lcl[:], -8.0, ALU.mult, r=[d_lp], w=[d_lp])

        TM = T
        xt = sb("xt", [128, 8, TM]); d_x = Dep()
        hb = sb("hb", [128, 8, TM], BF16); d_h = Dep()
        sq = sb("sq", [128, 8, TM], BF16); d_sq = Dep()
        rstd = sb("rstd", [128, TM]); d_rstd = Dep()
        tmpf = sb("tmpf", [128, TM]); d_tmpf = Dep()
        tmpf2 = sb("tmpf2", [128, TM]); d_tmpf2 = Dep()
        ropet = sb("ropet", [128, 2, TM]); d_rope = Dep()
        d_xs_tiles = {}
        yb = [sb("yb%d" % n, [128, 4, TM], BF16) for n in range(4)]
        d_yb = [Dep() for _ in range(4)]
        qaqm = sb("qaqm", [128, 16, TM], BF16)
        qkzs = sb("qkzs", [128, 12, TM], F32)
        d_qabs, d_qm, d_qk, d_zs = Dep(), Dep(), Dep(), Dep()
        macc = qaqm[:, :, :].bitcast(F32).rearrange("p a t -> p (a t)").rearrange("p (c t) -> p c t", c=8)
        d_macc = Dep()
        maccb, d_maccb = sq, d_sq
        gsb = sb("gsb", [128, TM]); d_gsb = Dep()
        hid = qkzs[:, :, :].bitcast(BF16).rearrange("p a t -> p (a t)").rearrange("p (c t) -> p c t", c=24)
        d_hid = Dep()

        def bc(ap2, nseq, tps):
            return ap2.unsqueeze(2).to_broadcast([128, nseq, tps])

        def v3(ap2, nseq, tps):
            return ap2.rearrange("p (a b) -> p a b", a=nseq)

        def rmsnorm_mod(l, tc, Amod, shbase, gcol):
            Tt, nseq, tps, s0 = tc["T"], tc["nseq"], tc["tps"], tc["s0"]
            for c in range(8):
                K.act(sq[:, c, 0:Tt], xt[:, c, 0:Tt], AF.Square, r=[d_x], w=[d_sq])
            ps, dp = ps_next()
            for c in range(8):
                K.mm(ps[:, 0:Tt], ones_b[:], sq[:, c, 0:Tt], start=(c == 0), stop=(c == 7), r=[d_sq, d_cb], w=[dp])
            K.act(rstd[:, 0:Tt], ps[:, 0:Tt], AF.Sqrt, scale=1.0 / D, bias=1e-6, r=[dp], w=[d_rstd])
            K.recip(rstd[:, 0:Tt], rstd[:, 0:Tt], r=[d_rstd], w=[d_rstd])
            for c in range(8):
                K.tt(tmpf[:, 0:Tt], xt[:, c, 0:Tt], rstd[:, 0:Tt], ALU.mult, r=[d_x, d_rstd], w=[d_tmpf])
                if nseq == 1:
                    K.act(hb[:, c, 0:Tt], tmpf[:, 0:Tt], AF.Identity, scale=Amod[:, c, s0:s0 + 1], bias=modT[:, shbase + c, s0:s0 + 1],
                          r=[d_tmpf, d_mod], w=[d_h])
                else:
                    K.tt(v3(tmpf[:, 0:Tt], nseq, tps), v3(tmpf[:, 0:Tt], nseq, tps), bc(Amod[:, c, s0:s0 + nseq], nseq, tps), ALU.mult,
                         r=[d_tmpf, d_mod], w=[d_tmpf])
                    K.tt(v3(hb[:, c, 0:Tt], nseq, tps), v3(tmpf[:, 0:Tt], nseq, tps), bc(modT[:, shbase + c, s0:s0 + nseq], nseq, tps), ALU.add,
                         r=[d_tmpf, d_mod], w=[d_h])

        def resid_add(tc, ps, dp, oc, gtbase):
            Tt, nseq, tps, s0 = tc["T"], tc["nseq"], tc["tps"], tc["s0"]
            if nseq == 1:
                K.stt(xt[:, oc, 0:Tt], ps[:, 0:Tt], modT[:, gtbase + oc, s0:s0 + 1], xt[:, oc, 0:Tt], ALU.mult, ALU.add,
                      r=[dp, d_mod, d_x], w=[d_x])
            else:
                K.tt(v3(tmpf[:, 0:Tt], nseq, tps), v3(ps[:, 0:Tt], nseq, tps), bc(modT[:, gtbase + oc, s0:s0 + nseq], nseq, tps), ALU.mult,
                     r=[dp, d_mod], w=[d_tmpf])
                K.tt(xt[:, oc, 0:Tt], xt[:, oc, 0:Tt], tmpf[:, 0:Tt], ALU.add, r=[d_tmpf, d_x], w=[d_x])

        ctx = dict(nc=nc, K=K, S=S, sb=sb, cfg=cfg, par=par, d_par=d_par, const=const, d_const=d_const, psb=psb, d_ps=d_ps, ps_next=ps_next,
                   ident_b=ident_b, bones_b=bones_b, ones_b=ones_b, ones_f=ones_f, d_cb=d_cb, identf=identf, nbs=nbs, nbt=nbt, offd=offd,
                   triu=triu, eexp=eexp, i8=i8, iota_f=iota_f, hb=hb, d_h=d_h, yb=yb, d_yb=d_yb, ropet=ropet, d_rope=d_rope,
                   tmpf=tmpf, d_tmpf=d_tmpf, tmpf2=tmpf2, d_tmpf2=d_tmpf2, sq=sq, d_sq=d_sq, rstd=rstd, d_rstd=d_rstd,
                   negA=negA, lcl=lcl, lcl2=lcl2, d_lp=d_lp, dma_out=dma_out, wload=wload, v3=v3, bc=bc,
                   o_ckv_d=o_ckv_d, o_kpe_d=o_kpe_d, o_sconv_d=o_sconv_d, o_gconv_d=o_gconv_d, o_lconv_d=o_lconv_d, o_lru_d=o_lru_d, o_gdn_d=o_gdn_d,
                   st_sconv_d=st_sconv_d, st_gconv_d=st_gconv_d, st_lconv_d=st_lconv_d, st_lru_d=st_lru_d, st_gdn_d=st_gdn_d,
                   qaqm=qaqm, qkzs=qkzs, d_qabs=d_qabs, d_qm=d_qm, d_qk=d_qk, d_zs=d_zs,
                   gmask_d=gmask_d, pt_d=pt_d, cckv_d=cckv_d, ckpe_d=ckpe_d, wmisc_d=wmisc_d, win_d=win_d)
        mix = Mixers(ctx)
        if NSS > 0:
            mix.setup_pages()

        tiles = []
        for s in range(NSP):
            for t0 in range(0, cfg.seq, T):
                tiles.append(dict(kind="p", T=T, nseq=1, tps=T, s0=s, col0=s * cfg.seq + t0, pos0=t0, first=(t0 == 0), last=(t0 + T >= cfg.seq)))
        if NSS > 0:
            tiles.append(dict(kind="s", T=NSS * TS, nseq=NSS, tps=TS, s0=NSP, col0=cfg.ntokp, pos0=cfg.past, first=True, last=True))

        for l in range(L):
            layer_setup(l)
            for ti, tc in enumerate(tiles):
                Tt, c0 = tc["T"], tc["col0"]
                src = xin if l == 0 else xs_d
                dxs = d_xs_tiles.setdefault(ti, Dep())
                K.dma("sp", xt[:, :, 0:Tt], src[:, :, c0:c0 + Tt].rearrange("c p t -> p c t"), d_x, r=[dxs], w=[d_x])
                K.dma("sp", ropet[:, :, 0:Tt], rope_d[:, :, c0:c0 + Tt], d_rope, w=[d_rope])
                rmsnorm_mod(l, tc, A1, 0, P_GMIX)
                mix.run_layer_tile(l, tc)
                for n in range(4):
                    for half in range(2):
                        wbo, dwbo = wload(wbo_d[l, n].rearrange("p (k n) -> p k n", k=4)[:, :, half * 512:(half + 1) * 512], nel=2048, kview=4)
                        wbov = wbo[:, 0:2048].rearrange("p (k n) -> p k n", k=4)
                        wg, dwg = wload(wmg_d[l, n * 2 + half])
                        wgv = wg[:, :].rearrange("p (k n) -> p k n", k=8)
                        for o4 in range(4):
                            oc = half * 4 + o4
                            psg, dpg = ps_next()
                            for kc in range(8):
                                K.mm(psg[:, 0:Tt], wgv[:, kc, o4 * 128:(o4 + 1) * 128], hb[:, kc, 0:Tt], start=(kc == 0), stop=(kc == 7),
                                     r=[dwg, d_h], w=[dpg])
                            K.act(gsb[:, 0:Tt], psg[:, 0:Tt], AF.Sigmoid, r=[dpg], w=[d_gsb])
                            psp, dpp = ps_next()
                            for kc in range(4):
                                K.mm(psp[:, 0:Tt], wbov[:, kc, o4 * 128:(o4 + 1) * 128], yb[n][:, kc, 0:Tt], start=(kc == 0), stop=(kc == 3),
                                     r=[dwbo, d_yb[n]], w=[dpp])
                            if n == 0:
                                K.tt(macc[:, oc, 0:Tt], gsb[:, 0:Tt], psp[:, 0:Tt], ALU.mult, r=[d_gsb, dpp], w=[d_macc, d_qabs, d_qm])
                            else:
                                K.tt(gsb[:, 0:Tt], gsb[:, 0:Tt], psp[:, 0:Tt], ALU.mult, r=[d_gsb, dpp], w=[d_gsb])
                                if n < 3:
                                    K.tt(macc[:, oc, 0:Tt], macc[:, oc, 0:Tt], gsb[:, 0:Tt], ALU.add, r=[d_gsb, d_macc, d_qabs, d_qm], w=[d_macc, d_qabs, d_qm], eng="pool")
                                else:
                                    K.tt(maccb[:, oc, 0:Tt], macc[:, oc, 0:Tt], gsb[:, 0:Tt], ALU.add, r=[d_gsb, d_macc, d_qabs, d_qm], w=[d_maccb], eng="pool")
                for b in range(2):
                    wm, dwm = wload(wmo_d[l, b])
                    wmv = wm[:, :].rearrange("p (k n) -> p k n", k=8)
                    for o4 in range(4):
                        oc = b * 4 + o4
                        ps, dp = ps_next()
                        for kc in range(8):
                            K.mm(ps[:, 0:Tt], wmv[:, kc, o4 * 128:(o4 + 1) * 128], maccb[:, kc, 0:Tt], start=(kc == 0), stop=(kc == 7),
                                 r=[dwm, d_maccb], w=[dp])
                        resid_add(tc, ps, dp, oc, 16)
                rmsnorm_mod(l, tc, A2, 24, P_GFFN)
                for b in range(11):
                    wf, dwf = wload(wfi_d[l, b])
                    wfv = wf[:, :].rearrange("p (k n) -> p k n", k=8)
                    for jj in range(2):
                        j = b * 2 + jj
                        psg, dpg = ps_next()
                        for kc in range(8):
                            K.mm(psg[:, 0:Tt], wfv[:, kc, (2 * jj) * 128:(2 * jj + 1) * 128], hb[:, kc, 0:Tt], start=(kc == 0), stop=(kc == 7),
                                 r=[dwf, d_h], w=[dpg])
                        psu, dpu = ps_next()
                        for kc in range(8):
                            K.mm(psu[:, 0:Tt], wfv[:, kc, (2 * jj + 1) * 128:(2 * jj + 2) * 128], hb[:, kc, 0:Tt], start=(kc == 0), stop=(kc == 7),
                                 r=[dwf, d_h], w=[dpu])
                        K.act(gsb[:, 0:Tt], psg[:, 0:Tt], AF.Silu, r=[dpg], w=[d_gsb])
                        K.tt(hid[:, j, 0:Tt], gsb[:, 0:Tt], psu[:, 0:Tt], ALU.mult, r=[d_gsb, dpu], w=[d_hid, d_qk, d_zs])
                for oc in range(8):
                    wo, dwo = wload(wfo_d[l, oc], nel=22 * 128)
                    wov = wo[:, 0:22 * 128].rearrange("p (k n) -> p k n", k=22)
                    ps, dp = ps_next()
                    for kc in range(22):
                        K.mm(ps[:, 0:Tt], wov[:, kc, :], hid[:, kc, 0:Tt], start=(kc == 0), stop=(kc == 21), r=[dwo, d_hid, d_qk, d_zs], w=[dp])
                    resid_add(tc, ps, dp, oc, 40)
                if l < L - 1:
                    K.dma("sp", xs_d[:, :, c0:c0 + Tt].rearrange("c p t -> p c t"), xt[:, :, 0:Tt], d_x, r=[d_x], w=[dxs])
                else:
                    for c in range(8):
                        K.act(sq[:, c, 0:Tt], xt[:, c, 0:Tt], AF.Square, r=[d_x], w=[d_sq])
                    ps, dp = ps_next()
                    for c in range(8):
                        K.mm(ps[:, 0:Tt], ones_b[:], sq[:, c, 0:Tt], start=(c == 0), stop=(c == 7), r=[d_sq, d_cb], w=[dp])
                    K.act(rstd[:, 0:Tt], ps[:, 0:Tt], AF.Sqrt, scale=1.0 / D, bias=1e-6, r=[dp], w=[d_rstd])
                    K.recip(rstd[:, 0:Tt], rstd[:, 0:Tt], r=[d_rstd], w=[d_rstd])
                    for c in range(8):
                        K.stt(macc[:, c, 0:Tt], xt[:, c, 0:Tt], gfin[:, c:c + 1], rstd[:, 0:Tt], ALU.mult, ALU.mult,
                              r=[d_x, d_gfin, d_rstd], w=[d_macc, d_qabs, d_qm])
                    dma_out(y_d[:, :, c0:c0 + Tt].rearrange("c p t -> p c t"), macc[:, :, 0:Tt], [d_macc, d_qabs, d_qm])
        S.replay([("D", d.chan, d.chan.count) for d in out_deps])
    return nc


class Mixers:
    def __init__(self, ctx):
        self.__dict__.update(ctx)
        cfg, sb = self.cfg, self.sb
        T = cfg.T
        TM = max(T, cfg.nss * cfg.ts)
        self.TM = TM
        S = self.S
        self.uext = sb("uext", [128, 4, TM + 2 * max(1, cfg.nss)]); self.d_uext = Dep()
        self.gext = sb("gext", [128, 12, TM + 3 * max(1, cfg.nss)]); self.d_gext = Dep()
        self.lext = sb("lext", [128, 4, TM + 3 * max(1, cfg.nss)]); self.d_lext = Dep()
        self.cg = sb("cg", [128, TM]); self.d_cg = Dep()
        self.qa = sb("qa", [128, 2, TM]); self.d_qa = Dep()
        self.ckvr = sb("ckvr", [128, TM]); self.d_ckvr = Dep()
        self.kr = sb("kr", [128, TM]); self.d_kr = Dep()
        self.qk = self.qkzs[:, 0:8, :]
        self.vT = sb("vT", [128, 4, TM], BF16); self.d_vT = Dep()
        self.zs = self.qkzs[:, 8:12, :]
        self.ga = sb("ga", [8, TM]); self.gb = sb("gb", [8, TM]); self.d_gab = Dep()
        self.lgel = sb("lgel", [128, 4, TM], BF16); self.d_lgel = Dep()
        self.cq = sb("cq", [128, 2, TM], BF16); self.d_cq = Dep()
        self.qn2 = sb("qn2", [128, 4, TM], BF16); self.d_qn2 = Dep()
        self.qabs = self.qaqm[:, 0:8, :]
        self.qrot = sb("qrot", [128, 2, TM]); self.d_qrot = Dep()
        self.qm = self.qaqm[:, 8:16, :]
        NK = cfg.seq
        self.ckvT = sb("ckvT", [128, NK], BF16); self.d_ckvT = Dep()
        self.kpeR = sb("kpeR", [128, NK], BF16); self.d_kpeR = Dep()
        self.ckvtok = sb("ckvtok", [128, NK // 128, 128], BF16); self.d_ckvtok = Dep()
        self.ckvf = sb("ckvf", [128, TM]); self.d_ckvf = Dep()
        self.pT = [sb("pT%d" % i, [128, TM], BF16) for i in range(2)]; self.d_pT = [Dep(), Dep()]
        self.olat = sb("olat", [128, TM], BF16); self.d_olat = Dep()
        self.rsum = sb("rsum", [128, TM]); self.d_rsum = Dep()
        self.lu = sb("lu", [128, TM]); self.d_lu = Dep()
        self.lub = sb("lub", [128, TM], BF16); self.d_lub = Dep()
        self.la = sb("la", [128, TM]); self.d_la = Dep()
        self.lb = sb("lb", [128, TM]); self.d_lb = Dep()
        self.lhs_ = sb("lhs_", [128, TM]); self.d_lhs = Dep()
        self.lstate = sb("lstate", [128, 4, max(1, cfg.nss)]); self.d_lstate = Dep()
        self.qnT = sb("qnT", [128, 4, TM], BF16); self.knT = sb("knT", [128, 4, TM], BF16)
        self.kbT = sb("kbT", [128, 4, TM], BF16); self.d_gT = Dep()
        self.knTm = sb("knTm", [128, 2, 4, TM], BF16); self.qgT = sb("qgT", [128, 4, TM], BF16)
        self.wT = sb("wT", [128, 4, 64], BF16); self.d_wTm = Dep()
        self.Sbm = sb("Sbm", [128, 2, 4, 64], BF16)
        self.K.memset(self.knTm[:, :, :, :], 0.0, w=[self.d_gT])
        self.betaf = sb("betaf", [8, TM]); self.gcf = sb("gcf", [8, TM]); self.egcf = sb("egcf", [8, TM]); self.d_scal = Dep()
        self.gfm = sb("gfm", [8, TM]); self.d_gfm = Dep()
        self.oT = sb("oT", [128, 4, TM]); self.d_oT = Dep()
        self.Sst = sb("Sst", [128, 4, 64]); self.d_S = Dep()
        C = 64
        self.gd = {}
        for nm, shp, dt in [("tok", [64, 96], F32), ("rhsR", [8, 8, C], F32), ("d1", [64, 8, C], F32), ("d2", [64, 8, C], F32),
                            ("Ds", [64, 8, C], BF16), ("DTi", [64, 8, C], BF16), ("DTs", [64, 8, C], BF16),
                            ("Q0", [64, 8, C], BF16), ("P0", [64, 8, C], BF16), ("inT", [64, 8, C], BF16),
                            ("U0", [64, 8, C], BF16), ("U1", [64, 8, C], BF16), ("V0", [64, 8, C], BF16), ("V1", [64, 8, C], BF16),
                            ("O", [64, 8, C], BF16), ("OT", [64, 8, C], BF16), ("W1", [64, 8, C], BF16), ("W2", [64, 8, C], BF16),
                            ("vb", [64, 8, 64], BF16), ("kbg", [64, 8, 64], BF16), ("kw", [64, 8, 64], BF16),
                            ("bg", [64, 8], F32), ("ew", [64, 8], F32), ("egl", [128, 8], F32),
                            ("u", [64, 8, 64], F32), ("vn", [64, 8, 64], BF16)]:
            self.gd[nm] = (sb("g_" + nm, shp, dt), Dep())
        self.gmask = sb("gmask", [64, 12, 64], BF16); self.d_gmask = Dep()
        self.K.dma("pool", self.gmask[:, :, :], self.gmask_d, self.d_gmask, w=[self.d_gmask])
        if cfg.nss > 0:
            self.ptb = sb("ptb", [128, cfg.npages], I32); self.d_ptb = Dep()
            NG = 2
            self.pgc = [sb("pgc%d" % i, [128, 4, 128]) for i in range(NG)]
            self.pgk = [sb("pgk%d" % i, [128, 4, 32]) for i in range(NG)]
            self.d_pgc = [[Dep() for _ in range(4)] for _ in range(NG)]; self.d_pgk = [[Dep() for _ in range(4)] for _ in range(NG)]
            self.pgcb = [sb("pgcb%d" % i, [128, 4, 129], BF16) for i in range(NG)]; self.d_pgcb = [Dep() for _ in range(NG)]
            self.pgkb = [sb("pgkb%d" % i, [128, 4, 128], BF16) for i in range(NG)]; self.d_pgkb = [Dep() for _ in range(NG)]
            self.pcT = [sb("pcT%d" % i, [128, 512], BF16) for i in range(NG)]; self.d_pcT = [Dep() for _ in range(NG)]
            self.pkT = [sb("pkT%d" % i, [128, 512], BF16) for i in range(NG)]; self.d_pkT = [Dep() for _ in range(NG)]
            self.spT = [sb("spT%d" % i, [128, 256], BF16) for i in range(NG)]; self.d_spT = [Dep() for _ in range(NG)]
            for i in range(NG):
                self.K.memset(self.pgcb[i][:, :, 128:129], 1.0, w=[self.d_pgcb[i]])
            self.newtok = sb("newtok", [8, 129], BF16); self.d_newtok = Dep()
            self.K.memset(self.newtok[:, 128:129], 1.0, w=[self.d_newtok])
            self.so = sb("so", [64, 129]); self.d_so = Dep()
            self.sob = sb("sob", [64, 128], BF16); self.d_sob = Dep()
            self.solT = sb("solT", [128, 64], BF16); self.d_solT = Dep()
            self.pgi = 0

    def run_layer_tile(self, l, tc):
        K, par, d_par = self.K, self.par, self.d_par
        Tt, nseq, tps, s0 = tc["T"], tc["nseq"], tc["tps"], tc["s0"]
        hb, d_h = self.hb, self.d_h
        samp = tc["kind"] == "s"
        hist2 = 2 * nseq if samp else 2
        hist3 = 3 * nseq if samp else 3
        def extv(buf, c, h):
            return buf[:, c, 0:nseq * (h + tps)].rearrange("p (a b) -> p a b", a=nseq)
        self.extv = extv
        if samp:
            for (buf, dd, src, h, nchk) in ((self.uext, self.d_uext, self.st_sconv_d, 2, 4), (self.gext, self.d_gext, self.st_gconv_d, 3, 12),
                                            (self.lext, self.d_lext, self.st_lconv_d, 3, 4)):
                for c in range(nchk):
                    K.dma("sp", extv(buf, c, h)[:, :, 0:h], src[l, :, c, :, :], dd, w=[dd])
            K.dma("sp", self.lstate[:, :, 0:nseq], self.st_lru_d[l], self.d_lstate, w=[self.d_lstate])
        elif tc["first"]:
            for (buf, dd, h, nchk) in ((self.uext, self.d_uext, 2, 4), (self.gext, self.d_gext, 3, 12), (self.lext, self.d_lext, 3, 4)):
                K.memset(buf[:, :, 0:h], 0.0, w=[dd])
            K.memset(self.lstate[:, :, 0:1], 0.0, w=[self.d_lstate])
        else:
            for (buf, dd, h, nchk) in ((self.uext, self.d_uext, 2, 4), (self.gext, self.d_gext, 3, 12), (self.lext, self.d_lext, 3, 4)):
                K.cp(buf[:, :, 0:h], buf[:, :, tps:tps + h], r=[dd], w=[dd], eng="pool")

        wcur = {"b": -1, "wt": None, "dw": None}

        def proj(ci, M):
            b = ci // 4
            if b != wcur["b"]:
                wcur["wt"], wcur["dw"] = self.wload(self.win_d[l, b])
                wcur["b"] = b
            wv = wcur["wt"][:, :].rearrange("p (k n) -> p k n", k=8)
            ps, dp = self.ps_next()
            o = (ci % 4) * 128
            for kc in range(8):
                K.mm(ps[0:M, 0:Tt], wv[:, kc, o:o + M], hb[:, kc, 0:Tt], start=(kc == 0), stop=(kc == 7), r=[wcur["dw"], d_h], w=[dp])
            return ps, dp

        v3 = self.v3
        for j in range(4):
            ps, dp = proj(2 * j, 128)
            K.cp(self.cg[:, 0:Tt], ps[:, 0:Tt], r=[dp], w=[self.d_cg], eng="act")
            ps, dp = proj(2 * j + 1, 128)
            K.tt(extv(self.uext, j, 2)[:, :, 2:2 + tps], v3(self.cg[:, 0:Tt], nseq, tps), v3(ps[:, 0:Tt], nseq, tps), ALU.mult,
                 r=[self.d_cg, dp], w=[self.d_uext])
        for j in range(4):
            ps, dp = proj(8 + j, 128)
            e = extv(self.uext, j, 2)
            tv = v3(self.tmpf[:, 0:Tt], nseq, tps)
            K.ts(tv, e[:, :, 0:tps], par[:, l, P_SCW + j:P_SCW + j + 1], ALU.mult, r=[self.d_uext, d_par], w=[self.d_tmpf])
            for tap in (1, 2):
                K.stt(tv, e[:, :, tap:tap + tps], par[:, l, P_SCW + tap * 4 + j:P_SCW + tap * 4 + j + 1], tv, ALU.mult, ALU.add,
                      r=[self.d_uext, d_par, self.d_tmpf], w=[self.d_tmpf])
            K.tt(self.yb[0][:, j, 0:Tt], self.tmpf[:, 0:Tt], ps[:, 0:Tt], ALU.mult, r=[self.d_tmpf, dp], w=[self.d_yb[0]])
        for j in range(2):
            ps, dp = proj(12 + j, 128)
            K.cp(self.qa[:, j, 0:Tt], ps[:, 0:Tt], r=[dp], w=[self.d_qa], eng="act")
        ps, dp = proj(14, 128)
        K.cp(self.ckvr[:, 0:Tt], ps[:, 0:Tt], r=[dp], w=[self.d_ckvr], eng="act")
        ps, dp = proj(15, 128)
        K.tt(self.tmpf[:, 0:Tt], ps[:, 0:Tt], self.ropet[:, 0, 0:Tt], ALU.mult, r=[dp, self.d_rope], w=[self.d_tmpf])
        ps, dp = proj(16, 128)
        K.tt(self.tmpf2[:, 0:Tt], ps[:, 0:Tt], self.ropet[:, 1, 0:Tt], ALU.mult, r=[dp, self.d_rope], w=[self.d_tmpf2])
        K.tt(self.kr[:, 0:Tt], self.tmpf[:, 0:Tt], self.tmpf2[:, 0:Tt], ALU.add, r=[self.d_tmpf, self.d_tmpf2], w=[self.d_kr])
        for j in range(12):
            ps, dp = proj(17 + j, 128)
            e = extv(self.gext, j, 3)
            K.cp(e[:, :, 3:3 + tps], v3(ps[:, 0:Tt], nseq, tps), r=[dp], w=[self.d_gext], eng="act")
            tv = v3(self.tmpf[:, 0:Tt], nseq, tps)
            K.ts(tv, e[:, :, 0:tps], par[:, l, P_GDNW + j:P_GDNW + j + 1], ALU.mult, r=[self.d_gext, d_par], w=[self.d_tmpf])
            for tap in (1, 2, 3):
                K.stt(tv, e[:, :, tap:tap + tps], par[:, l, P_GDNW + tap * 12 + j:P_GDNW + tap * 12 + j + 1], tv, ALU.mult, ALU.add,
                      r=[self.d_gext, d_par, self.d_tmpf], w=[self.d_tmpf])
            if j < 8:
                K.act(self.qk[:, j, 0:Tt], self.tmpf[:, 0:Tt], AF.Silu, r=[self.d_tmpf], w=[self.d_qk])
            else:
                K.act(self.vT[:, j - 8, 0:Tt], self.tmpf[:, 0:Tt], AF.Silu, r=[self.d_tmpf], w=[self.d_vT])
        for j in range(4):
            ps, dp = proj(29 + j, 128)
            K.act(self.zs[:, j, 0:Tt], ps[:, 0:Tt], AF.Silu, r=[dp], w=[self.d_zs])
        ps, dp = proj(33, 8)
        K.cp(self.ga[:, 0:Tt], ps[0:8, 0:Tt], r=[dp], w=[self.d_gab], eng="act")
        ps, dp = proj(34, 8)
        K.cp(self.gb[:, 0:Tt], ps[0:8, 0:Tt], r=[dp], w=[self.d_gab], eng="act")
        for j in range(4):
            ps, dp = proj(35 + j, 128)
            K.cp(extv(self.lext, j, 3)[:, :, 3:3 + tps], v3(ps[:, 0:Tt], nseq, tps), r=[dp], w=[self.d_lext], eng="act")
        for j in range(4):
            ps, dp = proj(39 + j, 128)
            K.act(self.lgel[:, j, 0:Tt], ps[:, 0:Tt], AF.Gelu_apprx_tanh, r=[dp], w=[self.d_lgel])
        self.wm, self.dwm = self.wload(self.wmisc_d[l])
        if tc["last"]:
            sl = slice(s0, s0 + nseq)
            for c in range(4):
                self.dma_out(self.o_sconv_d[l, :, c, sl, :], extv(self.uext, c, 2)[:, :, tps:tps + 2], [self.d_uext])
                self.dma_out(self.o_lconv_d[l, :, c, sl, :], extv(self.lext, c, 3)[:, :, tps:tps + 3], [self.d_lext])
            for c in range(12):
                self.dma_out(self.o_gconv_d[l, :, c, sl, :], extv(self.gext, c, 3)[:, :, tps:tps + 3], [self.d_gext])
        skip = getattr(self.cfg, "skip", ())
        for nm, fn, ybi in (("lru", self.lru, 3), ("mla", self.mla, 1), ("gdn", self.gdn, 2)):
            if nm in skip:
                K.memset(self.yb[ybi][:, :, 0:Tt], 0.0, w=[self.d_yb[ybi]])
            else:
                fn(l, tc)

    def lru(self, l, tc):
        K, par, d_par, v3 = self.K, self.par, self.d_par, self.v3
        Tt, nseq, tps, s0 = tc["T"], tc["nseq"], tc["tps"], tc["s0"]
        lu, lub, la, lb, lhs_, tmpf = self.lu, self.lub, self.la, self.lb, self.lhs_, self.tmpf
        wg = self.wm[:, M_LRUG:M_LRUG + 1024].rearrange("p (c g m) -> p c g m", c=4, g=2)
        for c in range(4):
            e = self.extv(self.lext, c, 3)
            uv = v3(lu[:, 0:Tt], nseq, tps)
            K.ts(uv, e[:, :, 0:tps], par[:, l, P_LRUW + c:P_LRUW + c + 1], ALU.mult, par[:, l, P_LRUB + c:P_LRUB + c + 1], ALU.add,
                 r=[self.d_lext, d_par], w=[self.d_lu])
            for tap in (1, 2, 3):
                K.stt(uv, e[:, :, tap:tap + tps], par[:, l, P_LRUW + tap * 4 + c:P_LRUW + tap * 4 + c + 1], uv, ALU.mult, ALU.add,
                      r=[self.d_lext, d_par, self.d_lu], w=[self.d_lu])
            K.cp(lub[:, 0:Tt], lu[:, 0:Tt], r=[self.d_lu], w=[self.d_lub], eng="act")
            ps, dp = self.ps_next()
            K.mm(ps[:, 0:Tt], wg[:, c, 0, :], lub[:, 0:Tt], r=[self.dwm, self.d_lub], w=[dp])
            K.act(la[:, 0:Tt], ps[:, 0:Tt], AF.Sigmoid, bias=par[:, l, P_BA + c:P_BA + c + 1], r=[dp, d_par], w=[self.d_la])
            ps, dp = self.ps_next()
            K.mm(ps[:, 0:Tt], wg[:, c, 1, :], lub[:, 0:Tt], r=[self.dwm, self.d_lub], w=[dp])
            K.act(lb[:, 0:Tt], ps[:, 0:Tt], AF.Sigmoid, bias=par[:, l, P_BX + c:P_BX + c + 1], r=[dp, d_par], w=[self.d_lb])
            K.act(tmpf[:, 0:Tt], la[:, 0:Tt], AF.Exp, scale=self.lcl2[:, c:c + 1], r=[self.d_la, self.d_lp], w=[self.d_tmpf])
            K.act(la[:, 0:Tt], la[:, 0:Tt], AF.Exp, scale=self.lcl[:, c:c + 1], r=[self.d_la, self.d_lp], w=[self.d_la])
            K.ts(tmpf[:, 0:Tt], tmpf[:, 0:Tt], 1.0, ALU.min, r=[self.d_tmpf], w=[self.d_tmpf])
            K.act(tmpf[:, 0:Tt], tmpf[:, 0:Tt], AF.Sqrt, scale=-1.0, bias=1.0, r=[self.d_tmpf], w=[self.d_tmpf])
            if tc["kind"] == "p" and tc["first"]:
                K.memset(tmpf[:, 0:1], 1.0, w=[self.d_tmpf], eng="dve")
            K.tt(lb[:, 0:Tt], lb[:, 0:Tt], lu[:, 0:Tt], ALU.mult, r=[self.d_lb, self.d_lu], w=[self.d_lb])
            K.tt(lb[:, 0:Tt], lb[:, 0:Tt], tmpf[:, 0:Tt], ALU.mult, r=[self.d_lb, self.d_tmpf], w=[self.d_lb])
            for b in range(nseq):
                seg = slice(b * tps, (b + 1) * tps)
                K.scan(lhs_[:, seg], la[:, seg], lb[:, seg], self.lstate[:, c, b:b + 1], r=[self.d_la, self.d_lb, self.d_lstate], w=[self.d_lhs])
            K.cp(self.lstate[:, c, 0:nseq], v3(lhs_[:, 0:Tt], nseq, tps)[:, :, tps - 1], r=[self.d_lhs], w=[self.d_lstate])
            K.tt(self.yb[3][:, c, 0:Tt], lhs_[:, 0:Tt], self.lgel[:, c, 0:Tt], ALU.mult, r=[self.d_lhs, self.d_lgel], w=[self.d_yb[3]])
        if tc["last"]:
            self.dma_out(self.o_lru_d[l, :, :, s0:s0 + nseq], self.lstate[:, :, 0:nseq], [self.d_lstate])

    def mla(self, l, tc):
        K, par, d_par = self.K, self.par, self.d_par
        Tt, nseq, tps, s0, c0 = tc["T"], tc["nseq"], tc["tps"], tc["s0"], tc["col0"]
        sq, rstd, tmpf, tmpf2 = self.sq, self.rstd, self.tmpf, self.tmpf2
        wm, dwm = self.wm, self.dwm
        for j in range(2):
            K.act(sq[:, j, 0:Tt], self.qa[:, j, 0:Tt], AF.Square, r=[self.d_qa], w=[self.d_sq])
        ps, dp = self.ps_next()
        for j in range(2):
            K.mm(ps[:, 0:Tt], self.ones_b[:], sq[:, j, 0:Tt], start=(j == 0), stop=(j == 1), r=[self.d_sq, self.d_cb], w=[dp])
        K.act(rstd[:, 0:Tt], ps[:, 0:Tt], AF.Sqrt, scale=1.0 / 256, bias=1e-6, r=[dp], w=[self.d_rstd])
        K.recip(rstd[:, 0:Tt], rstd[:, 0:Tt], r=[self.d_rstd], w=[self.d_rstd])
        for j in range(2):
            K.stt(self.cq[:, j, 0:Tt], self.qa[:, j, 0:Tt], par[:, l, P_GQ + j:P_GQ + j + 1], rstd[:, 0:Tt], ALU.mult, ALU.mult,
                  r=[self.d_qa, d_par, self.d_rstd], w=[self.d_cq])
        K.act(sq[:, 2, 0:Tt], self.ckvr[:, 0:Tt], AF.Square, r=[self.d_ckvr], w=[self.d_sq])
        ps, dp = self.ps_next()
        K.mm(ps[:, 0:Tt], self.ones_b[:], sq[:, 2, 0:Tt], r=[self.d_sq, self.d_cb], w=[dp])
        K.act(rstd[:, 0:Tt], ps[:, 0:Tt], AF.Sqrt, scale=1.0 / 128, bias=1e-6, r=[dp], w=[self.d_rstd])
        K.recip(rstd[:, 0:Tt], rstd[:, 0:Tt], r=[self.d_rstd], w=[self.d_rstd])
        K.stt(self.ckvf[:, 0:Tt], self.ckvr[:, 0:Tt], par[:, l, P_GKV:P_GKV + 1], rstd[:, 0:Tt], ALU.mult, ALU.mult,
              r=[self.d_ckvr, d_par, self.d_rstd], w=[self.d_ckvf])
        self.dma_out(self.o_ckv_d[l, :, c0:c0 + Tt], self.ckvf[:, 0:Tt], [self.d_ckvf])
        self.dma_out(self.o_kpe_d[l, :, c0:c0 + Tt], self.kr[0:32, 0:Tt], [self.d_kr])
        wq = wm[:, M_WQ:M_WQ + 2048].rearrange("p (k n) -> p k n", k=2)
        wkbT = wm[:, M_WKBT:M_WKBT + 512].rearrange("p (a c) -> p a c", a=4)
        for pr in range(4):
            ps, dp = self.ps_next()
            for kc in range(2):
                K.mm(ps[:, 0:Tt], wq[:, kc, pr * 128:(pr + 1) * 128], self.cq[:, kc, 0:Tt], start=(kc == 0), stop=(kc == 1), r=[dwm, self.d_cq], w=[dp])
            K.cp(self.qn2[:, pr, 0:Tt], ps[:, 0:Tt], r=[dp], w=[self.d_qn2], eng="act")
        for h in range(8):
            pr, r0 = h // 2, 64 * (h % 2)
            ps, dp = self.ps_next()
            K.mm(ps[:, 0:Tt], wkbT[r0:r0 + 64, pr, :], self.qn2[r0:r0 + 64, pr, 0:Tt], r=[dwm, self.d_qn2], w=[dp])
            K.cp(self.qabs[:, h, 0:Tt], ps[:, 0:Tt], r=[dp], w=[self.d_qabs], eng=("act" if h % 2 else "dve"))
        for g in range(2):
            ps, dp = self.ps_next()
            for kc in range(2):
                K.mm(ps[:, 0:Tt], wq[:, kc, 512 + g * 128:512 + (g + 1) * 128], self.cq[:, kc, 0:Tt], start=(kc == 0), stop=(kc == 1),
                     r=[dwm, self.d_cq], w=[dp])
            K.tt(tmpf[:, 0:Tt], ps[:, 0:Tt], self.ropet[:, 0, 0:Tt], ALU.mult, r=[dp, self.d_rope], w=[self.d_tmpf])
            ps, dp = self.ps_next()
            for kc in range(2):
                K.mm(ps[:, 0:Tt], wq[:, kc, 768 + g * 128:768 + (g + 1) * 128], self.cq[:, kc, 0:Tt], start=(kc == 0), stop=(kc == 1),
                     r=[dwm, self.d_cq], w=[dp])
            K.tt(tmpf2[:, 0:Tt], ps[:, 0:Tt], self.ropet[:, 1, 0:Tt], ALU.mult, r=[dp, self.d_rope], w=[self.d_tmpf2])
            K.tt(self.qrot[:, g, 0:Tt], tmpf[:, 0:Tt], tmpf2[:, 0:Tt], ALU.add, r=[self.d_tmpf, self.d_tmpf2], w=[self.d_qrot])
            for h4 in range(4):
                h = 4 * g + h4
                K.ts(self.qm[:, h, 0:Tt], self.qrot[:, g, 0:Tt], par[:, l, P_HM + h:P_HM + h + 1], ALU.mult, r=[self.d_qrot, d_par], w=[self.d_qm])
        skip = getattr(self.cfg, "skip", ())
        if tc["kind"] == "p" and "attnp" not in skip:
            self.attn_prompt(l, tc)
        elif tc["kind"] == "s" and "attns" not in skip:
            self.attn_sample(l, tc)
        else:
            K.memset(self.yb[1][:, :, 0:Tt], 0.0, w=[self.d_yb[1]])

    def attn_prompt(self, l, tc):
        K = self.K
        Tt, pos0 = tc["T"], tc["pos0"]
        kt0 = pos0 // 128
        wvb = self.wm[:, M_WVB:M_WVB + 512]
        K.cp(self.ckvT[:, pos0:pos0 + Tt], self.ckvf[:, 0:Tt], r=[self.d_ckvf], w=[self.d_ckvT], eng="act")
        K.cp(self.kpeR[:, pos0:pos0 + Tt], self.kr[:, 0:Tt], r=[self.d_kr], w=[self.d_kpeR], eng="act")
        for i in range(Tt // 128):
            ps, dp = self.ps_next()
            K.mm(ps[:, 0:128], self.ckvT[:, pos0 + i * 128:pos0 + (i + 1) * 128], self.ident_b[:], r=[self.d_ckvT, self.d_cb], w=[dp])
            K.cp(self.ckvtok[:, kt0 + i, :], ps[:, 0:128], r=[dp], w=[self.d_ckvtok])
        scale = 96.0 ** -0.5
        nkt = (pos0 + Tt) // 128
        pi = 0
        for h in range(8):
            accO, dO = self.psb[4 + h % 2], self.d_ps[4 + h % 2]
            accS, dS = self.psb[6 + h % 2], self.d_ps[6 + h % 2]
            for kt in range(nkt):
                off = kt * 128 - pos0
                q0 = max(off, 0)
                ps, dp = self.ps_next()
                K.mm(ps[:, q0:Tt], self.ckvT[:, kt * 128:(kt + 1) * 128], self.qabs[:, h, q0:Tt], start=True, stop=False,
                     r=[self.d_ckvT, self.d_qabs], w=[dp])
                K.mm(ps[:, q0:Tt], self.kpeR[:, kt * 128:(kt + 1) * 128], self.qm[:, h, q0:Tt], start=False, stop=True,
                     r=[self.d_kpeR, self.d_qm], w=[dp])
                pt, dpt = self.pT[pi % 2], self.d_pT[pi % 2]
                pi += 1
                K.act(pt[:, q0:Tt], ps[:, q0:Tt], AF.Exp, scale=scale, r=[dp], w=[dpt])
                if off >= 0:
                    K.tt(pt[:, q0:q0 + 128], pt[:, q0:q0 + 128], self.triu, ALU.mult, r=[dpt, self.d_const], w=[dpt], eng="pool")
                K.mm(accO[:, q0:Tt], self.ckvtok[:, kt, :], pt[:, q0:Tt], start=(kt == 0), stop=(kt == nkt - 1), r=[self.d_ckvtok, dpt], w=[dO])
                K.mm(accS[:, q0:Tt], self.ones_b[:], pt[:, q0:Tt], start=(kt == 0), stop=(kt == nkt - 1), r=[self.d_cb, dpt], w=[dS])
            K.recip(self.rsum[:, 0:Tt], accS[:, 0:Tt], r=[dS], w=[self.d_rsum])
            K.tt(self.olat[:, 0:Tt], accO[:, 0:Tt], self.rsum[:, 0:Tt], ALU.mult, r=[dO, self.d_rsum], w=[self.d_olat])
            pr, r0 = h // 2, 64 * (h % 2)
            ps, dp = self.ps_next()
            K.mm(ps[r0:r0 + 64, 0:Tt], wvb[:, h * 64:(h + 1) * 64], self.olat[:, 0:Tt], r=[self.dwm, self.d_olat], w=[dp])
            K.cp(self.yb[1][r0:r0 + 64, pr, 0:Tt], ps[r0:r0 + 64, 0:Tt], r=[dp], w=[self.d_yb[1]], eng="act")

    def setup_pages(self):
        K, cfg = self.K, self.cfg
        self.idx_seq = [self.sb("idx_seq%d" % i, [128, cfg.npages], I32) for i in range(2)]
        self.idxf_seq = self.sb("idxf_seq", [128, cfg.npages])
        self.iota_l = self.sb("iota_l", [128, cfg.depth])
        self.d_idxseq = [Dep(), Dep()]
        self.d_idxf = Dep()
        self.d_iotal = Dep()
        for l in range(cfg.depth):
            K.ts(self.iota_l[:, l:l + 1], self.iota_f, float(l * cfg.npool * 128), ALU.add, r=[self.d_const], w=[self.d_iotal])

    def seq_pages(self, l, b):
        K = self.K
        i = b % 2
        K.dma("sp", self.ptb[:], self.pt_d[b:b + 1, :].partition_broadcast(128), self.d_ptb, w=[self.d_ptb])
        K.cp(self.idxf_seq[:], self.ptb[:], r=[self.d_ptb], w=[self.d_idxf])
        K.ts(self.idxf_seq[:], self.idxf_seq[:], 128.0, ALU.mult, self.iota_l[:, l:l + 1], ALU.add, r=[self.d_idxf, self.d_iotal], w=[self.d_idxf])
        K.cp(self.idx_seq[i][:], self.idxf_seq[:], r=[self.d_idxf], w=[self.d_idxseq[i]])
        return self.idx_seq[i], self.d_idxseq[i]

    def attn_sample(self, l, tc):
        K, cfg = self.K, self.cfg
        nseq, tps = tc["nseq"], tc["tps"]
        NPG = cfg.npages
        wvb = self.wm[:, M_WVB:M_WVB + 512]
        scale = 96.0 ** -0.5
        accO, dO = self.psb[4], self.d_ps[4]
        cflat = self.cckv_d.rearrange("l n c -> (l n) c")
        kflat = self.ckpe_d.rearrange("l n c -> (l n) c")
        ckvb, krb = self.pT[0], self.pT[1]
        Tt = tc["T"]
        K.cp(ckvb[:, 0:Tt], self.ckvf[:, 0:Tt], r=[self.d_ckvf], w=[self.d_pT[0]], eng="act")
        K.cp(krb[:, 0:Tt], self.kr[:, 0:Tt], r=[self.d_kr], w=[self.d_pT[1]], eng="act")
        for b in range(nseq):
            cs = slice(b * tps, (b + 1) * tps)
            qa_b = self.qabs[:, :, cs]
            qm_b = self.qm[:, :, cs]
            ps, dp = self.ps_next()
            K.mm(ps[0:tps, 0:128], ckvb[:, cs], self.ident_b[:], r=[self.d_pT[0], self.d_cb], w=[dp])
            K.cp(self.newtok[0:tps, 0:128], ps[0:tps, 0:128], r=[dp], w=[self.d_newtok])
            first = True
            idxs, d_idxs = self.seq_pages(l, b)
            for g0 in range(0, NPG, 4):
                i = self.pgi % 2
                self.pgi += 1
                ng = min(4, NPG - g0)
                for j in range(ng):
                    K.gather(self.pgc[i][:, j, :], cflat, idxs[:, g0 + j:g0 + j + 1], self.d_pgc[i][j], r=[d_idxs], w=[self.d_pgc[i][j]])
                    K.gather(self.pgk[i][:, j, :], kflat, idxs[:, g0 + j:g0 + j + 1], self.d_pgk[i][j], r=[d_idxs], w=[self.d_pgk[i][j]])
                K.cp(self.pgcb[i][:, 0:ng, 0:128], self.pgc[i][:, 0:ng, :], r=self.d_pgc[i][0:ng], w=[self.d_pgcb[i]])
                for j in range(ng):
                    K.cp(self.pgkb[i][:, j, :].rearrange("p (a b) -> p a b", a=4), self.pgk[i][:, j, :].unsqueeze(1).to_broadcast([128, 4, 32]),
                         r=[self.d_pgk[i][j]], w=[self.d_pgkb[i]], eng="act")
                psT, dpT_ = self.ps_next()
                for j in range(ng):
                    K.mm(psT[:, j * 128:(j + 1) * 128], self.pgcb[i][:, j, 0:128], self.ident_b[:], r=[self.d_pgcb[i], self.d_cb], w=[dpT_])
                K.cp(self.pcT[i][:, 0:ng * 128], psT[:, 0:ng * 128], r=[dpT_], w=[self.d_pcT[i]])
                psK, dpK = self.ps_next()
                for j in range(ng):
                    K.mm(psK[:, j * 128:(j + 1) * 128], self.pgkb[i][:, j, :], self.ident_b[:], r=[self.d_pgkb[i], self.d_cb], w=[dpK])
                K.cp(self.pkT[i][:, 0:ng * 128], psK[:, 0:ng * 128], r=[dpK], w=[self.d_pkT[i]], eng="act")
                psS, dpS = self.ps_next()
                for j in range(ng):
                    K.mm(psS[:, j * 64:(j + 1) * 64], self.pcT[i][:, j * 128:(j + 1) * 128], qa_b, start=True, stop=False,
                         r=[self.d_pcT[i], self.d_qabs], w=[dpS])
                    K.mm(psS[:, j * 64:(j + 1) * 64], self.pkT[i][:, j * 128:(j + 1) * 128], qm_b, start=False, stop=True,
                         r=[self.d_pkT[i], self.d_qm], w=[dpS])
                K.act(self.spT[i][:, 0:ng * 64], psS[:, 0:ng * 64], AF.Exp, scale=scale, r=[dpS], w=[self.d_spT[i]])
                for j in range(ng):
                    K.mm(accO[0:64, 0:129], self.spT[i][:, j * 64:(j + 1) * 64], self.pgcb[i][:, j, :], start=first, stop=False,
                         r=[self.d_spT[i], self.d_pgcb[i]], w=[dO])
                    first = False
            i = self.pgi % 2
            self.pgi += 1
            psS, dpS = self.ps_next()
            K.mm(psS[0:tps, 0:64], ckvb[:, cs], qa_b, start=True, stop=False, r=[self.d_pT[0], self.d_qabs], w=[dpS])
            K.mm(psS[0:tps, 0:64], krb[:, cs], qm_b, start=False, stop=True, r=[self.d_pT[1], self.d_qm], w=[dpS])
            K.act(self.spT[i][0:tps, 0:64], psS[0:tps, 0:64], AF.Exp, scale=scale, r=[dpS], w=[self.d_spT[i]])
            spv = self.spT[i][0:tps, 0:64].rearrange("p (a b) -> p a b", a=8)
            K.tt(spv, spv, self.triu[0:tps, 0:tps].unsqueeze(1).to_broadcast([tps, 8, tps]), ALU.mult, r=[self.d_spT[i], self.d_const], w=[self.d_spT[i]])
            K.mm(accO[0:64, 0:129], self.spT[i][0:tps, 0:64], self.newtok[0:tps, :], start=first, stop=True,
                 r=[self.d_spT[i], self.d_newtok], w=[dO])
            K.cp(self.so[:, :], accO[0:64, 0:129], r=[dO], w=[self.d_so])
            K.recip(self.so[:, 128:129], self.so[:, 128:129], r=[self.d_so], w=[self.d_so])
            K.ts(self.sob[:, :], self.so[:, 0:128], self.so[:, 128:129], ALU.mult, r=[self.d_so], w=[self.d_sob])
            ps, dp = self.ps_next()
            K.mm(ps[:, 0:64], self.sob[:, :], self.ident_b[0:64, 0:64], r=[self.d_sob, self.d_cb], w=[dp])
            K.cp(self.solT[:, :], ps[:, 0:64], r=[dp], w=[self.d_solT], eng="act")
            for h in range(8):
                pr, r0 = h // 2, 64 * (h % 2)
                ps, dp = self.ps_next()
                K.mm(ps[r0:r0 + 64, 0:tps], wvb[:, h * 64:(h + 1) * 64], self.solT[:, h * tps:(h + 1) * tps], r=[self.dwm, self.d_solT], w=[dp])
                K.cp(self.yb[1][r0:r0 + 64, pr, cs], ps[r0:r0 + 64, 0:tps], r=[dp], w=[self.d_yb[1]], eng="act")

    def gdn(self, l, tc):
        K, par, d_par, v3 = self.K, self.par, self.d_par, self.v3
        Tt, nseq, tps, s0 = tc["T"], tc["nseq"], tc["tps"], tc["s0"]
        samp = tc["kind"] == "s"
        C = tps if samp else 64
        NL = {64: 6, 8: 3}[C]
        sq, rstd, tmpf = self.sq, self.rstd, self.tmpf
        gd = self.gd
        gstop = getattr(self.cfg, 'gstop', 0)
        if gstop:
            K.memset(self.oT[:, :, 0:Tt], 0.0, w=[self.d_oT])
        for j in range(8):
            K.act(sq[:, j, 0:Tt], self.qk[:, j, 0:Tt], AF.Square, r=[self.d_qk], w=[self.d_sq])
            ps, dp = self.ps_next()
            K.mm(ps[:, 0:Tt], self.bones_b[:], sq[:, j, 0:Tt], r=[self.d_sq, self.d_cb], w=[dp])
            K.act(rstd[:, 0:Tt], ps[:, 0:Tt], AF.Sqrt, bias=1e-6, r=[dp], w=[self.d_rstd])
            K.recip(rstd[:, 0:Tt], rstd[:, 0:Tt], r=[self.d_rstd], w=[self.d_rstd])
            dst = self.qnT[:, j, 0:Tt] if j < 4 else self.knT[:, j - 4, 0:Tt]
            K.stt(dst, self.qk[:, j, 0:Tt], (0.125 if j < 4 else 1.0), rstd[:, 0:Tt], ALU.mult, ALU.mult,
                  r=[self.d_qk, self.d_rstd], w=[self.d_gT])
            if j >= 4:
                K.cp(self.knTm[0:64, 0, j - 4, 0:Tt], self.knT[0:64, j - 4, 0:Tt], r=[self.d_gT], w=[self.d_gT], eng="act")
                K.cp(self.knTm[64:128, 1, j - 4, 0:Tt], self.knT[64:128, j - 4, 0:Tt], r=[self.d_gT], w=[self.d_gT], eng="act")
        if gstop and gstop <= 1:
            return self._gdn_tail(l, tc)
        betaf, gcf, egcf, gfm = self.betaf, self.gcf, self.egcf, self.gfm
        d_scal = self.d_scal
        K.act(betaf[:, 0:Tt], self.gb[:, 0:Tt], AF.Sigmoid, r=[self.d_gab], w=[d_scal])
        K.act(gfm[:, 0:Tt], self.ga[:, 0:Tt], AF.Exp, bias=par[0:8, l, P_DTB:P_DTB + 1], r=[self.d_gab, d_par], w=[self.d_gfm])
        K.act(gfm[:, 0:Tt], gfm[:, 0:Tt], AF.Ln, bias=1.0, r=[self.d_gfm], w=[self.d_gfm])
        K.ts(gfm[:, 0:Tt], gfm[:, 0:Tt], self.negA[:, 0:1], ALU.mult, r=[self.d_gfm, self.d_lp], w=[self.d_gfm])
        nchunk = Tt // C
        for n in range(nchunk):
            cs = slice(n * C, (n + 1) * C)
            K.scan(gcf[:, cs], self.ones_f[0:8, 0:C], gfm[:, cs], 0.0, r=[self.d_gfm, self.d_cb], w=[d_scal])
        K.act(egcf[:, 0:Tt], gcf[:, 0:Tt], AF.Exp, r=[d_scal], w=[d_scal])
        if gstop and gstop <= 2:
            return self._gdn_tail(l, tc)
        for pr in range(4):
            ps, dp = self.ps_next()
            K.mm(ps[:, 0:Tt], self.eexp[:, pr, :], betaf[:, 0:Tt], r=[self.d_const, d_scal], w=[dp])
            K.tt(self.kbT[:, pr, 0:Tt], self.knT[:, pr, 0:Tt], ps[:, 0:Tt], ALU.mult, r=[self.d_gT, dp], w=[self.d_gT])
            ps, dp = self.ps_next()
            K.mm(ps[:, 0:Tt], self.eexp[:, pr, :], egcf[:, 0:Tt], r=[self.d_const, d_scal], w=[dp])
            K.tt(self.qgT[:, pr, 0:Tt], self.qnT[:, pr, 0:Tt], ps[:, 0:Tt], ALU.mult, r=[self.d_gT, dp], w=[self.d_gT])
        if gstop and gstop <= 3:
            return self._gdn_tail(l, tc)
        Sst, Sbm, d_S = self.Sst, self.Sbm, self.d_S
        ident_b, d_cb = self.ident_b, self.d_cb

        def G(nm):
            return gd[nm]

        for n in range(nchunk):
            cs = slice(n * C, (n + 1) * C)
            b = n if samp else 0
            if samp:
                K.dma("sp", Sst[:], self.st_gdn_d[l, b], d_S, w=[d_S])
                K.cp(Sbm[0:64, 0], Sst[0:64], r=[d_S], w=[d_S])
                K.cp(Sbm[64:128, 1], Sst[64:128], r=[d_S], w=[d_S], eng="act")
            elif tc["first"] and n == 0:
                K.memset(Sst[:], 0.0, w=[d_S], eng="dve")
                K.memset(Sbm[:, :, :, :], 0.0, w=[d_S], eng="dve")
            tok, d_tok = G("tok")
            ps, dp = self.ps_next()
            for qi, srcf in enumerate((betaf, gcf, egcf)):
                K.mm(ps[0:C, qi * 8:(qi + 1) * 8], srcf[:, cs], self.identf[0:8, 0:8], r=[d_scal, self.d_const], w=[dp])
            K.cp(tok[0:C, 0:24], ps[0:C, 0:24], r=[dp], w=[d_tok], eng="act")
            beta_t, gc_t, egc_t = tok[0:C, 0:8], tok[0:C, 8:16], tok[0:C, 16:24]
            rhsR, d_rhsR = G("rhsR")
            K.tt(rhsR[:, :, 0:C], gcf[:, cs].unsqueeze(1).to_broadcast([8, 8, C]), self.i8.unsqueeze(2).to_broadcast([8, 8, C]), ALU.mult,
                 r=[d_scal, self.d_const], w=[d_rhsR])
            psR, dpR = self.ps_next()
            K.mm(psR[:, 0:8 * C], self.ones_f[0:8, :], rhsR[:, :, 0:C], r=[d_cb, d_rhsR], w=[dpR])
            Rv = psR[:, 0:8 * C].rearrange("p (a b) -> p a b", a=8)
            d1, d_d1 = G("d1"); d2, d_d2 = G("d2"); Ds, d_Ds = G("Ds"); DTi, d_DTi = G("DTi"); DTs, d_DTs = G("DTs")
            K.tt(d1[0:C, :, 0:C], Rv[0:C], gc_t.unsqueeze(2).to_broadcast([C, 8, C]), ALU.subtract, r=[dpR, d_tok], w=[d_d1])
            K.tt(d2[0:C, :, 0:C], d1[0:C, :, 0:C], self.nbs[0:C, 0:C].unsqueeze(1).to_broadcast([C, 8, C]), ALU.max, r=[d_d1, self.d_const], w=[d_d2])
            K.act(Ds[0:C, :, 0:C], d2[0:C, :, 0:C], AF.Exp, scale=-1.0, r=[d_d2], w=[d_Ds])
            K.tt(d2[0:C, :, 0:C], d1[0:C, :, 0:C], self.nbt[0:C, 0:C].unsqueeze(1).to_broadcast([C, 8, C]), ALU.min, r=[d_d1, self.d_const, d_Ds], w=[d_d2])
            K.act(DTi[0:C, :, 0:C], d2[0:C, :, 0:C], AF.Exp, r=[d_d2], w=[d_DTi])
            K.tt(DTs[0:C, :, 0:C], DTi[0:C, :, 0:C], self.offd[0:C, 0:C].unsqueeze(1).to_broadcast([C, 8, C]), ALU.mult, r=[d_DTi, self.d_const], w=[d_DTs])
            egl, d_egl = G("egl")
            K.act(egl[:, :], Rv[:, :, C - 1], AF.Exp, r=[dpR], w=[d_egl])
            ew, d_ew = G("ew"); bg, d_bg = G("bg")
            K.tt(ew[0:C, :], Rv[0:C, :, C - 1], gc_t, ALU.subtract, r=[dpR, d_tok], w=[d_ew])
            K.act(ew[0:C, :], ew[0:C, :], AF.Exp, r=[d_ew], w=[d_ew])
            K.tt(bg[0:C, :], beta_t, egc_t, ALU.mult, r=[d_tok], w=[d_bg])
            if gstop and gstop <= 4:
                continue
            psA, dpA = self.ps_next()
            psAT, dpAT = self.ps_next()
            psQ, dpQ = self.ps_next()
            for h in range(8):
                pr, r0 = h // 2, 64 * (h % 2)
                hf = h % 2
                K.mm(psA[0:C, h * C:(h + 1) * C], self.kbT[:, pr, cs], self.knTm[:, hf, pr, cs], r=[self.d_gT], w=[dpA])
                K.mm(psAT[0:C, h * C:(h + 1) * C], self.knTm[:, hf, pr, cs], self.kbT[:, pr, cs], r=[self.d_gT], w=[dpAT])
                K.mm(psQ[0:C, h * C:(h + 1) * C], self.knTm[:, hf, pr, cs], self.qnT[:, pr, cs], r=[self.d_gT], w=[dpQ])
            if gstop and gstop <= 5:
                continue
            (Q0, dQ0), (P0, dP0) = G("Q0"), G("P0")
            inT, d_inT = G("inT")

            def pv(ps):
                return ps[0:C, 0:8 * C].rearrange("p (a b) -> p a b", a=8)

            def mk(li, tr):
                return self.gmask[0:C, (6 if tr else 0) + li, 0:C].unsqueeze(1).to_broadcast([C, 8, C])
            K.tt(Q0[0:C, :, 0:C], pv(psA), Ds[0:C, :, 0:C], ALU.mult, r=[dpA, d_Ds], w=[dQ0])
            K.tt(P0[0:C, :, 0:C], pv(psAT), DTs[0:C, :, 0:C], ALU.mult, r=[dpAT, d_DTs], w=[dP0])
            K.tt(inT[0:C, :, 0:C], pv(psQ), DTi[0:C, :, 0:C], ALU.mult, r=[dpQ, d_DTi], w=[d_inT])
            if gstop and gstop <= 6:
                continue
            U = [G("U0"), G("U1")]; V = [G("V0"), G("V1")]
            (O, dO_), (OT, dOT) = G("O"), G("OT")
            (W1, dW1), (W2, dW2) = G("W1"), G("W2")
            idb = self.identf[0:C, 0:C].unsqueeze(1).to_broadcast([C, 8, C])
            K.tt(O[0:C, :, 0:C], Q0[0:C, :, 0:C], mk(0, False), ALU.mult, r=[dQ0, self.d_gmask], w=[dO_])
            K.stt(U[0][0][0:C, :, 0:C], O[0:C, :, 0:C], -1.0, idb, ALU.mult, ALU.add, r=[dO_, self.d_const], w=[U[0][1]])
            K.tt(OT[0:C, :, 0:C], P0[0:C, :, 0:C], mk(0, True), ALU.mult, r=[dP0, self.d_gmask], w=[dOT])
            K.stt(V[0][0][0:C, :, 0:C], OT[0:C, :, 0:C], -1.0, idb, ALU.mult, ALU.add, r=[dOT, self.d_const], w=[V[0][1]])
            cur = 0
            levels = [sz for sz in (2, 4, 8, 16, 32) if sz < C]
            for li, sz in enumerate(levels):
                lastl = (li == len(levels) - 1)
                nxt = 1 - cur
                (Uc, dUc), (Vc, dVc) = U[cur], V[cur]
                (Un, dUn), (Vn, dVn) = U[nxt], V[nxt]
                K.tt(O[0:C, :, 0:C], Q0[0:C, :, 0:C], mk(li + 1, False), ALU.mult, r=[dQ0, self.d_gmask], w=[dO_])
                psw2, dpw2 = self.ps_next()
                for h in range(8):
                    K.mm(psw2[0:C, h * C:(h + 1) * C], O[0:C, h, 0:C], Vc[0:C, h, 0:C], r=[dO_, dVc], w=[dpw2])
                K.cp(W2[0:C, :, 0:C], pv(psw2), r=[dpw2], w=[dW2], eng="act")
                psv_, dpv_ = self.ps_next()
                for h in range(8):
                    K.mm(psv_[0:C, h * C:(h + 1) * C], Uc[0:C, h, 0:C], W2[0:C, h, 0:C], r=[dUc, dW2], w=[dpv_])
                K.tt(Vn[0:C, :, 0:C], Vc[0:C, :, 0:C], pv(psv_), ALU.subtract, r=[dVc, dpv_], w=[dVn])
                if not lastl:
                    K.tt(OT[0:C, :, 0:C], P0[0:C, :, 0:C], mk(li + 1, True), ALU.mult, r=[dP0, self.d_gmask], w=[dOT])
                    psw1, dpw1 = self.ps_next()
                    for h in range(8):
                        K.mm(psw1[0:C, h * C:(h + 1) * C], OT[0:C, h, 0:C], Uc[0:C, h, 0:C], r=[dOT, dUc], w=[dpw1])
                    K.cp(W1[0:C, :, 0:C], pv(psw1), r=[dpw1], w=[dW1], eng="act")
                    psu_, dpu_ = self.ps_next()
                    for h in range(8):
                        K.mm(psu_[0:C, h * C:(h + 1) * C], Vc[0:C, h, 0:C], W1[0:C, h, 0:C], r=[dVc, dW1], w=[dpu_])
                    K.tt(Un[0:C, :, 0:C], Uc[0:C, :, 0:C], pv(psu_), ALU.subtract, r=[dUc, dpu_], w=[dUn])
                cur = nxt
            TT, dTT = V[cur]
            if gstop and gstop <= 7:
                continue
            psk, dpk = self.ps_next()
            psv, dpv = self.ps_next()
            for pr in range(4):
                K.mm(psk[0:C, pr * 128:(pr + 1) * 128], self.knT[:, pr, cs], ident_b[:], r=[self.d_gT, d_cb], w=[dpk])
                K.mm(psv[0:C, pr * 128:(pr + 1) * 128], self.vT[:, pr, cs], ident_b[:], r=[self.d_vT, d_cb], w=[dpv])
            vb, d_vb = G("vb"); kbg, d_kbg = G("kbg"); kw, d_kw = G("kw")

            def p64(ps):
                return ps[0:C, 0:512].rearrange("p (a b) -> p a b", a=8)
            K.tt(vb[0:C], p64(psv), beta_t.unsqueeze(2).to_broadcast([C, 8, 64]), ALU.mult, r=[dpv, d_tok], w=[d_vb])
            K.tt(kbg[0:C], p64(psk), bg[0:C, :].unsqueeze(2).to_broadcast([C, 8, 64]), ALU.mult, r=[dpk, d_bg], w=[d_kbg])
            K.tt(kw[0:C], p64(psk), ew[0:C, :].unsqueeze(2).to_broadcast([C, 8, 64]), ALU.mult, r=[dpk, d_ew], w=[d_kw])
            if gstop and gstop <= 8:
                continue
            psu, dpu = self.ps_next()
            psw, dpw = self.ps_next()
            for h in range(8):
                pr, r0 = h // 2, 64 * (h % 2)
                K.mm(psu[0:C, h * 64:(h + 1) * 64], TT[0:C, h, 0:C], vb[0:C, h, :], r=[dTT, d_vb], w=[dpu])
                K.mm(psw[r0:r0 + 64, pr * C:(pr + 1) * C], kbg[0:C, h, :], TT[0:C, h, 0:C], r=[dTT, d_kbg], w=[dpw])
            u, d_u = G("u"); vn, d_vn = G("vn")
            wT, d_wT = self.wT, self.d_wTm
            K.cp(u[0:C], p64(psu), r=[dpu], w=[d_u], eng="act")
            K.cp(wT[:, :, 0:C], psw[:, 0:4 * C].rearrange("p (a b) -> p a b", a=4), r=[dpw], w=[d_wT])
            if gstop and gstop <= 9:
                continue
            pss, dps = self.ps_next()
            for h in range(8):
                pr, r0 = h // 2, 64 * (h % 2)
                K.mm(pss[0:C, h * 64:(h + 1) * 64], wT[:, pr, 0:C], Sbm[:, h % 2, pr, :], r=[d_wT, d_S], w=[dps])
            K.tt(vn[0:C], u[0:C], p64(pss), ALU.subtract, r=[d_u, dps], w=[d_vn])
            if gstop and gstop <= 10:
                continue
            pso, dpo = self.ps_next()
            psS, dpS = self.ps_next()
            for h in range(8):
                pr, r0 = h // 2, 64 * (h % 2)
                K.mm(pso[r0:r0 + 64, pr * C:(pr + 1) * C], Sbm[:, h % 2, pr, :], self.qgT[:, pr, cs], start=True, stop=False,
                     r=[d_S, self.d_gT], w=[dpo])
                K.mm(pso[r0:r0 + 64, pr * C:(pr + 1) * C], vn[0:C, h, :], inT[0:C, h, 0:C], start=False, stop=True, r=[d_vn, d_inT], w=[dpo])
                K.mm(psS[r0:r0 + 64, pr * 64:(pr + 1) * 64], kw[0:C, h, :], vn[0:C, h, :], r=[d_kw, d_vn], w=[dpS])
            K.cp(self.oT[:, :, cs], pso[:, 0:4 * C].rearrange("p (a b) -> p a b", a=4), r=[dpo], w=[self.d_oT], eng="act")
            eglv = egl[:, :].rearrange("p (a b) -> p a b", b=2)
            K.tt(Sst[0:64], Sst[0:64], eglv[0:64, :, 0].unsqueeze(2).to_broadcast([64, 4, 64]), ALU.mult, r=[d_S, d_egl], w=[d_S])
            K.tt(Sst[64:128], Sst[64:128], eglv[64:128, :, 1].unsqueeze(2).to_broadcast([64, 4, 64]), ALU.mult, r=[d_S, d_egl], w=[d_S])
            K.tt(Sst[:], Sst[:], psS[:, 0:256].rearrange("p (a b) -> p a b", a=4), ALU.add, r=[d_S, dpS], w=[d_S])
            K.cp(Sbm[0:64, 0], Sst[0:64], r=[d_S], w=[d_S], eng="act")
            K.cp(Sbm[64:128, 1], Sst[64:128], r=[d_S], w=[d_S])
            if samp or (tc["last"] and n == nchunk - 1):
                self.dma_out(self.o_gdn_d[l, s0 + b], Sst[:], [d_S])
        return self._gdn_tail(l, tc)

    def _gdn_tail(self, l, tc):
        K, par, d_par = self.K, self.par, self.d_par
        Tt = tc["T"]
        sq, rstd, tmpf = self.sq, self.rstd, self.tmpf
        for pr in range(4):
            K.act(sq[:, pr, 0:Tt], self.oT[:, pr, 0:Tt], AF.Square, r=[self.d_oT], w=[self.d_sq])
            ps, dp = self.ps_next()
            K.mm(ps[:, 0:Tt], self.bones_b[:], sq[:, pr, 0:Tt], r=[self.d_sq, self.d_cb], w=[dp])
            K.act(rstd[:, 0:Tt], ps[:, 0:Tt], AF.Sqrt, scale=1.0 / 64, bias=1e-6, r=[dp], w=[self.d_rstd])
            K.recip(rstd[:, 0:Tt], rstd[:, 0:Tt], r=[self.d_rstd], w=[self.d_rstd])
            K.tt(tmpf[:, 0:Tt], self.oT[:, pr, 0:Tt], rstd[:, 0:Tt], ALU.mult, r=[self.d_oT, self.d_rstd], w=[self.d_tmpf])
            K.stt(self.yb[2][:, pr, 0:Tt], tmpf[:, 0:Tt], par[:, l, P_GGDN:P_GGDN + 1], self.zs[:, pr, 0:Tt], ALU.mult, ALU.mult,
                  r=[self.d_tmpf, d_par, self.d_zs], w=[self.d_yb[2]])


def _blk(W, KC, NB):
    L, Kd, N = W.shape
    nb = N // NB
    return np.ascontiguousarray(W.reshape(L, KC, 128, nb, NB).transpose(0, 3, 2, 1, 4)).reshape(L, nb, 128, KC * NB)


def _fm(v, nch):
    lead = v.shape[:-1]
    a = v.reshape(lead + (nch, 128))
    return np.moveaxis(a, -1, 0)


def prep_shared(inp, cfg):
    L = cfg.depth
    f32 = np.float32
    sh = {}
    sh["wada"] = _blk(np.asarray(inp["w_ada"][:L], f32), 8, 512)
    win = np.asarray(inp["w_in"][:L], f32)
    wp = np.zeros((L, 1024, 44 * 128), f32)
    sc = win[:, :, 0:1536]
    bg_, cg_, xt_ = sc[:, :, 0:512], sc[:, :, 512:1024], sc[:, :, 1024:1536]
    for j in range(4):
        wp[:, :, (2 * j) * 128:(2 * j + 1) * 128] = cg_[:, :, j * 128:(j + 1) * 128]
        wp[:, :, (2 * j + 1) * 128:(2 * j + 2) * 128] = xt_[:, :, j * 128:(j + 1) * 128]
        wp[:, :, (8 + j) * 128:(9 + j) * 128] = bg_[:, :, j * 128:(j + 1) * 128]
    wp[:, :, 12 * 128:14 * 128] = win[:, :, 1536:1792]
    wp[:, :, 14 * 128:15 * 128] = win[:, :, 1792:1920]
    kpe = win[:, :, 1920:1952]
    kpes = np.concatenate([kpe[:, :, 16:32], kpe[:, :, 0:16]], axis=2)
    for rep in range(4):
        wp[:, :, 15 * 128 + rep * 32:15 * 128 + (rep + 1) * 32] = kpe
        wp[:, :, 16 * 128 + rep * 32:16 * 128 + (rep + 1) * 32] = kpes
    wp[:, :, 17 * 128:29 * 128] = win[:, :, 1952:3488]
    wp[:, :, 29 * 128:33 * 128] = win[:, :, 3488:4000]
    wp[:, :, 33 * 128:33 * 128 + 8] = win[:, :, 4000:4008]
    wp[:, :, 34 * 128:34 * 128 + 8] = win[:, :, 4008:4016]
    wp[:, :, 35 * 128:39 * 128] = win[:, :, 4016:4528]
    wp[:, :, 39 * 128:43 * 128] = win[:, :, 4528:5040]
    sh["win"] = _blk(wp, 8, 512)
    sh["wmg"] = _blk(np.asarray(inp["w_merge_gate"][:L], f32), 8, 512)
    wbo = np.asarray(inp["w_branch_out"][:L], f32)
    sh["wbo"] = np.ascontiguousarray(wbo.reshape(L, 4, 4, 128, 1024).transpose(0, 1, 3, 2, 4)).reshape(L, 4, 128, 4096)
    sh["wmo"] = _blk(np.asarray(inp["w_mix_out"][:L], f32), 8, 512)
    wfi = np.asarray(inp["w_ffn_in"][:L], f32)
    g = wfi[:, :, :FFN].reshape(L, 1024, 22, 1, 128)
    u = wfi[:, :, FFN:].reshape(L, 1024, 22, 1, 128)
    sh["wfi"] = _blk(np.concatenate([g, u], axis=3).reshape(L, 1024, 2 * FFN), 8, 512)
    sh["wfo"] = _blk(np.asarray(inp["w_ffn_out"][:L], f32), 22, 128)
    wm = np.zeros((L, 128, 4096), f32)
    wqb = np.asarray(inp["w_qb"][:L], f32).reshape(L, 256, 8, 96)
    nope = wqb[..., 0:64].reshape(L, 256, 512)
    pe = wqb[..., 64:96]
    peA = pe.reshape(L, 256, 256)
    peB = np.concatenate([pe[..., 16:32], pe[..., 0:16]], axis=-1).reshape(L, 256, 256)
    wq = np.concatenate([nope, peA, peB], axis=2)
    wm[:, :, M_WQ:M_WQ + 2048] = wq.reshape(L, 2, 128, 1024).transpose(0, 2, 1, 3).reshape(L, 128, 2048)
    wkvb = np.asarray(inp["w_kvb"][:L], f32)
    kb = wkvb[..., 0:64].reshape(L, 128, 4, 2, 64)
    wm[:, :, M_WKBT:M_WKBT + 512] = kb.transpose(0, 3, 4, 2, 1).reshape(L, 128, 512)
    wm[:, :, M_WVB:M_WVB + 512] = wkvb[..., 64:128].reshape(L, 128, 512)
    G = np.zeros((L, 128, 4, 2, 128), f32)
    for gi, nm in enumerate(("w_lru_gate_a", "w_lru_gate_x")):
        W = np.asarray(inp[nm][:L], f32)
        for c in range(4):
            for half in range(2):
                G[:, half * 64:(half + 1) * 64, c, gi, half * 64:(half + 1) * 64] = W[:, 2 * c + half]
    wm[:, :, M_LRUG:M_LRUG + 1024] = G.reshape(L, 128, 1024)
    sh["wmisc"] = wm
    par = np.zeros((128, L, NPAR), f32)

    def put(col, v, nch):
        par[:, :, col:col + nch] = _fm(np.asarray(v[:L], f32), nch)
    put(P_BADA, inp["b_ada"], 48)
    put(P_GMIX, inp["g_norm_mix"], 8)
    put(P_GFFN, inp["g_norm_ffn"], 8)
    w = np.asarray(inp["w_sc_conv"][:L], f32)
    for tap in range(3):
        par[:, :, P_SCW + tap * 4:P_SCW + tap * 4 + 4] = _fm(w[:, tap], 4)
    put(P_GQ, inp["g_q_norm"], 2)
    put(P_GKV, inp["g_kv_norm"], 1)
    w = np.asarray(inp["w_gdn_conv"][:L], f32)
    for tap in range(4):
        par[:, :, P_GDNW + tap * 12:P_GDNW + tap * 12 + 12] = _fm(w[:, tap], 12)
    w = np.asarray(inp["w_lru_conv"][:L], f32)
    for tap in range(4):
        par[:, :, P_LRUW + tap * 4:P_LRUW + tap * 4 + 4] = _fm(w[:, tap], 4)
    put(P_LRUB, inp["b_lru_conv"], 4)
    put(P_BA, inp["b_lru_gate_a"], 4)
    put(P_BX, inp["b_lru_gate_x"], 4)
    put(P_LAM, inp["lru_lambda"], 4)
    gg = np.asarray(inp["g_gdn_norm"][:L], f32)
    par[:, :, P_GGDN] = np.concatenate([gg, gg], axis=1).T
    for h in range(8):
        par[:, :, P_HM + h] = ((np.arange(128) // 32) == (h % 4)).astype(f32)[:, None]
    par[0:8, :, P_DTB] = np.asarray(inp["gdn_dt_bias"][:L], f32).T
    par[0:8, :, P_ALOG] = np.asarray(inp["gdn_a_log"][:L], f32).T
    sh["par"] = par
    sh["gfin"] = np.ascontiguousarray(_fm(np.asarray(inp["g_final"], f32), 8))
    cst = np.zeros((128, NCONST), f32)
    p = np.arange(128)[:, None]
    x = np.arange(128)[None, :]
    cst[:, C_ID:C_ID + 128] = (p == x)
    cst[:, C_BONES:C_BONES + 128] = ((p // 64) == (x // 64))
    cst[:, C_NBS:C_NBS + 128] = np.where(x < p, 0.0, 1e4)
    cst[:, C_NBT:C_NBT + 128] = np.where(x >= p, 0.0, -1e4)
    cst[:, C_OFFD:C_OFFD + 128] = (p != x)
    cst[:, C_TRIU:C_TRIU + 128] = (x >= p)
    ee = np.zeros((8, 4, 128), f32)
    for h in range(8):
        ee[h, h // 2, (h % 2) * 64:(h % 2) * 64 + 64] = 1.0
    cst[0:8, C_EEXP:C_EEXP + 512] = ee.reshape(8, 512)
    cst[0:8, C_I8:C_I8 + 8] = np.eye(8)
    cst[:, C_IOTA] = np.arange(128)
    sh["const"] = cst
    gm = np.zeros((64, 12, 64), f32)
    ii = np.arange(64)[:, None]
    jj = np.arange(64)[None, :]
    for li, sz in enumerate((1, 2, 4, 8, 16, 32)):
        m_ = ((ii // (2 * sz)) == (jj // (2 * sz))) & ((ii % (2 * sz)) >= sz) & ((jj % (2 * sz)) < sz)
        gm[:, li, :] = m_
        gm[:, 6 + li, :] = m_.T
    sh["gmask"] = gm
    return sh


def prep_core(inp, cfg, sh, pseqs, sseqs):
    f32 = np.float32
    L = cfg.depth
    m = dict(sh)
    xp = np.asarray(inp["x_prompt"], f32)[pseqs].reshape(-1, D)
    xs = np.asarray(inp["x_sample"], f32)[sseqs].reshape(-1, D)
    X = np.concatenate([xp, xs], axis=0)
    NTOK = X.shape[0]
    m["xin"] = np.ascontiguousarray(X.T).reshape(8, 128, NTOK)
    cc = np.concatenate([np.asarray(inp["c_prompt"], f32)[pseqs], np.asarray(inp["c_sample"], f32)[sseqs]], axis=0)
    m["cT"] = np.ascontiguousarray(cc.T.reshape(8, 128, -1).transpose(1, 0, 2))
    pos = np.concatenate([np.tile(np.arange(cfg.seq), len(pseqs)), np.tile(cfg.past + np.arange(cfg.ts), len(sseqs))]).astype(f32)
    inv = (np.float32(10000.0) ** (-np.arange(16, dtype=f32) / np.float32(16))).astype(f32)
    ang = (pos[:, None] * inv[None, :]).astype(f32)
    cos, sin = np.cos(ang).astype(f32), np.sin(ang).astype(f32)
    rope = np.zeros((128, 2, NTOK), f32)
    for p in range(128):
        f, half = p % 16, (p % 32) // 16
        rope[p, 0] = cos[:, f]
        rope[p, 1] = sin[:, f] if half == 1 else -sin[:, f]
    m["rope"] = rope
    ss = list(sseqs)
    m["st_sconv"] = np.ascontiguousarray(_fm(np.asarray(inp["state_sconv"], f32)[:L][:, ss], 4).transpose(0, 1, 4, 2, 3).transpose(1, 0, 2, 3, 4))
    m["st_gconv"] = np.ascontiguousarray(_fm(np.asarray(inp["state_gdn_conv"], f32)[:L][:, ss], 12).transpose(0, 1, 4, 2, 3).transpose(1, 0, 2, 3, 4))
    m["st_lconv"] = np.ascontiguousarray(_fm(np.asarray(inp["state_lru_conv"], f32)[:L][:, ss], 4).transpose(0, 1, 4, 2, 3).transpose(1, 0, 2, 3, 4))
    m["st_lru"] = np.ascontiguousarray(_fm(np.asarray(inp["state_lru"], f32)[:L][:, ss], 4).transpose(0, 1, 3, 2).transpose(1, 0, 2, 3))
    sg = np.asarray(inp["state_gdn"], f32)[:L][:, ss]
    m["st_gdn"] = np.ascontiguousarray(sg.reshape(L, len(ss), 4, 2, 64, 64).transpose(0, 1, 3, 4, 2, 5)).reshape(L, len(ss), 128, 4, 64)
    m["pt"] = np.ascontiguousarray(np.asarray(inp["page_table"], np.int32)[ss])
    m["cckv"] = np.asarray(inp["cache_mla_ckv"], f32)[:L].reshape(L, -1, 128)
    m["ckpe"] = np.asarray(inp["cache_mla_kpe"], f32)[:L].reshape(L, -1, 32)
    return m


def assemble(results, cfg, ncores):
    L, NSP, NSS, TS, SEQ = cfg.depth, cfg.nsp, cfg.nss, cfg.ts, cfg.seq
    ntp = cfg.ntokp

    def tokp(a):
        return a[:, :ntp].T.reshape(NSP, SEQ, -1)

    def toks(a):
        return a[:, ntp:].T.reshape(NSS, TS, -1)
    yp = np.concatenate([tokp(r["y"].reshape(D, -1)) for r in results], axis=0)
    ys = np.concatenate([toks(r["y"].reshape(D, -1)) for r in results], axis=0)
    outs_p, outs_s = {}, {}

    def both(key, fnp, fns):
        outs_p[key] = np.concatenate([fnp(r) for r in results], axis=1)
        outs_s[key] = np.concatenate([fns(r) for r in results], axis=1)
    both("ckv", lambda r: np.stack([tokp(r["o_ckv"][l]) for l in range(L)]), lambda r: np.stack([toks(r["o_ckv"][l]) for l in range(L)]))
    both("kpe", lambda r: np.stack([tokp(r["o_kpe"][l]) for l in range(L)]), lambda r: np.stack([toks(r["o_kpe"][l]) for l in range(L)]))

    def conv(a, nch):
        return a.transpose(0, 3, 4, 2, 1).reshape(a.shape[0], a.shape[3], a.shape[4], nch * 128)
    both("sconv", lambda r: conv(r["o_sconv"], 4)[:, :NSP], lambda r: conv(r["o_sconv"], 4)[:, NSP:])
    both("gdn_conv", lambda r: conv(r["o_gconv"], 12)[:, :NSP], lambda r: conv(r["o_gconv"], 12)[:, NSP:])
    both("lru_conv", lambda r: conv(r["o_lconv"], 4)[:, :NSP], lambda r: conv(r["o_lconv"], 4)[:, NSP:])

    def lru(a):
        return a.transpose(0, 3, 2, 1).reshape(a.shape[0], a.shape[3], 512)
    both("lru", lambda r: lru(r["o_lru"])[:, :NSP], lambda r: lru(r["o_lru"])[:, NSP:])

    def gdn(a):
        Ls, ns = a.shape[0], a.shape[1]
        return a.reshape(Ls, ns, 2, 64, 4, 64).transpose(0, 1, 4, 2, 3, 5).reshape(Ls, ns, 8, 64, 64)
    both("gdn", lambda r: gdn(r["o_gdn"])[:, :NSP], lambda r: gdn(r["o_gdn"])[:, NSP:])
    keys = ("ckv", "kpe", "sconv", "gdn_conv", "gdn", "lru_conv", "lru")
    out = [yp, ys] + [outs_p[k] for k in keys] + [outs_s[k] for k in keys]
    return tuple(np.ascontiguousarray(o, dtype=np.float32) for o in out)


def run(inp, cfg, ncores):
    sh = prep_shared(inp, cfg)
    in_maps = []
    for i in range(ncores):
        pseqs = list(range(i * cfg.nsp, (i + 1) * cfg.nsp))
        sseqs = list(range(i * cfg.nss, (i + 1) * cfg.nss))
        in_maps.append(prep_core(inp, cfg, sh, pseqs, sseqs))
    nc = build_program(cfg)
    res = run_bass_kernel_spmd(nc, in_maps, core_ids=list(range(ncores)))
    return assemble(res.results, cfg, ncores)


def kernel(**inputs):
    cfg = Cfg(depth=4, seq=2048, nsp=2, nss=16, ts=8, npages=64, npool=10240, T=256)
    return run(inputs, cfg, 8)
```

```python
import os
import numpy as np
from contextlib import ExitStack
import concourse.bass as bass
import concourse.mybir as mybir
from concourse.bass_utils import run_bass_kernel_spmd

F32 = mybir.dt.float32
BF16 = mybir.dt.bfloat16
I32 = mybir.dt.int32
AF = mybir.ActivationFunctionType
ALU = mybir.AluOpType

EPOCH = int(os.environ.get("MK_EPOCH", "24000"))


class Dep:
    __slots__ = ("w", "r", "chan")

    def __init__(self):
        self.w = None
        self.r = {}
        self.chan = None


class Chan:
    def __init__(self, sem):
        self.sem = sem
        self.count = 0


class Sched:
    ENGS = ("pe", "act", "dve", "pool", "sp")

    def __init__(self, nc, es):
        self.nc = nc
        self.es = es
        self.streams = {e: [] for e in self.ENGS}
        self.count = {e: 0 for e in self.ENGS}
        self.known = {e: {} for e in self.ENGS}
        self.esems = {}
        self.nsem = 0
        self.targets = {e: set() for e in self.ENGS}

    def new_sem(self, name):
        self.nsem += 1
        return self.es.enter_context(self.nc.semaphore(name))

    def chan(self, name):
        return Chan(self.new_sem("c_" + name))

    def _esem(self, eng, epoch):
        k = (eng, epoch)
        if k not in self.esems:
            self.esems[k] = self.new_sem("e_%s_%d" % k)
        return self.esems[k]

    def _key(self, ev):
        if ev[0] == "D":
            return ("D", id(ev[1]))
        return ("E", ev[1])

    def emit(self, eng, fn, reads=(), writes=(), chan=None):
        need = {}

        def add(ev, kind):
            if ev is None:
                return
            if ev[0] == "E" and ev[1] == eng:
                if eng == "pe" or kind != "raw":
                    return
            key = self._key(ev)
            val = ev[2]
            if self.known[eng].get(key, 0) >= val:
                return
            if key not in need or need[key][2] < val:
                need[key] = ev

        for d in reads:
            add(d.w, "raw")
        for d in writes:
            add(d.w, "waw")
            for ev in d.r.values():
                add(ev, "war")
        for key, ev in need.items():
            self.known[eng][key] = ev[2]
            if ev[0] == "E":
                self.targets[ev[1]].add(ev[2])
        if chan is not None:
            chan.count += 16
            ev = ("D", chan, chan.count)
            rk = ("D", id(chan))
        else:
            self.count[eng] += 1
            ev = ("E", eng, self.count[eng])
            rk = ("E", eng)
        for d in reads:
            d.r[rk] = ev
        for d in writes:
            d.w = ev
            d.r = {}
        self.streams[eng].append((list(need.values()), fn, ev))
        return ev

    def replay(self, final_waits):
        nc = self.nc
        rank = {}
        for e in self.ENGS:
            rank[e] = {c: i + 1 for i, c in enumerate(sorted(self.targets[e]))}
            for ep in range((max(len(rank[e]), 1) - 1) // EPOCH + 1):
                self._esem(e, ep)
        S = self

        def semval(ev):
            if ev[0] == "D":
                return ev[1].sem, ev[2]
            c = rank[ev[1]][ev[2]]
            ep = (c - 1) // EPOCH
            return S._esem(ev[1], ep), c - ep * EPOCH

        block = self.es.enter_context(nc.Block())

        def run(engname, engobj):
            for waits, fn, ev in S.streams[engname]:
                for wev in waits:
                    sem, val = semval(wev)
                    engobj.wait_ge(sem, val)
                ins = fn(engobj)
                if ev[0] == "D":
                    ins.then_inc(ev[1].sem, 16)
                elif ev[2] in rank[engname]:
                    sem, _ = semval(ev)
                    ins.then_inc(sem, 1)
            if engname == "sp":
                for ev in final_waits:
                    sem, val = semval(ev)
                    engobj.wait_ge(sem, val)

        @block.sync
        def _(e):
            run("sp", e)

        @block.gpsimd
        def _(e):
            run("pool", e)

        @block.vector
        def _(e):
            run("dve", e)

        @block.scalar
        def _(e):
            run("act", e)

        @block.tensor
        def _(e):
            run("pe", e)


D = 1024
NCH = 8
DEPTH_FULL = 4
H = 8
FFN = 2816
NPAR = 176
P_BADA, P_GMIX, P_GFFN, P_SCW, P_GQ, P_GKV, P_GDNW, P_LRUW, P_LRUB, P_BA, P_BX, P_LAM, P_GGDN, P_HM, P_DTB, P_ALOG = (
    0, 48, 56, 64, 76, 78, 79, 127, 143, 147, 151, 155, 159, 160, 168, 169)
INCH = {"sc": (0, 12, 128), "qa": (12, 2, 128), "ckv": (14, 1, 128), "kpeA": (15, 1, 128), "kpeB": (16, 1, 128),
        "gqkv": (17, 12, 128), "gz": (29, 4, 128), "ga": (33, 1, 8), "gb": (34, 1, 8), "lx": (35, 4, 128), "lg": (39, 4, 128)}
NINCH = 43


class Cfg:
    def __init__(self, depth=4, seq=2048, nsp=2, nss=16, ts=8, npages=64, npool=10240, T=512):
        self.depth, self.seq, self.nsp, self.nss, self.ts, self.npages, self.npool, self.T = depth, seq, nsp, nss, ts, npages, npool, T
        self.ntokp = nsp * seq
        self.ntoks = nss * ts
        self.nseq = nsp + nss
        self.past = npages * 128


class KB:
    def __init__(self, nc, es):
        self.nc, self.es = nc, es
        self.S = Sched(nc, es)
        self.psr = 0

    def sb(self, name, shape, dt=F32):
        return self.es.enter_context(self.nc.sbuf_tensor("s_" + name, list(shape), dt))

    def mm(self, out, lhsT, rhs, start=True, stop=True, r=(), w=()):
        return self.S.emit("pe", lambda e: e.matmul(out, lhsT=lhsT, rhs=rhs, start=start, stop=stop), r, w)

    def act(self, out, in_, func, r=(), w=(), bias=None, scale=None):
        kw = {}
        if bias is not None:
            kw["bias"] = bias
        if scale is not None:
            kw["scale"] = scale
        return self.S.emit("act", lambda e: e.activation(out=out, in_=in_, func=func, **kw), r, w)

    def tt(self, out, in0, in1, op, r=(), w=(), eng="dve"):
        return self.S.emit(eng, lambda e: e.tensor_tensor(out=out, in0=in0, in1=in1, op=op), r, w)

    def ts(self, out, in0, s1, op0, s2=None, op1=None, r=(), w=(), eng="dve"):
        if op1 is None:
            return self.S.emit(eng, lambda e: e.tensor_scalar(out=out, in0=in0, scalar1=s1, scalar2=None, op0=op0), r, w)
        return self.S.emit(eng, lambda e: e.tensor_scalar(out=out, in0=in0, scalar1=s1, scalar2=s2, op0=op0, op1=op1), r, w)

    def stt(self, out, in0, scalar, in1, op0, op1, r=(), w=()):
        return self.S.emit("dve", lambda e: e.scalar_tensor_tensor(out=out, in0=in0, scalar=scalar, in1=in1, op0=op0, op1=op1), r, w)

    def cp(self, out, in_, r=(), w=(), eng="dve"):
        if eng == "act":
            return self.S.emit("act", lambda e: e.activation(out=out, in_=in_, func=AF.Copy), r, w)
        return self.S.emit(eng, lambda e: e.tensor_copy(out=out, in_=in_), r, w)

    def recip(self, out, in_, r=(), w=()):
        return self.S.emit("dve", lambda e: e.reciprocal(out=out, in_=in_), r, w)

    def memset(self, ap, val, w=(), eng="pool"):
        return self.S.emit(eng, lambda e: e.memset(ap, val), (), w)

    def scan(self, out, d0, d1, init, r=(), w=()):
        return self.S.emit("dve", lambda e: e.tensor_tensor_scan(out=out, data0=d0, data1=d1, initial=init, op0=ALU.mult, op1=ALU.add), r, w)

    def dma(self, q, out, in_, owner, r=(), w=()):
        if owner.chan is None:
            owner.chan = self.S.chan("d%d" % self.S.nsem)
        chan = owner.chan
        return self.S.emit(q, lambda e: e.dma_start(out=out, in_=in_, allow_slow_non_contiguous=True), r, w, chan=chan)

    def gather(self, out, in_, idx_ap, owner, r=(), w=()):
        if owner.chan is None:
            owner.chan = self.S.chan("g%d" % self.S.nsem)
        chan = owner.chan
        return self.S.emit("pool", lambda e: e.indirect_dma_start(out=out, out_offset=None, in_=in_,
                                                                   in_offset=bass.IndirectOffsetOnAxis(ap=idx_ap, axis=0)), r, w, chan=chan)


C_ID, C_BONES, C_NBS, C_NBT, C_OFFD, C_TRIU, C_EEXP, C_I8, C_IOTA, NCONST = 0, 128, 256, 384, 512, 640, 768, 1280, 1288, 1296
M_WQ, M_WKBT, M_WVB, M_LRUG = 0, 2048, 2560, 3072


def build_program(cfg):
    nc = bass.Bass("TRN2", target_bir_lowering=False)
    L, T, NSEQ, NSP, NSS, TS = cfg.depth, cfg.T, cfg.nseq, cfg.nsp, cfg.nss, cfg.ts
    NTOK = cfg.ntokp + cfg.ntoks
    NPG = cfg.npages

    def din(name, shape, dt=F32):
        return nc.dram_tensor(name, list(shape), dt, kind="ExternalInput").ap()

    def dout(name, shape, dt=F32):
        return nc.dram_tensor(name, list(shape), dt, kind="ExternalOutput").ap()

    xin = din("xin", [8, 128, NTOK])
    cT_d = din("cT", [128, 8, NSEQ])
    par_d = din("par", [128, L, NPAR])
    const_d = din("const", [128, NCONST])
    rope_d = din("rope", [128, 2, NTOK])
    wada_d = din("wada", [L, 12, 128, 4096])
    win_d = din("win", [L, 11, 128, 4096])
    wmg_d = din("wmg", [L, 8, 128, 4096])
    wbo_d = din("wbo", [L, 4, 128, 4096])
    wmo_d = din("wmo", [L, 2, 128, 4096])
    wfi_d = din("wfi", [L, 11, 128, 4096])
    wfo_d = din("wfo", [L, 8, 128, 22 * 128])
    wmisc_d = din("wmisc", [L, 128, 4096])
    gfin_d = din("gfin", [128, 8])
    gmask_d = din("gmask", [64, 12, 64])
    st_sconv_d = din("st_sconv", [L, 128, 4, NSS, 2])
    st_gconv_d = din("st_gconv", [L, 128, 12, NSS, 3])
    st_lconv_d = din("st_lconv", [L, 128, 4, NSS, 3])
    st_lru_d = din("st_lru", [L, 128, 4, NSS])
    st_gdn_d = din("st_gdn", [L, NSS, 128, 4, 64])
    pt_d = din("pt", [NSS, NPG], I32)
    cckv_d = din("cckv", [L, cfg.npool * 128, 128])
    ckpe_d = din("ckpe", [L, cfg.npool * 128, 32])

    y_d = dout("y", [8, 128, NTOK])
    o_ckv_d = dout("o_ckv", [L, 128, NTOK])
    o_kpe_d = dout("o_kpe", [L, 32, NTOK])
    o_sconv_d = dout("o_sconv", [L, 128, 4, NSEQ, 2])
    o_gconv_d = dout("o_gconv", [L, 128, 12, NSEQ, 3])
    o_lconv_d = dout("o_lconv", [L, 128, 4, NSEQ, 3])
    o_lru_d = dout("o_lru", [L, 128, 4, NSEQ])
    o_gdn_d = dout("o_gdn", [L, NSEQ, 128, 4, 64])
    xs_d = nc.dram_tensor("xs", [8, 128, NTOK], F32, kind="Internal").ap()

    es = ExitStack()
    with es:
        K = KB(nc, es)
        S = K.S
        sb = K.sb
        out_deps = []

        def dma_out(dst, src, r):
            K.dma("sp", dst, src, r[0], r=r)
            if r[0] not in out_deps:
                out_deps.append(r[0])

        const = sb("const", [128, NCONST]); d_const = Dep()
        par = sb("par", [128, L, NPAR]); d_par = Dep()
        cTs = sb("cTs", [128, 8, NSEQ]); d_cT = Dep()
        gfin = sb("gfin", [128, 8]); d_gfin = Dep()
        K.dma("sp", const[:], const_d, d_const, w=[d_const])
        K.dma("sp", par[:], par_d, d_par, w=[d_par])
        K.dma("sp", cTs[:], cT_d, d_cT, w=[d_cT])
        K.dma("sp", gfin[:], gfin_d, d_gfin, w=[d_gfin])
        identf = const[:, C_ID:C_ID + 128]
        ident_b = sb("ident_b", [128, 128], BF16)
        bones_b = sb("bones_b", [128, 128], BF16)
        ones_b = sb("ones_b", [128, 128], BF16)
        ones_f = sb("ones_f", [128, 128], F32)
        d_cb = Dep()
        K.cp(ident_b[:], identf, r=[d_const], w=[d_cb])
        K.cp(bones_b[:], const[:, C_BONES:C_BONES + 128], r=[d_const], w=[d_cb])
        K.memset(ones_b[:], 1.0, w=[d_cb])
        K.memset(ones_f[:], 1.0, w=[d_cb])
        nbs = const[:, C_NBS:C_NBS + 128]
        nbt = const[:, C_NBT:C_NBT + 128]
        offd = const[:, C_OFFD:C_OFFD + 128]
        triu = const[:, C_TRIU:C_TRIU + 128]
        eexp = const[0:8, C_EEXP:C_EEXP + 512].rearrange("p (a b) -> p a b", a=4)
        i8 = const[0:8, C_I8:C_I8 + 8]
        iota_f = const[:, C_IOTA:C_IOTA + 1]
        csil = sb("csil", [128, 8, NSEQ], BF16); d_csil = Dep()
        K.act(csil[:], cTs[:], AF.Silu, r=[d_cT], w=[d_csil])

        psb = [es.enter_context(nc.psum_tensor("ps%d" % i, [128, 512], F32)) for i in range(8)]
        d_ps = [Dep() for _ in range(8)]

        psr2 = {"i": 0}

        def ps_next(grp=0):
            if grp == 0:
                i = K.psr % 3
                K.psr += 1
            else:
                i = 5 + psr2["i"] % 3
                psr2["i"] += 1
            return psb[i], d_ps[i]

        NSLOT = 2
        wring = [sb("wr%d" % i, [128, 4096], BF16) for i in range(NSLOT)]
        d_wr = [Dep() for _ in range(NSLOT)]
        wstate = {"i": 0}

        wsrc = {"wada": (wada_d, 12, 4096), "win": (win_d, 11, 4096), "wmg": (wmg_d, 8, 4096), "wbo": (wbo_d, 4, 4096),
                "wmo": (wmo_d, 2, 4096), "wfi": (wfi_d, 11, 4096), "wfo": (wfo_d, 8, 22 * 128), "wmisc": (wmisc_d, 1, 4096)}
        wbase = {}
        nblk = 0
        for nm, (_, nb, _) in wsrc.items():
            wbase[nm] = nblk
            nblk += nb
        wbf_d = nc.dram_tensor("wbf", [L * nblk, 128, 4096], BF16, kind="Internal").ap()
        d_wbf = {}
        for l in range(L):
            for nm, (src, nb, nel) in wsrc.items():
                for b in range(nb):
                    i = wstate["i"] % NSLOT
                    wstate["i"] += 1
                    sap = src[l] if nm == "wmisc" else src[l, b]
                    K.dma("pool", wring[i][:, 0:nel], sap, d_wr[i], w=[d_wr[i]])
                    idx = l * nblk + wbase[nm] + b
                    d_wbf[idx] = Dep()
                    K.dma("sp", wbf_d[idx][:, 0:nel], wring[i][:, 0:nel], d_wr[i], r=[d_wr[i]], w=[d_wbf[idx]])

        def wload(nm, l, b, nel=4096, kview=None, half=None):
            i = wstate["i"] % NSLOT
            wstate["i"] += 1
            idx = l * nblk + wbase[nm] + b
            src_ap = wbf_d[idx][:, 0:nel]
            dst = wring[i][:, 0:nel]
            if half is not None:
                src_ap = wbf_d[idx][:, 0:4096].rearrange("p (k n) -> p k n", k=kview)[:, :, half * 512:(half + 1) * 512]
                dst = wring[i][:, 0:nel].rearrange("p (k n) -> p k n", k=kview)
            K.dma("pool", dst, src_ap, d_wr[i], r=[d_wbf[idx]], w=[d_wr[i]])
            return wring[i], d_wr[i]

        modT = sb("modT", [128, 48, NSEQ]); d_mod = Dep()
        A1 = sb("A1", [128, 8, NSEQ]); A2 = sb("A2", [128, 8, NSEQ])
        negA = sb("negA", [8, 1]); lcl = sb("lcl", [128, 4]); lcl2 = sb("lcl2", [128, 4]); d_lp = Dep()
        hm_dummy = None

        def layer_setup(l):
            for b in range(12):
                wt, dw = wload("wada", l, b)
                wv = wt[:, :].rearrange("p (k n) -> p k n", k=8)
                for j4 in range(4):
                    j = b * 4 + j4
                    ps, dp = ps_next()
                    for kc in range(8):
                        K.mm(ps[:, 0:NSEQ], wv[:, kc, j4 * 128:(j4 + 1) * 128], csil[:, kc, :], start=(kc == 0), stop=(kc == 7),
                             r=[dw, d_csil], w=[dp])
                    K.act(modT[:, j, :], ps[:, 0:NSEQ], AF.Identity, bias=par[:, l, P_BADA + j:P_BADA + j + 1], r=[dp, d_par], w=[d_mod])
            for oc in range(8):
                K.ts(A1[:, oc, :], modT[:, 8 + oc, :], 1.0, ALU.add, par[:, l, P_GMIX + oc:P_GMIX + oc + 1], ALU.mult, r=[d_mod, d_par], w=[d_mod])
                K.ts(A2[:, oc, :], modT[:, 32 + oc, :], 1.0, ALU.add, par[:, l, P_GFFN + oc:P_GFFN + oc + 1], ALU.mult, r=[d_mod, d_par], w=[d_mod])
            K.act(negA[:], par[0:8, l, P_ALOG:P_ALOG + 1], AF.Exp, r=[d_par], w=[d_lp])
            K.ts(negA[:], negA[:], -1.0, ALU.mult, r=[d_lp], w=[d_lp])
            K.act(lcl[:], par[:, l, P_LAM:P_LAM + 4], AF.Exp, scale=-1.0, r=[d_par], w=[d_lp])
            K.act(lcl[:], lcl[:], AF.Ln, bias=1.0, r=[d_lp], w=[d_lp])
            K.ts(lcl2[:], lcl[:], -16.0, ALU.mult, r=[d_lp], w=[d_lp])
            K.ts(lcl[:], lcl[:], -8.0, ALU.mult, r=[d_lp], w=[d_lp])

        TM = T
        xt = sb("xt", [128, 8, TM]); d_x = Dep()
        hb = sb("hb", [128, 8, TM], BF16); d_h = Dep()
        sq = sb("sq", [128, 8, TM], BF16); d_sq = Dep()
        rstd = sb("rstd", [128, TM]); d_rstd = Dep()
        tmpf = sb("tmpf", [128, TM]); d_tmpf = Dep()
        tmpf2 = sb("tmpf2", [128, TM]); d_tmpf2 = Dep()
        ropet = sb("ropet", [128, 2, TM]); d_rope = Dep()
        d_xs_tiles = {}
        yb = [sb("yb%d" % n, [128, 4, TM], BF16) for n in range(4)]
        d_yb = [Dep() for _ in range(4)]
        qaqm = sb("qaqm", [128, 16, TM], BF16)
        qkzs = sb("qkzs", [128, 12, TM], F32)
        d_qabs, d_qm, d_qk, d_zs = Dep(), Dep(), Dep(), Dep()
        macc = qaqm[:, :, :].bitcast(F32).rearrange("p a t -> p (a t)").rearrange("p (c t) -> p c t", c=8)
        d_macc = Dep()
        maccb, d_maccb = sq, d_sq
        gsb = sb("gsb", [128, TM]); d_gsb = Dep()
        hid = qkzs[:, :, :].bitcast(BF16).rearrange("p a t -> p (a t)").rearrange("p (c t) -> p c t", c=24)
        d_hid = Dep()

        def bc(ap2, nseq, tps):
            return ap2.unsqueeze(2).to_broadcast([128, nseq, tps])

        def v3(ap2, nseq, tps):
            return ap2.rearrange("p (a b) -> p a b", a=nseq)

        def rmsnorm_mod(l, tc, Amod, shbase, gcol):
            Tt, nseq, tps, s0 = tc["T"], tc["nseq"], tc["tps"], tc["s0"]
            for c in range(8):
                K.act(sq[:, c, 0:Tt], xt[:, c, 0:Tt], AF.Square, r=[d_x], w=[d_sq])
            ps, dp = ps_next()
            for c in range(8):
                K.mm(ps[:, 0:Tt], ones_b[:], sq[:, c, 0:Tt], start=(c == 0), stop=(c == 7), r=[d_sq, d_cb], w=[dp])
            K.act(rstd[:, 0:Tt], ps[:, 0:Tt], AF.Sqrt, scale=1.0 / D, bias=1e-6, r=[dp], w=[d_rstd])
            K.recip(rstd[:, 0:Tt], rstd[:, 0:Tt], r=[d_rstd], w=[d_rstd])
            for c in range(8):
                K.tt(tmpf[:, 0:Tt], xt[:, c, 0:Tt], rstd[:, 0:Tt], ALU.mult, r=[d_x, d_rstd], w=[d_tmpf])
                if nseq == 1:
                    K.act(hb[:, c, 0:Tt], tmpf[:, 0:Tt], AF.Identity, scale=Amod[:, c, s0:s0 + 1], bias=modT[:, shbase + c, s0:s0 + 1],
                          r=[d_tmpf, d_mod], w=[d_h])
                else:
                    K.tt(v3(tmpf[:, 0:Tt], nseq, tps), v3(tmpf[:, 0:Tt], nseq, tps), bc(Amod[:, c, s0:s0 + nseq], nseq, tps), ALU.mult,
                         r=[d_tmpf, d_mod], w=[d_tmpf])
                    K.tt(v3(hb[:, c, 0:Tt], nseq, tps), v3(tmpf[:, 0:Tt], nseq, tps), bc(modT[:, shbase + c, s0:s0 + nseq], nseq, tps), ALU.add,
                         r=[d_tmpf, d_mod], w=[d_h])

        def resid_add(tc, ps, dp, oc, gtbase):
            Tt, nseq, tps, s0 = tc["T"], tc["nseq"], tc["tps"], tc["s0"]
            if nseq == 1:
                K.stt(xt[:, oc, 0:Tt], ps[:, 0:Tt], modT[:, gtbase + oc, s0:s0 + 1], xt[:, oc, 0:Tt], ALU.mult, ALU.add,
                      r=[dp, d_mod, d_x], w=[d_x])
            else:
                K.tt(v3(tmpf[:, 0:Tt], nseq, tps), v3(ps[:, 0:Tt], nseq, tps), bc(modT[:, gtbase + oc, s0:s0 + nseq], nseq, tps), ALU.mult,
                     r=[dp, d_mod], w=[d_tmpf])
                K.tt(xt[:, oc, 0:Tt], xt[:, oc, 0:Tt], tmpf[:, 0:Tt], ALU.add, r=[d_tmpf, d_x], w=[d_x])

        ctx = dict(nc=nc, K=K, S=S, sb=sb, cfg=cfg, par=par, d_par=d_par, const=const, d_const=d_const, psb=psb, d_ps=d_ps, ps_next=ps_next,
                   ident_b=ident_b, bones_b=bones_b, ones_b=ones_b, ones_f=ones_f, d_cb=d_cb, identf=identf, nbs=nbs, nbt=nbt, offd=offd,
                   triu=triu, eexp=eexp, i8=i8, iota_f=iota_f, hb=hb, d_h=d_h, yb=yb, d_yb=d_yb, ropet=ropet, d_rope=d_rope,
                   tmpf=tmpf, d_tmpf=d_tmpf, tmpf2=tmpf2, d_tmpf2=d_tmpf2, sq=sq, d_sq=d_sq, rstd=rstd, d_rstd=d_rstd,
                   negA=negA, lcl=lcl, lcl2=lcl2, d_lp=d_lp, dma_out=dma_out, wload=wload, v3=v3, bc=bc,
                   o_ckv_d=o_ckv_d, o_kpe_d=o_kpe_d, o_sconv_d=o_sconv_d, o_gconv_d=o_gconv_d, o_lconv_d=o_lconv_d, o_lru_d=o_lru_d, o_gdn_d=o_gdn_d,
                   st_sconv_d=st_sconv_d, st_gconv_d=st_gconv_d, st_lconv_d=st_lconv_d, st_lru_d=st_lru_d, st_gdn_d=st_gdn_d,
                   qaqm=qaqm, qkzs=qkzs, d_qabs=d_qabs, d_qm=d_qm, d_qk=d_qk, d_zs=d_zs,
                   gmask_d=gmask_d, pt_d=pt_d, cckv_d=cckv_d, ckpe_d=ckpe_d, wmisc_d=wmisc_d, win_d=win_d)
        mix = Mixers(ctx)
        if NSS > 0:
            mix.setup_pages()

        tiles = []
        for s in range(NSP):
            for t0 in range(0, cfg.seq, T):
                tiles.append(dict(kind="p", T=T, nseq=1, tps=T, s0=s, col0=s * cfg.seq + t0, pos0=t0, first=(t0 == 0), last=(t0 + T >= cfg.seq)))
        if NSS > 0:
            tiles.append(dict(kind="s", T=NSS * TS, nseq=NSS, tps=TS, s0=NSP, col0=cfg.ntokp, pos0=cfg.past, first=True, last=True))

        for l in range(L):
            layer_setup(l)
            for ti, tc in enumerate(tiles):
                Tt, c0 = tc["T"], tc["col0"]
                src = xin if l == 0 else xs_d
                dxs = d_xs_tiles.setdefault(ti, Dep())
                K.dma("sp", xt[:, :, 0:Tt], src[:, :, c0:c0 + Tt].rearrange("c p t -> p c t"), d_x, r=[dxs], w=[d_x])
                K.dma("sp", ropet[:, :, 0:Tt], rope_d[:, :, c0:c0 + Tt], d_rope, w=[d_rope])
                rmsnorm_mod(l, tc, A1, 0, P_GMIX)
                mix.run_layer_tile(l, tc)
                for n in range(4):
                    for half in range(2):
                        wbo, dwbo = wload("wbo", l, n, nel=2048, kview=4, half=half)
                        wbov = wbo[:, 0:2048].rearrange("p (k n) -> p k n", k=4)
                        wg, dwg = wload("wmg", l, n * 2 + half)
                        wgv = wg[:, :].rearrange("p (k n) -> p k n", k=8)
                        for o4 in range(4):
                            oc = half * 4 + o4
                            psg, dpg = ps_next()
                            for kc in range(8):
                                K.mm(psg[:, 0:Tt], wgv[:, kc, o4 * 128:(o4 + 1) * 128], hb[:, kc, 0:Tt], start=(kc == 0), stop=(kc == 7),
                                     r=[dwg, d_h], w=[dpg])
                            K.act(gsb[:, 0:Tt], psg[:, 0:Tt], AF.Sigmoid, r=[dpg], w=[d_gsb])
                            psp, dpp = ps_next()
                            for kc in range(4):
                                K.mm(psp[:, 0:Tt], wbov[:, kc, o4 * 128:(o4 + 1) * 128], yb[n][:, kc, 0:Tt], start=(kc == 0), stop=(kc == 3),
                                     r=[dwbo, d_yb[n]], w=[dpp])
                            if n == 0:
                                K.tt(macc[:, oc, 0:Tt], gsb[:, 0:Tt], psp[:, 0:Tt], ALU.mult, r=[d_gsb, dpp], w=[d_macc, d_qabs, d_qm])
                            else:
                                K.tt(gsb[:, 0:Tt], gsb[:, 0:Tt], psp[:, 0:Tt], ALU.mult, r=[d_gsb, dpp], w=[d_gsb])
                                if n < 3:
                                    K.tt(macc[:, oc, 0:Tt], macc[:, oc, 0:Tt], gsb[:, 0:Tt], ALU.add, r=[d_gsb, d_macc, d_qabs, d_qm], w=[d_macc, d_qabs, d_qm])
                                else:
                                    K.tt(maccb[:, oc, 0:Tt], macc[:, oc, 0:Tt], gsb[:, 0:Tt], ALU.add, r=[d_gsb, d_macc, d_qabs, d_qm], w=[d_maccb])
                for b in range(2):
                    wm, dwm = wload("wmo", l, b)
                    wmv = wm[:, :].rearrange("p (k n) -> p k n", k=8)
                    for o4 in range(4):
                        oc = b * 4 + o4
                        ps, dp = ps_next()
                        for kc in range(8):
                            K.mm(ps[:, 0:Tt], wmv[:, kc, o4 * 128:(o4 + 1) * 128], maccb[:, kc, 0:Tt], start=(kc == 0), stop=(kc == 7),
                                 r=[dwm, d_maccb], w=[dp])
                        resid_add(tc, ps, dp, oc, 16)
                rmsnorm_mod(l, tc, A2, 24, P_GFFN)
                for b in range(11):
                    wf, dwf = wload("wfi", l, b)
                    wfv = wf[:, :].rearrange("p (k n) -> p k n", k=8)
                    for jj in range(2):
                        j = b * 2 + jj
                        psg, dpg = ps_next()
                        for kc in range(8):
                            K.mm(psg[:, 0:Tt], wfv[:, kc, (2 * jj) * 128:(2 * jj + 1) * 128], hb[:, kc, 0:Tt], start=(kc == 0), stop=(kc == 7),
                                 r=[dwf, d_h], w=[dpg])
                        psu, dpu = ps_next()
                        for kc in range(8):
                            K.mm(psu[:, 0:Tt], wfv[:, kc, (2 * jj + 1) * 128:(2 * jj + 2) * 128], hb[:, kc, 0:Tt], start=(kc == 0), stop=(kc == 7),
                                 r=[dwf, d_h], w=[dpu])
                        K.act(gsb[:, 0:Tt], psg[:, 0:Tt], AF.Silu, r=[dpg], w=[d_gsb])
                        K.tt(hid[:, j, 0:Tt], gsb[:, 0:Tt], psu[:, 0:Tt], ALU.mult, r=[d_gsb, dpu], w=[d_hid, d_qk, d_zs])
                for oc in range(8):
                    wo, dwo = wload("wfo", l, oc, nel=22 * 128)
                    wov = wo[:, 0:22 * 128].rearrange("p (k n) -> p k n", k=22)
                    ps, dp = ps_next()
                    for kc in range(22):
                        K.mm(ps[:, 0:Tt], wov[:, kc, :], hid[:, kc, 0:Tt], start=(kc == 0), stop=(kc == 21), r=[dwo, d_hid, d_qk, d_zs], w=[dp])
                    resid_add(tc, ps, dp, oc, 40)
                if l < L - 1:
                    K.dma("sp", xs_d[:, :, c0:c0 + Tt].rearrange("c p t -> p c t"), xt[:, :, 0:Tt], d_x, r=[d_x], w=[dxs])
                else:
                    for c in range(8):
                        K.act(sq[:, c, 0:Tt], xt[:, c, 0:Tt], AF.Square, r=[d_x], w=[d_sq])
                    ps, dp = ps_next()
                    for c in range(8):
                        K.mm(ps[:, 0:Tt], ones_b[:], sq[:, c, 0:Tt], start=(c == 0), stop=(c == 7), r=[d_sq, d_cb], w=[dp])
                    K.act(rstd[:, 0:Tt], ps[:, 0:Tt], AF.Sqrt, scale=1.0 / D, bias=1e-6, r=[dp], w=[d_rstd])
                    K.recip(rstd[:, 0:Tt], rstd[:, 0:Tt], r=[d_rstd], w=[d_rstd])
                    for c in range(8):
                        K.stt(macc[:, c, 0:Tt], xt[:, c, 0:Tt], gfin[:, c:c + 1], rstd[:, 0:Tt], ALU.mult, ALU.mult,
                              r=[d_x, d_gfin, d_rstd], w=[d_macc, d_qabs, d_qm])
                    dma_out(y_d[:, :, c0:c0 + Tt].rearrange("c p t -> p c t"), macc[:, :, 0:Tt], [d_macc, d_qabs, d_qm])
        S.replay([("D", d.chan, d.chan.count) for d in out_deps])
    return nc


class Mixers:
    def __init__(self, ctx):
        self.__dict__.update(ctx)
        cfg, sb = self.cfg, self.sb
        T = cfg.T
        TM = max(T, cfg.nss * cfg.ts)
        self.TM = TM
        S = self.S
        self.uext = sb("uext", [128, 4, TM + 2 * max(1, cfg.nss)]); self.d_uext = Dep()
        self.gext = sb("gext", [128, 12, TM + 3 * max(1, cfg.nss)]); self.d_gext = Dep()
        self.lext = sb("lext", [128, 4, TM + 3 * max(1, cfg.nss)]); self.d_lext = Dep()
        self.cg = sb("cg", [128, TM]); self.d_cg = Dep()
        self.qa = sb("qa", [128, 2, TM]); self.d_qa = Dep()
        self.ckvr = sb("ckvr", [128, TM]); self.d_ckvr = Dep()
        self.kr = sb("kr", [128, TM]); self.d_kr = Dep()
        self.qk = self.qkzs[:, 0:8, :]
        self.vT = sb("vT", [128, 4, TM], BF16); self.d_vT = Dep()
        self.zs = self.qkzs[:, 8:12, :]
        self.ga = sb("ga", [8, TM]); self.gb = sb("gb", [8, TM]); self.d_gab = Dep()
        self.lgel = sb("lgel", [128, 4, TM], BF16); self.d_lgel = Dep()
        self.cq = sb("cq", [128, 2, TM], BF16); self.d_cq = Dep()
        self.qn2 = sb("qn2", [128, 4, TM], BF16); self.d_qn2 = Dep()
        self.qabs = self.qaqm[:, 0:8, :]
        self.qrot = sb("qrot", [128, 2, TM]); self.d_qrot = Dep()
        self.qm = self.qaqm[:, 8:16, :]
        NK = cfg.seq
        self.ckvT = sb("ckvT", [128, NK], BF16); self.d_ckvT = Dep()
        self.kpeR = sb("kpeR", [128, NK], BF16); self.d_kpeR = Dep()
        self.ckvtok = sb("ckvtok", [128, NK // 128, 128], BF16); self.d_ckvtok = Dep()
        self.ckvf = sb("ckvf", [128, TM]); self.d_ckvf = Dep()
        self.pT = [sb("pT%d" % i, [128, TM], BF16) for i in range(2)]; self.d_pT = [Dep(), Dep()]
        self.olat = sb("olat", [128, TM], BF16); self.d_olat = Dep()
        self.rsum = sb("rsum", [128, TM]); self.d_rsum = Dep()
        self.lu = sb("lu", [128, TM]); self.d_lu = Dep()
        self.lub = sb("lub", [128, TM], BF16); self.d_lub = Dep()
        self.la = sb("la", [128, TM]); self.d_la = Dep()
        self.lb = sb("lb", [128, TM]); self.d_lb = Dep()
        self.lhs_ = sb("lhs_", [128, TM]); self.d_lhs = Dep()
        self.lstate = sb("lstate", [128, 4, max(1, cfg.nss)]); self.d_lstate = Dep()
        self.qnT = sb("qnT", [128, 4, TM], BF16); self.knT = sb("knT", [128, 4, TM], BF16)
        self.kbT = sb("kbT", [128, 4, TM], BF16); self.d_gT = Dep()
        self.knTm = sb("knTm", [128, 2, 4, TM], BF16); self.qgT = sb("qgT", [128, 4, TM], BF16)
        self.wT = sb("wT", [128, 4, 64], BF16); self.d_wTm = Dep()
        self.Sbm = sb("Sbm", [128, 2, 4, 64], BF16)
        self.K.memset(self.knTm[:, :, :, :], 0.0, w=[self.d_gT])
        self.betaf = sb("betaf", [8, TM]); self.gcf = sb("gcf", [8, TM]); self.egcf = sb("egcf", [8, TM]); self.d_scal = Dep()
        self.gfm = sb("gfm", [8, TM]); self.d_gfm = Dep()
        self.oT = sb("oT", [128, 4, TM]); self.d_oT = Dep()
        self.Sst = sb("Sst", [128, 4, 64]); self.d_S = Dep()
        C = 64
        self.gd = {}
        for nm, shp, dt in [("tok", [64, 96], F32), ("rhsR", [8, 8, C], F32), ("d1", [64, 8, C], F32), ("d2", [64, 8, C], F32),
                            ("Ds", [64, 8, C], BF16), ("DTi", [64, 8, C], BF16), ("DTs", [64, 8, C], BF16),
                            ("Q0", [64, 8, C], BF16), ("P0", [64, 8, C], BF16), ("inT", [64, 8, C], BF16),
                            ("U0", [64, 8, C], BF16), ("U1", [64, 8, C], BF16), ("V0", [64, 8, C], BF16), ("V1", [64, 8, C], BF16),
                            ("O", [64, 8, C], BF16), ("OT", [64, 8, C], BF16), ("W1", [64, 8, C], BF16), ("W2", [64, 8, C], BF16),
                            ("vb", [64, 8, 64], BF16), ("kbg", [64, 8, 64], BF16), ("kw", [64, 8, 64], BF16),
                            ("bg", [64, 8], F32), ("ew", [64, 8], F32), ("egl", [128, 8], F32),
                            ("u", [64, 8, 64], F32), ("vn", [64, 8, 64], BF16)]:
            self.gd[nm] = (sb("g_" + nm, shp, dt), Dep())
        self.gmask = sb("gmask", [64, 12, 64], BF16); self.d_gmask = Dep()
        self.K.dma("pool", self.gmask[:, :, :], self.gmask_d, self.d_gmask, w=[self.d_gmask])
        if cfg.nss > 0:
            self.ptb = sb("ptb", [128, cfg.npages], I32); self.d_ptb = Dep()
            NG = 2
            self.pgc = [sb("pgc%d" % i, [128, 4, 128]) for i in range(NG)]
            self.pgk = [sb("pgk%d" % i, [128, 4, 32]) for i in range(NG)]
            self.d_pgc = [[Dep() for _ in range(4)] for _ in range(NG)]; self.d_pgk = [[Dep() for _ in range(4)] for _ in range(NG)]
            self.pgcb = [sb("pgcb%d" % i, [128, 4, 129], BF16) for i in range(NG)]; self.d_pgcb = [Dep() for _ in range(NG)]
            self.pgkb = [sb("pgkb%d" % i, [128, 4, 128], BF16) for i in range(NG)]; self.d_pgkb = [Dep() for _ in range(NG)]
            self.pcT = [sb("pcT%d" % i, [128, 512], BF16) for i in range(NG)]; self.d_pcT = [Dep() for _ in range(NG)]
            self.pkT = [sb("pkT%d" % i, [128, 512], BF16) for i in range(NG)]; self.d_pkT = [Dep() for _ in range(NG)]
            self.spT = [sb("spT%d" % i, [128, 256], BF16) for i in range(NG)]; self.d_spT = [Dep() for _ in range(NG)]
            for i in range(NG):
                self.K.memset(self.pgcb[i][:, :, 128:129], 1.0, w=[self.d_pgcb[i]])
            self.newtok = sb("newtok", [8, 129], BF16); self.d_newtok = Dep()
            self.K.memset(self.newtok[:, 128:129], 1.0, w=[self.d_newtok])
            self.so = sb("so", [64, 129]); self.d_so = Dep()
            self.sob = sb("sob", [64, 128], BF16); self.d_sob = Dep()
            self.solT = sb("solT", [128, 64], BF16); self.d_solT = Dep()
            self.pgi = 0

    def run_layer_tile(self, l, tc):
        K, par, d_par = self.K, self.par, self.d_par
        Tt, nseq, tps, s0 = tc["T"], tc["nseq"], tc["tps"], tc["s0"]
        hb, d_h = self.hb, self.d_h
        samp = tc["kind"] == "s"
        hist2 = 2 * nseq if samp else 2
        hist3 = 3 * nseq if samp else 3
        def extv(buf, c, h):
            return buf[:, c, 0:nseq * (h + tps)].rearrange("p (a b) -> p a b", a=nseq)
        self.extv = extv
        if samp:
            for (buf, dd, src, h, nchk) in ((self.uext, self.d_uext, self.st_sconv_d, 2, 4), (self.gext, self.d_gext, self.st_gconv_d, 3, 12),
                                            (self.lext, self.d_lext, self.st_lconv_d, 3, 4)):
                for c in range(nchk):
                    K.dma("sp", extv(buf, c, h)[:, :, 0:h], src[l, :, c, :, :], dd, w=[dd])
            K.dma("sp", self.lstate[:, :, 0:nseq], self.st_lru_d[l], self.d_lstate, w=[self.d_lstate])
        elif tc["first"]:
            for (buf, dd, h, nchk) in ((self.uext, self.d_uext, 2, 4), (self.gext, self.d_gext, 3, 12), (self.lext, self.d_lext, 3, 4)):
                K.memset(buf[:, :, 0:h], 0.0, w=[dd])
            K.memset(self.lstate[:, :, 0:1], 0.0, w=[self.d_lstate])
        else:
            for (buf, dd, h, nchk) in ((self.uext, self.d_uext, 2, 4), (self.gext, self.d_gext, 3, 12), (self.lext, self.d_lext, 3, 4)):
                K.cp(buf[:, :, 0:h], buf[:, :, tps:tps + h], r=[dd], w=[dd], eng="pool")

        wcur = {"b": -1, "wt": None, "dw": None}

        def proj(ci, M):
            b = ci // 4
            if b != wcur["b"]:
                wcur["wt"], wcur["dw"] = self.wload("win", l, b)
                wcur["b"] = b
            wv = wcur["wt"][:, :].rearrange("p (k n) -> p k n", k=8)
            ps, dp = self.ps_next()
            o = (ci % 4) * 128
            for kc in range(8):
                K.mm(ps[0:M, 0:Tt], wv[:, kc, o:o + M], hb[:, kc, 0:Tt], start=(kc == 0), stop=(kc == 7), r=[wcur["dw"], d_h], w=[dp])
            return ps, dp

        v3 = self.v3
        for j in range(4):
            ps, dp = proj(2 * j, 128)
            K.cp(self.cg[:, 0:Tt], ps[:, 0:Tt], r=[dp], w=[self.d_cg], eng="act")
            ps, dp = proj(2 * j + 1, 128)
            K.tt(extv(self.uext, j, 2)[:, :, 2:2 + tps], v3(self.cg[:, 0:Tt], nseq, tps), v3(ps[:, 0:Tt], nseq, tps), ALU.mult,
                 r=[self.d_cg, dp], w=[self.d_uext])
        for j in range(4):
            ps, dp = proj(8 + j, 128)
            e = extv(self.uext, j, 2)
            tv = v3(self.tmpf[:, 0:Tt], nseq, tps)
            K.ts(tv, e[:, :, 0:tps], par[:, l, P_SCW + j:P_SCW + j + 1], ALU.mult, r=[self.d_uext, d_par], w=[self.d_tmpf])
            for tap in (1, 2):
                K.stt(tv, e[:, :, tap:tap + tps], par[:, l, P_SCW + tap * 4 + j:P_SCW + tap * 4 + j + 1], tv, ALU.mult, ALU.add,
                      r=[self.d_uext, d_par, self.d_tmpf], w=[self.d_tmpf])
            K.tt(self.yb[0][:, j, 0:Tt], self.tmpf[:, 0:Tt], ps[:, 0:Tt], ALU.mult, r=[self.d_tmpf, dp], w=[self.d_yb[0]])
        for j in range(2):
            ps, dp = proj(12 + j, 128)
            K.cp(self.qa[:, j, 0:Tt], ps[:, 0:Tt], r=[dp], w=[self.d_qa], eng="act")
        ps, dp = proj(14, 128)
        K.cp(self.ckvr[:, 0:Tt], ps[:, 0:Tt], r=[dp], w=[self.d_ckvr], eng="act")
        ps, dp = proj(15, 128)
        K.tt(self.tmpf[:, 0:Tt], ps[:, 0:Tt], self.ropet[:, 0, 0:Tt], ALU.mult, r=[dp, self.d_rope], w=[self.d_tmpf])
        ps, dp = proj(16, 128)
        K.tt(self.tmpf2[:, 0:Tt], ps[:, 0:Tt], self.ropet[:, 1, 0:Tt], ALU.mult, r=[dp, self.d_rope], w=[self.d_tmpf2])
        K.tt(self.kr[:, 0:Tt], self.tmpf[:, 0:Tt], self.tmpf2[:, 0:Tt], ALU.add, r=[self.d_tmpf, self.d_tmpf2], w=[self.d_kr])
        for j in range(12):
            ps, dp = proj(17 + j, 128)
            e = extv(self.gext, j, 3)
            K.cp(e[:, :, 3:3 + tps], v3(ps[:, 0:Tt], nseq, tps), r=[dp], w=[self.d_gext], eng="act")
            tv = v3(self.tmpf[:, 0:Tt], nseq, tps)
            K.ts(tv, e[:, :, 0:tps], par[:, l, P_GDNW + j:P_GDNW + j + 1], ALU.mult, r=[self.d_gext, d_par], w=[self.d_tmpf])
            for tap in (1, 2, 3):
                K.stt(tv, e[:, :, tap:tap + tps], par[:, l, P_GDNW + tap * 12 + j:P_GDNW + tap * 12 + j + 1], tv, ALU.mult, ALU.add,
                      r=[self.d_gext, d_par, self.d_tmpf], w=[self.d_tmpf])
            if j < 8:
                K.act(self.qk[:, j, 0:Tt], self.tmpf[:, 0:Tt], AF.Silu, r=[self.d_tmpf], w=[self.d_qk])
            else:
                K.act(self.vT[:, j - 8, 0:Tt], self.tmpf[:, 0:Tt], AF.Silu, r=[self.d_tmpf], w=[self.d_vT])
        for j in range(4):
            ps, dp = proj(29 + j, 128)
            K.act(self.zs[:, j, 0:Tt], ps[:, 0:Tt], AF.Silu, r=[dp], w=[self.d_zs])
        ps, dp = proj(33, 8)
        K.cp(self.ga[:, 0:Tt], ps[0:8, 0:Tt], r=[dp], w=[self.d_gab], eng="act")
        ps, dp = proj(34, 8)
        K.cp(self.gb[:, 0:Tt], ps[0:8, 0:Tt], r=[dp], w=[self.d_gab], eng="act")
        for j in range(4):
            ps, dp = proj(35 + j, 128)
            K.cp(extv(self.lext, j, 3)[:, :, 3:3 + tps], v3(ps[:, 0:Tt], nseq, tps), r=[dp], w=[self.d_lext], eng="act")
        for j in range(4):
            ps, dp = proj(39 + j, 128)
            K.act(self.lgel[:, j, 0:Tt], ps[:, 0:Tt], AF.Gelu_apprx_tanh, r=[dp], w=[self.d_lgel])
        self.wm, self.dwm = self.wload("wmisc", l, 0)
        if tc["last"]:
            sl = slice(s0, s0 + nseq)
            for c in range(4):
                self.dma_out(self.o_sconv_d[l, :, c, sl, :], extv(self.uext, c, 2)[:, :, tps:tps + 2], [self.d_uext])
                self.dma_out(self.o_lconv_d[l, :, c, sl, :], extv(self.lext, c, 3)[:, :, tps:tps + 3], [self.d_lext])
            for c in range(12):
                self.dma_out(self.o_gconv_d[l, :, c, sl, :], extv(self.gext, c, 3)[:, :, tps:tps + 3], [self.d_gext])
        skip = getattr(self.cfg, "skip", ())
        gens = []
        for nm, fn, ybi in (("mla", self.mla, 1), ("gdn", self.gdn, 2), ("lru", self.lru, 3)):
            if nm in skip:
                K.memset(self.yb[ybi][:, :, 0:Tt], 0.0, w=[self.d_yb[ybi]])
            else:
                gens.append(fn(l, tc))
        while gens:
            for g in list(gens):
                try:
                    next(g)
                except StopIteration:
                    gens.remove(g)

    def lru(self, l, tc):
        K, par, d_par, v3 = self.K, self.par, self.d_par, self.v3
        Tt, nseq, tps, s0 = tc["T"], tc["nseq"], tc["tps"], tc["s0"]
        lu, lub, la, lb, lhs_, tmpf = self.lu, self.lub, self.la, self.lb, self.lhs_, self.tmpf
        wg = self.wm[:, M_LRUG:M_LRUG + 1024].rearrange("p (c g m) -> p c g m", c=4, g=2)
        for c in range(4):
            e = self.extv(self.lext, c, 3)
            uv = v3(lu[:, 0:Tt], nseq, tps)
            K.ts(uv, e[:, :, 0:tps], par[:, l, P_LRUW + c:P_LRUW + c + 1], ALU.mult, par[:, l, P_LRUB + c:P_LRUB + c + 1], ALU.add,
                 r=[self.d_lext, d_par], w=[self.d_lu])
            for tap in (1, 2, 3):
                K.stt(uv, e[:, :, tap:tap + tps], par[:, l, P_LRUW + tap * 4 + c:P_LRUW + tap * 4 + c + 1], uv, ALU.mult, ALU.add,
                      r=[self.d_lext, d_par, self.d_lu], w=[self.d_lu])
            K.cp(lub[:, 0:Tt], lu[:, 0:Tt], r=[self.d_lu], w=[self.d_lub], eng="act")
            ps, dp = self.ps_next()
            K.mm(ps[:, 0:Tt], wg[:, c, 0, :], lub[:, 0:Tt], r=[self.dwm, self.d_lub], w=[dp])
            K.act(la[:, 0:Tt], ps[:, 0:Tt], AF.Sigmoid, bias=par[:, l, P_BA + c:P_BA + c + 1], r=[dp, d_par], w=[self.d_la])
            ps, dp = self.ps_next()
            K.mm(ps[:, 0:Tt], wg[:, c, 1, :], lub[:, 0:Tt], r=[self.dwm, self.d_lub], w=[dp])
            K.act(lb[:, 0:Tt], ps[:, 0:Tt], AF.Sigmoid, bias=par[:, l, P_BX + c:P_BX + c + 1], r=[dp, d_par], w=[self.d_lb])
            K.act(tmpf[:, 0:Tt], la[:, 0:Tt], AF.Exp, scale=self.lcl2[:, c:c + 1], r=[self.d_la, self.d_lp], w=[self.d_tmpf])
            K.act(la[:, 0:Tt], la[:, 0:Tt], AF.Exp, scale=self.lcl[:, c:c + 1], r=[self.d_la, self.d_lp], w=[self.d_la])
            K.ts(tmpf[:, 0:Tt], tmpf[:, 0:Tt], 1.0, ALU.min, r=[self.d_tmpf], w=[self.d_tmpf])
            K.act(tmpf[:, 0:Tt], tmpf[:, 0:Tt], AF.Sqrt, scale=-1.0, bias=1.0, r=[self.d_tmpf], w=[self.d_tmpf])
            if tc["kind"] == "p" and tc["first"]:
                K.memset(tmpf[:, 0:1], 1.0, w=[self.d_tmpf], eng="dve")
            K.tt(lb[:, 0:Tt], lb[:, 0:Tt], lu[:, 0:Tt], ALU.mult, r=[self.d_lb, self.d_lu], w=[self.d_lb])
            K.tt(lb[:, 0:Tt], lb[:, 0:Tt], tmpf[:, 0:Tt], ALU.mult, r=[self.d_lb, self.d_tmpf], w=[self.d_lb])
            for b in range(nseq):
                seg = slice(b * tps, (b + 1) * tps)
                K.scan(lhs_[:, seg], la[:, seg], lb[:, seg], self.lstate[:, c, b:b + 1], r=[self.d_la, self.d_lb, self.d_lstate], w=[self.d_lhs])
            K.cp(self.lstate[:, c, 0:nseq], v3(lhs_[:, 0:Tt], nseq, tps)[:, :, tps - 1], r=[self.d_lhs], w=[self.d_lstate])
            K.tt(self.yb[3][:, c, 0:Tt], lhs_[:, 0:Tt], self.lgel[:, c, 0:Tt], ALU.mult, r=[self.d_lhs, self.d_lgel], w=[self.d_yb[3]])
            yield
        if tc["last"]:
            self.dma_out(self.o_lru_d[l, :, :, s0:s0 + nseq], self.lstate[:, :, 0:nseq], [self.d_lstate])

    def mla(self, l, tc):
        K, par, d_par = self.K, self.par, self.d_par
        Tt, nseq, tps, s0, c0 = tc["T"], tc["nseq"], tc["tps"], tc["s0"], tc["col0"]
        sq, rstd, tmpf, tmpf2 = self.sq, self.rstd, self.tmpf, self.tmpf2
        wm, dwm = self.wm, self.dwm
        for j in range(2):
            K.act(sq[:, j, 0:Tt], self.qa[:, j, 0:Tt], AF.Square, r=[self.d_qa], w=[self.d_sq])
        ps, dp = self.ps_next()
        for j in range(2):
            K.mm(ps[:, 0:Tt], self.ones_b[:], sq[:, j, 0:Tt], start=(j == 0), stop=(j == 1), r=[self.d_sq, self.d_cb], w=[dp])
        K.act(rstd[:, 0:Tt], ps[:, 0:Tt], AF.Sqrt, scale=1.0 / 256, bias=1e-6, r=[dp], w=[self.d_rstd])
        K.recip(rstd[:, 0:Tt], rstd[:, 0:Tt], r=[self.d_rstd], w=[self.d_rstd])
        for j in range(2):
            K.stt(self.cq[:, j, 0:Tt], self.qa[:, j, 0:Tt], par[:, l, P_GQ + j:P_GQ + j + 1], rstd[:, 0:Tt], ALU.mult, ALU.mult,
                  r=[self.d_qa, d_par, self.d_rstd], w=[self.d_cq])
        K.act(sq[:, 2, 0:Tt], self.ckvr[:, 0:Tt], AF.Square, r=[self.d_ckvr], w=[self.d_sq])
        ps, dp = self.ps_next()
        K.mm(ps[:, 0:Tt], self.ones_b[:], sq[:, 2, 0:Tt], r=[self.d_sq, self.d_cb], w=[dp])
        K.act(rstd[:, 0:Tt], ps[:, 0:Tt], AF.Sqrt, scale=1.0 / 128, bias=1e-6, r=[dp], w=[self.d_rstd])
        K.recip(rstd[:, 0:Tt], rstd[:, 0:Tt], r=[self.d_rstd], w=[self.d_rstd])
        K.stt(self.ckvf[:, 0:Tt], self.ckvr[:, 0:Tt], par[:, l, P_GKV:P_GKV + 1], rstd[:, 0:Tt], ALU.mult, ALU.mult,
              r=[self.d_ckvr, d_par, self.d_rstd], w=[self.d_ckvf])
        self.dma_out(self.o_ckv_d[l, :, c0:c0 + Tt], self.ckvf[:, 0:Tt], [self.d_ckvf])
        self.dma_out(self.o_kpe_d[l, :, c0:c0 + Tt], self.kr[0:32, 0:Tt], [self.d_kr])
        wq = wm[:, M_WQ:M_WQ + 2048].rearrange("p (k n) -> p k n", k=2)
        wkbT = wm[:, M_WKBT:M_WKBT + 512].rearrange("p (a c) -> p a c", a=4)
        for pr in range(4):
            ps, dp = self.ps_next()
            for kc in range(2):
                K.mm(ps[:, 0:Tt], wq[:, kc, pr * 128:(pr + 1) * 128], self.cq[:, kc, 0:Tt], start=(kc == 0), stop=(kc == 1), r=[dwm, self.d_cq], w=[dp])
            K.cp(self.qn2[:, pr, 0:Tt], ps[:, 0:Tt], r=[dp], w=[self.d_qn2], eng="act")
        for h in range(8):
            pr, r0 = h // 2, 64 * (h % 2)
            ps, dp = self.ps_next()
            K.mm(ps[:, 0:Tt], wkbT[r0:r0 + 64, pr, :], self.qn2[r0:r0 + 64, pr, 0:Tt], r=[dwm, self.d_qn2], w=[dp])
            K.cp(self.qabs[:, h, 0:Tt], ps[:, 0:Tt], r=[dp], w=[self.d_qabs], eng=("act" if h % 2 else "dve"))
        for g in range(2):
            ps, dp = self.ps_next()
            for kc in range(2):
                K.mm(ps[:, 0:Tt], wq[:, kc, 512 + g * 128:512 + (g + 1) * 128], self.cq[:, kc, 0:Tt], start=(kc == 0), stop=(kc == 1),
                     r=[dwm, self.d_cq], w=[dp])
            K.tt(tmpf[:, 0:Tt], ps[:, 0:Tt], self.ropet[:, 0, 0:Tt], ALU.mult, r=[dp, self.d_rope], w=[self.d_tmpf])
            ps, dp = self.ps_next()
            for kc in range(2):
                K.mm(ps[:, 0:Tt], wq[:, kc, 768 + g * 128:768 + (g + 1) * 128], self.cq[:, kc, 0:Tt], start=(kc == 0), stop=(kc == 1),
                     r=[dwm, self.d_cq], w=[dp])
            K.tt(tmpf2[:, 0:Tt], ps[:, 0:Tt], self.ropet[:, 1, 0:Tt], ALU.mult, r=[dp, self.d_rope], w=[self.d_tmpf2])
            K.tt(self.qrot[:, g, 0:Tt], tmpf[:, 0:Tt], tmpf2[:, 0:Tt], ALU.add, r=[self.d_tmpf, self.d_tmpf2], w=[self.d_qrot])
            for h4 in range(4):
                h = 4 * g + h4
                K.ts(self.qm[:, h, 0:Tt], self.qrot[:, g, 0:Tt], par[:, l, P_HM + h:P_HM + h + 1], ALU.mult, r=[self.d_qrot, d_par], w=[self.d_qm])
        skip = getattr(self.cfg, "skip", ())
        yield
        if tc["kind"] == "p" and "attnp" not in skip:
            yield from self.attn_prompt(l, tc)
        elif tc["kind"] == "s" and "attns" not in skip:
            yield from self.attn_sample(l, tc)
        else:
            K.memset(self.yb[1][:, :, 0:Tt], 0.0, w=[self.d_yb[1]])

    def attn_prompt(self, l, tc):
        K = self.K
        Tt, pos0 = tc["T"], tc["pos0"]
        kt0 = pos0 // 128
        wvb = self.wm[:, M_WVB:M_WVB + 512]
        K.cp(self.ckvT[:, pos0:pos0 + Tt], self.ckvf[:, 0:Tt], r=[self.d_ckvf], w=[self.d_ckvT], eng="act")
        K.cp(self.kpeR[:, pos0:pos0 + Tt], self.kr[:, 0:Tt], r=[self.d_kr], w=[self.d_kpeR], eng="act")
        for i in range(Tt // 128):
            ps, dp = self.ps_next()
            K.mm(ps[:, 0:128], self.ckvT[:, pos0 + i * 128:pos0 + (i + 1) * 128], self.ident_b[:], r=[self.d_ckvT, self.d_cb], w=[dp])
            K.cp(self.ckvtok[:, kt0 + i, :], ps[:, 0:128], r=[dp], w=[self.d_ckvtok])
        scale = 96.0 ** -0.5
        nkt = (pos0 + Tt) // 128
        pi = 0
        for h in range(8):
            accO, dO = self.psb[3], self.d_ps[3]
            accS, dS = self.psb[4], self.d_ps[4]
            for kt in range(nkt):
                off = kt * 128 - pos0
                q0 = max(off, 0)
                ps, dp = self.ps_next()
                K.mm(ps[:, q0:Tt], self.ckvT[:, kt * 128:(kt + 1) * 128], self.qabs[:, h, q0:Tt], start=True, stop=False,
                     r=[self.d_ckvT, self.d_qabs], w=[dp])
                K.mm(ps[:, q0:Tt], self.kpeR[:, kt * 128:(kt + 1) * 128], self.qm[:, h, q0:Tt], start=False, stop=True,
                     r=[self.d_kpeR, self.d_qm], w=[dp])
                pt, dpt = self.pT[pi % 2], self.d_pT[pi % 2]
                pi += 1
                K.act(pt[:, q0:Tt], ps[:, q0:Tt], AF.Exp, scale=scale, r=[dp], w=[dpt])
                if off >= 0:
                    K.tt(pt[:, q0:q0 + 128], pt[:, q0:q0 + 128], self.triu, ALU.mult, r=[dpt, self.d_const], w=[dpt], eng="pool")
                K.mm(accO[:, q0:Tt], self.ckvtok[:, kt, :], pt[:, q0:Tt], start=(kt == 0), stop=(kt == nkt - 1), r=[self.d_ckvtok, dpt], w=[dO])
                K.mm(accS[:, q0:Tt], self.ones_b[:], pt[:, q0:Tt], start=(kt == 0), stop=(kt == nkt - 1), r=[self.d_cb, dpt], w=[dS])
            K.recip(self.rsum[:, 0:Tt], accS[:, 0:Tt], r=[dS], w=[self.d_rsum])
            K.tt(self.olat[:, 0:Tt], accO[:, 0:Tt], self.rsum[:, 0:Tt], ALU.mult, r=[dO, self.d_rsum], w=[self.d_olat])
            pr, r0 = h // 2, 64 * (h % 2)
            ps, dp = self.ps_next()
            K.mm(ps[r0:r0 + 64, 0:Tt], wvb[:, h * 64:(h + 1) * 64], self.olat[:, 0:Tt], r=[self.dwm, self.d_olat], w=[dp])
            K.cp(self.yb[1][r0:r0 + 64, pr, 0:Tt], ps[r0:r0 + 64, 0:Tt], r=[dp], w=[self.d_yb[1]], eng="act")
            yield

    def setup_pages(self):
        K, cfg = self.K, self.cfg
        self.idx_seq = [self.sb("idx_seq%d" % i, [128, cfg.npages], I32) for i in range(2)]
        self.idxf_seq = self.sb("idxf_seq", [128, cfg.npages])
        self.iota_l = self.sb("iota_l", [128, cfg.depth])
        self.d_idxseq = [Dep(), Dep()]
        self.d_idxf = Dep()
        self.d_iotal = Dep()
        for l in range(cfg.depth):
            K.ts(self.iota_l[:, l:l + 1], self.iota_f, float(l * cfg.npool * 128), ALU.add, r=[self.d_const], w=[self.d_iotal])

    def seq_pages(self, l, b):
        K = self.K
        i = b % 2
        K.dma("sp", self.ptb[:], self.pt_d[b:b + 1, :].partition_broadcast(128), self.d_ptb, w=[self.d_ptb])
        K.cp(self.idxf_seq[:], self.ptb[:], r=[self.d_ptb], w=[self.d_idxf])
        K.ts(self.idxf_seq[:], self.idxf_seq[:], 128.0, ALU.mult, self.iota_l[:, l:l + 1], ALU.add, r=[self.d_idxf, self.d_iotal], w=[self.d_idxf])
        K.cp(self.idx_seq[i][:], self.idxf_seq[:], r=[self.d_idxf], w=[self.d_idxseq[i]])
        return self.idx_seq[i], self.d_idxseq[i]

    def attn_sample(self, l, tc):
        K, cfg = self.K, self.cfg
        nseq, tps = tc["nseq"], tc["tps"]
        NPG = cfg.npages
        wvb = self.wm[:, M_WVB:M_WVB + 512]
        scale = 96.0 ** -0.5
        accO, dO = self.psb[3], self.d_ps[3]
        cflat = self.cckv_d.rearrange("l n c -> (l n) c")
        kflat = self.ckpe_d.rearrange("l n c -> (l n) c")
        ckvb, krb = self.pT[0], self.pT[1]
        Tt = tc["T"]
        K.cp(ckvb[:, 0:Tt], self.ckvf[:, 0:Tt], r=[self.d_ckvf], w=[self.d_pT[0]], eng="act")
        K.cp(krb[:, 0:Tt], self.kr[:, 0:Tt], r=[self.d_kr], w=[self.d_pT[1]], eng="act")
        for b in range(nseq):
            cs = slice(b * tps, (b + 1) * tps)
            qa_b = self.qabs[:, :, cs]
            qm_b = self.qm[:, :, cs]
            ps, dp = self.ps_next()
            K.mm(ps[0:tps, 0:128], ckvb[:, cs], self.ident_b[:], r=[self.d_pT[0], self.d_cb], w=[dp])
            K.cp(self.newtok[0:tps, 0:128], ps[0:tps, 0:128], r=[dp], w=[self.d_newtok])
            first = True
            idxs, d_idxs = self.seq_pages(l, b)
            for g0 in range(0, NPG, 4):
                i = self.pgi % 2
                self.pgi += 1
                ng = min(4, NPG - g0)
                for j in range(ng):
                    K.gather(self.pgc[i][:, j, :], cflat, idxs[:, g0 + j:g0 + j + 1], self.d_pgc[i][j], r=[d_idxs], w=[self.d_pgc[i][j]])
                    K.gather(self.pgk[i][:, j, :], kflat, idxs[:, g0 + j:g0 + j + 1], self.d_pgk[i][j], r=[d_idxs], w=[self.d_pgk[i][j]])
                K.cp(self.pgcb[i][:, 0:ng, 0:128], self.pgc[i][:, 0:ng, :], r=self.d_pgc[i][0:ng], w=[self.d_pgcb[i]])
                for j in range(ng):
                    K.cp(self.pgkb[i][:, j, :].rearrange("p (a b) -> p a b", a=4), self.pgk[i][:, j, :].unsqueeze(1).to_broadcast([128, 4, 32]),
                         r=[self.d_pgk[i][j]], w=[self.d_pgkb[i]], eng="act")
                psT, dpT_ = self.ps_next()
                for j in range(ng):
                    K.mm(psT[:, j * 128:(j + 1) * 128], self.pgcb[i][:, j, 0:128], self.ident_b[:], r=[self.d_pgcb[i], self.d_cb], w=[dpT_])
                K.cp(self.pcT[i][:, 0:ng * 128], psT[:, 0:ng * 128], r=[dpT_], w=[self.d_pcT[i]])
                psK, dpK = self.ps_next()
                for j in range(ng):
                    K.mm(psK[:, j * 128:(j + 1) * 128], self.pgkb[i][:, j, :], self.ident_b[:], r=[self.d_pgkb[i], self.d_cb], w=[dpK])
                K.cp(self.pkT[i][:, 0:ng * 128], psK[:, 0:ng * 128], r=[dpK], w=[self.d_pkT[i]], eng="act")
                psS, dpS = self.ps_next()
                for j in range(ng):
                    K.mm(psS[:, j * 64:(j + 1) * 64], self.pcT[i][:, j * 128:(j + 1) * 128], qa_b, start=True, stop=False,
                         r=[self.d_pcT[i], self.d_qabs], w=[dpS])
                    K.mm(psS[:, j * 64:(j + 1) * 64], self.pkT[i][:, j * 128:(j + 1) * 128], qm_b, start=False, stop=True,
                         r=[self.d_pkT[i], self.d_qm], w=[dpS])
                K.act(self.spT[i][:, 0:ng * 64], psS[:, 0:ng * 64], AF.Exp, scale=scale, r=[dpS], w=[self.d_spT[i]])
                for j in range(ng):
                    K.mm(accO[0:64, 0:129], self.spT[i][:, j * 64:(j + 1) * 64], self.pgcb[i][:, j, :], start=first, stop=False,
                         r=[self.d_spT[i], self.d_pgcb[i]], w=[dO])
                    first = False
            i = self.pgi % 2
            self.pgi += 1
            psS, dpS = self.ps_next()
            K.mm(psS[0:tps, 0:64], ckvb[:, cs], qa_b, start=True, stop=False, r=[self.d_pT[0], self.d_qabs], w=[dpS])
            K.mm(psS[0:tps, 0:64], krb[:, cs], qm_b, start=False, stop=True, r=[self.d_pT[1], self.d_qm], w=[dpS])
            K.act(self.spT[i][0:tps, 0:64], psS[0:tps, 0:64], AF.Exp, scale=scale, r=[dpS], w=[self.d_spT[i]])
            spv = self.spT[i][0:tps, 0:64].rearrange("p (a b) -> p a b", a=8)
            K.tt(spv, spv, self.triu[0:tps, 0:tps].unsqueeze(1).to_broadcast([tps, 8, tps]), ALU.mult, r=[self.d_spT[i], self.d_const], w=[self.d_spT[i]])
            K.mm(accO[0:64, 0:129], self.spT[i][0:tps, 0:64], self.newtok[0:tps, :], start=first, stop=True,
                 r=[self.d_spT[i], self.d_newtok], w=[dO])
            K.cp(self.so[:, :], accO[0:64, 0:129], r=[dO], w=[self.d_so])
            K.recip(self.so[:, 128:129], self.so[:, 128:129], r=[self.d_so], w=[self.d_so])
            K.ts(self.sob[:, :], self.so[:, 0:128], self.so[:, 128:129], ALU.mult, r=[self.d_so], w=[self.d_sob])
            ps, dp = self.ps_next()
            K.mm(ps[:, 0:64], self.sob[:, :], self.ident_b[0:64, 0:64], r=[self.d_sob, self.d_cb], w=[dp])
            K.cp(self.solT[:, :], ps[:, 0:64], r=[dp], w=[self.d_solT], eng="act")
            for h in range(8):
                pr, r0 = h // 2, 64 * (h % 2)
                ps, dp = self.ps_next()
                K.mm(ps[r0:r0 + 64, 0:tps], wvb[:, h * 64:(h + 1) * 64], self.solT[:, h * tps:(h + 1) * tps], r=[self.dwm, self.d_solT], w=[dp])
                K.cp(self.yb[1][r0:r0 + 64, pr, cs], ps[r0:r0 + 64, 0:tps], r=[dp], w=[self.d_yb[1]], eng="act")
            yield

    def gdn(self, l, tc):
        K, par, d_par, v3 = self.K, self.par, self.d_par, self.v3
        Tt, nseq, tps, s0 = tc["T"], tc["nseq"], tc["tps"], tc["s0"]
        samp = tc["kind"] == "s"
        C = tps if samp else 64
        NL = {64: 6, 8: 3}[C]
        sq, rstd, tmpf = self.sq, self.rstd, self.tmpf
        gd = self.gd
        gstop = getattr(self.cfg, 'gstop', 0)
        if gstop:
            K.memset(self.oT[:, :, 0:Tt], 0.0, w=[self.d_oT])
        for j in range(8):
            K.act(sq[:, j, 0:Tt], self.qk[:, j, 0:Tt], AF.Square, r=[self.d_qk], w=[self.d_sq])
            ps, dp = self.ps_next()
            K.mm(ps[:, 0:Tt], self.bones_b[:], sq[:, j, 0:Tt], r=[self.d_sq, self.d_cb], w=[dp])
            K.act(rstd[:, 0:Tt], ps[:, 0:Tt], AF.Sqrt, bias=1e-6, r=[dp], w=[self.d_rstd])
            K.recip(rstd[:, 0:Tt], rstd[:, 0:Tt], r=[self.d_rstd], w=[self.d_rstd])
            dst = self.qnT[:, j, 0:Tt] if j < 4 else self.knT[:, j - 4, 0:Tt]
            K.stt(dst, self.qk[:, j, 0:Tt], (0.125 if j < 4 else 1.0), rstd[:, 0:Tt], ALU.mult, ALU.mult,
                  r=[self.d_qk, self.d_rstd], w=[self.d_gT])
            if j >= 4:
                K.cp(self.knTm[0:64, 0, j - 4, 0:Tt], self.knT[0:64, j - 4, 0:Tt], r=[self.d_gT], w=[self.d_gT], eng="act")
                K.cp(self.knTm[64:128, 1, j - 4, 0:Tt], self.knT[64:128, j - 4, 0:Tt], r=[self.d_gT], w=[self.d_gT], eng="act")
        if gstop and gstop <= 1:
            self._gdn_tail(l, tc)
            return
        betaf, gcf, egcf, gfm = self.betaf, self.gcf, self.egcf, self.gfm
        d_scal = self.d_scal
        K.act(betaf[:, 0:Tt], self.gb[:, 0:Tt], AF.Sigmoid, r=[self.d_gab], w=[d_scal])
        K.act(gfm[:, 0:Tt], self.ga[:, 0:Tt], AF.Exp, bias=par[0:8, l, P_DTB:P_DTB + 1], r=[self.d_gab, d_par], w=[self.d_gfm])
        K.act(gfm[:, 0:Tt], gfm[:, 0:Tt], AF.Ln, bias=1.0, r=[self.d_gfm], w=[self.d_gfm])
        K.ts(gfm[:, 0:Tt], gfm[:, 0:Tt], self.negA[:, 0:1], ALU.mult, r=[self.d_gfm, self.d_lp], w=[self.d_gfm])
        nchunk = Tt // C
        for n in range(nchunk):
            cs = slice(n * C, (n + 1) * C)
            K.scan(gcf[:, cs], self.ones_f[0:8, 0:C], gfm[:, cs], 0.0, r=[self.d_gfm, self.d_cb], w=[d_scal])
        K.act(egcf[:, 0:Tt], gcf[:, 0:Tt], AF.Exp, r=[d_scal], w=[d_scal])
        if gstop and gstop <= 2:
            self._gdn_tail(l, tc)
            return
        for pr in range(4):
            ps, dp = self.ps_next()
            K.mm(ps[:, 0:Tt], self.eexp[:, pr, :], betaf[:, 0:Tt], r=[self.d_const, d_scal], w=[dp])
            K.tt(self.kbT[:, pr, 0:Tt], self.knT[:, pr, 0:Tt], ps[:, 0:Tt], ALU.mult, r=[self.d_gT, dp], w=[self.d_gT])
            ps, dp = self.ps_next()
            K.mm(ps[:, 0:Tt], self.eexp[:, pr, :], egcf[:, 0:Tt], r=[self.d_const, d_scal], w=[dp])
            K.tt(self.qgT[:, pr, 0:Tt], self.qnT[:, pr, 0:Tt], ps[:, 0:Tt], ALU.mult, r=[self.d_gT, dp], w=[self.d_gT])
        if gstop and gstop <= 3:
            self._gdn_tail(l, tc)
            return
        Sst, Sbm, d_S = self.Sst, self.Sbm, self.d_S
        ident_b, d_cb = self.ident_b, self.d_cb

        def G(nm):
            return gd[nm]

        yield
        for n in range(nchunk):
            cs = slice(n * C, (n + 1) * C)
            b = n if samp else 0
            if samp:
                K.dma("sp", Sst[:], self.st_gdn_d[l, b], d_S, w=[d_S])
                K.cp(Sbm[0:64, 0], Sst[0:64], r=[d_S], w=[d_S])
                K.cp(Sbm[64:128, 1], Sst[64:128], r=[d_S], w=[d_S], eng="act")
            elif tc["first"] and n == 0:
                K.memset(Sst[:], 0.0, w=[d_S], eng="dve")
                K.memset(Sbm[:, :, :, :], 0.0, w=[d_S], eng="dve")
            tok, d_tok = G("tok")
            ps, dp = self.ps_next(1)
            for qi, srcf in enumerate((betaf, gcf, egcf)):
                K.mm(ps[0:C, qi * 8:(qi + 1) * 8], srcf[:, cs], self.identf[0:8, 0:8], r=[d_scal, self.d_const], w=[dp])
            K.cp(tok[0:C, 0:24], ps[0:C, 0:24], r=[dp], w=[d_tok], eng="act")
            beta_t, gc_t, egc_t = tok[0:C, 0:8], tok[0:C, 8:16], tok[0:C, 16:24]
            rhsR, d_rhsR = G("rhsR")
            K.tt(rhsR[:, :, 0:C], gcf[:, cs].unsqueeze(1).to_broadcast([8, 8, C]), self.i8.unsqueeze(2).to_broadcast([8, 8, C]), ALU.mult,
                 r=[d_scal, self.d_const], w=[d_rhsR])
            psR, dpR = self.ps_next(1)
            K.mm(psR[:, 0:8 * C], self.ones_f[0:8, :], rhsR[:, :, 0:C], r=[d_cb, d_rhsR], w=[dpR])
            Rv = psR[:, 0:8 * C].rearrange("p (a b) -> p a b", a=8)
            d1, d_d1 = G("d1"); d2, d_d2 = G("d2"); Ds, d_Ds = G("Ds"); DTi, d_DTi = G("DTi"); DTs, d_DTs = G("DTs")
            K.tt(d1[0:C, :, 0:C], Rv[0:C], gc_t.unsqueeze(2).to_broadcast([C, 8, C]), ALU.subtract, r=[dpR, d_tok], w=[d_d1])
            K.tt(d2[0:C, :, 0:C], d1[0:C, :, 0:C], self.nbs[0:C, 0:C].unsqueeze(1).to_broadcast([C, 8, C]), ALU.max, r=[d_d1, self.d_const], w=[d_d2])
            K.act(Ds[0:C, :, 0:C], d2[0:C, :, 0:C], AF.Exp, scale=-1.0, r=[d_d2], w=[d_Ds])
            K.tt(d2[0:C, :, 0:C], d1[0:C, :, 0:C], self.nbt[0:C, 0:C].unsqueeze(1).to_broadcast([C, 8, C]), ALU.min, r=[d_d1, self.d_const, d_Ds], w=[d_d2])
            K.act(DTi[0:C, :, 0:C], d2[0:C, :, 0:C], AF.Exp, r=[d_d2], w=[d_DTi])
            K.tt(DTs[0:C, :, 0:C], DTi[0:C, :, 0:C], self.offd[0:C, 0:C].unsqueeze(1).to_broadcast([C, 8, C]), ALU.mult, r=[d_DTi, self.d_const], w=[d_DTs])
            egl, d_egl = G("egl")
            K.act(egl[:, :], Rv[:, :, C - 1], AF.Exp, r=[dpR], w=[d_egl])
            ew, d_ew = G("ew"); bg, d_bg = G("bg")
            K.tt(ew[0:C, :], Rv[0:C, :, C - 1], gc_t, ALU.subtract, r=[dpR, d_tok], w=[d_ew])
            K.act(ew[0:C, :], ew[0:C, :], AF.Exp, r=[d_ew], w=[d_ew])
            K.tt(bg[0:C, :], beta_t, egc_t, ALU.mult, r=[d_tok], w=[d_bg])
            if gstop and gstop <= 4:
                continue
            psA, dpA = self.ps_next(1)
            psAT, dpAT = self.ps_next(1)
            psQ, dpQ = self.ps_next(1)
            for h in range(8):
                pr, r0 = h // 2, 64 * (h % 2)
                hf = h % 2
                K.mm(psA[0:C, h * C:(h + 1) * C], self.kbT[:, pr, cs], self.knTm[:, hf, pr, cs], r=[self.d_gT], w=[dpA])
                K.mm(psAT[0:C, h * C:(h + 1) * C], self.knTm[:, hf, pr, cs], self.kbT[:, pr, cs], r=[self.d_gT], w=[dpAT])
                K.mm(psQ[0:C, h * C:(h + 1) * C], self.knTm[:, hf, pr, cs], self.qnT[:, pr, cs], r=[self.d_gT], w=[dpQ])
            if gstop and gstop <= 5:
                continue
            (Q0, dQ0), (P0, dP0) = G("Q0"), G("P0")
            inT, d_inT = G("inT")

            def pv(ps):
                return ps[0:C, 0:8 * C].rearrange("p (a b) -> p a b", a=8)

            def mk(li, tr):
                return self.gmask[0:C, (6 if tr else 0) + li, 0:C].unsqueeze(1).to_broadcast([C, 8, C])
            K.tt(Q0[0:C, :, 0:C], pv(psA), Ds[0:C, :, 0:C], ALU.mult, r=[dpA, d_Ds], w=[dQ0])
            K.tt(P0[0:C, :, 0:C], pv(psAT), DTs[0:C, :, 0:C], ALU.mult, r=[dpAT, d_DTs], w=[dP0])
            K.tt(inT[0:C, :, 0:C], pv(psQ), DTi[0:C, :, 0:C], ALU.mult, r=[dpQ, d_DTi], w=[d_inT])
            if gstop and gstop <= 6:
                continue
            U = [G("U0"), G("U1")]; V = [G("V0"), G("V1")]
            (O, dO_), (OT, dOT) = G("O"), G("OT")
            (W1, dW1), (W2, dW2) = G("W1"), G("W2")
            idb = self.identf[0:C, 0:C].unsqueeze(1).to_broadcast([C, 8, C])
            K.tt(O[0:C, :, 0:C], Q0[0:C, :, 0:C], mk(0, False), ALU.mult, r=[dQ0, self.d_gmask], w=[dO_])
            K.stt(U[0][0][0:C, :, 0:C], O[0:C, :, 0:C], -1.0, idb, ALU.mult, ALU.add, r=[dO_, self.d_const], w=[U[0][1]])
            K.tt(OT[0:C, :, 0:C], P0[0:C, :, 0:C], mk(0, True), ALU.mult, r=[dP0, self.d_gmask], w=[dOT])
            K.stt(V[0][0][0:C, :, 0:C], OT[0:C, :, 0:C], -1.0, idb, ALU.mult, ALU.add, r=[dOT, self.d_const], w=[V[0][1]])
            cur = 0
            levels = [sz for sz in (2, 4, 8, 16, 32) if sz < C]
            for li, sz in enumerate(levels):
                lastl = (li == len(levels) - 1)
                nxt = 1 - cur
                (Uc, dUc), (Vc, dVc) = U[cur], V[cur]
                (Un, dUn), (Vn, dVn) = U[nxt], V[nxt]
                K.tt(O[0:C, :, 0:C], Q0[0:C, :, 0:C], mk(li + 1, False), ALU.mult, r=[dQ0, self.d_gmask], w=[dO_])
                psw2, dpw2 = self.ps_next(1)
                for h in range(8):
                    K.mm(psw2[0:C, h * C:(h + 1) * C], O[0:C, h, 0:C], Vc[0:C, h, 0:C], r=[dO_, dVc], w=[dpw2])
                K.cp(W2[0:C, :, 0:C], pv(psw2), r=[dpw2], w=[dW2], eng="act")
                psv_, dpv_ = self.ps_next(1)
                for h in range(8):
                    K.mm(psv_[0:C, h * C:(h + 1) * C], Uc[0:C, h, 0:C], W2[0:C, h, 0:C], r=[dUc, dW2], w=[dpv_])
                K.tt(Vn[0:C, :, 0:C], Vc[0:C, :, 0:C], pv(psv_), ALU.subtract, r=[dVc, dpv_], w=[dVn])
                if not lastl:
                    K.tt(OT[0:C, :, 0:C], P0[0:C, :, 0:C], mk(li + 1, True), ALU.mult, r=[dP0, self.d_gmask], w=[dOT])
                    psw1, dpw1 = self.ps_next(1)
                    for h in range(8):
                        K.mm(psw1[0:C, h * C:(h + 1) * C], OT[0:C, h, 0:C], Uc[0:C, h, 0:C], r=[dOT, dUc], w=[dpw1])
                    K.cp(W1[0:C, :, 0:C], pv(psw1), r=[dpw1], w=[dW1], eng="act")
                    psu_, dpu_ = self.ps_next(1)
                    for h in range(8):
                        K.mm(psu_[0:C, h * C:(h + 1) * C], Vc[0:C, h, 0:C], W1[0:C, h, 0:C], r=[dVc, dW1], w=[dpu_])
                    K.tt(Un[0:C, :, 0:C], Uc[0:C, :, 0:C], pv(psu_), ALU.subtract, r=[dUc, dpu_], w=[dUn])
                cur = nxt
            TT, dTT = V[cur]
            if gstop and gstop <= 7:
                continue
            psk, dpk = self.ps_next(1)
            psv, dpv = self.ps_next(1)
            for pr in range(4):
                K.mm(psk[0:C, pr * 128:(pr + 1) * 128], self.knT[:, pr, cs], ident_b[:], r=[self.d_gT, d_cb], w=[dpk])
                K.mm(psv[0:C, pr * 128:(pr + 1) * 128], self.vT[:, pr, cs], ident_b[:], r=[self.d_vT, d_cb], w=[dpv])
            vb, d_vb = G("vb"); kbg, d_kbg = G("kbg"); kw, d_kw = G("kw")

            def p64(ps):
                return ps[0:C, 0:512].rearrange("p (a b) -> p a b", a=8)
            K.tt(vb[0:C], p64(psv), beta_t.unsqueeze(2).to_broadcast([C, 8, 64]), ALU.mult, r=[dpv, d_tok], w=[d_vb])
            K.tt(kbg[0:C], p64(psk), bg[0:C, :].unsqueeze(2).to_broadcast([C, 8, 64]), ALU.mult, r=[dpk, d_bg], w=[d_kbg])
            K.tt(kw[0:C], p64(psk), ew[0:C, :].unsqueeze(2).to_broadcast([C, 8, 64]), ALU.mult, r=[dpk, d_ew], w=[d_kw])
            if gstop and gstop <= 8:
                continue
            psu, dpu = self.ps_next(1)
            psw, dpw = self.ps_next(1)
            for h in range(8):
                pr, r0 = h // 2, 64 * (h % 2)
                K.mm(psu[0:C, h * 64:(h + 1) * 64], TT[0:C, h, 0:C], vb[0:C, h, :], r=[dTT, d_vb], w=[dpu])
                K.mm(psw[r0:r0 + 64, pr * C:(pr + 1) * C], kbg[0:C, h, :], TT[0:C, h, 0:C], r=[dTT, d_kbg], w=[dpw])
            u, d_u = G("u"); vn, d_vn = G("vn")
            wT, d_wT = self.wT, self.d_wTm
            K.cp(u[0:C], p64(psu), r=[dpu], w=[d_u], eng="act")
            K.cp(wT[:, :, 0:C], psw[:, 0:4 * C].rearrange("p (a b) -> p a b", a=4), r=[dpw], w=[d_wT])
            if gstop and gstop <= 9:
                continue
            pss, dps = self.ps_next(1)
            for h in range(8):
                pr, r0 = h // 2, 64 * (h % 2)
                K.mm(pss[0:C, h * 64:(h + 1) * 64], wT[:, pr, 0:C], Sbm[:, h % 2, pr, :], r=[d_wT, d_S], w=[dps])
            K.tt(vn[0:C], u[0:C], p64(pss), ALU.subtract, r=[d_u, dps], w=[d_vn])
            if gstop and gstop <= 10:
                continue
            pso, dpo = self.ps_next(1)
            psS, dpS = self.ps_next(1)
            for h in range(8):
                pr, r0 = h // 2, 64 * (h % 2)
                K.mm(pso[r0:r0 + 64, pr * C:(pr + 1) * C], Sbm[:, h % 2, pr, :], self.qgT[:, pr, cs], start=True, stop=False,
                     r=[d_S, self.d_gT], w=[dpo])
                K.mm(pso[r0:r0 + 64, pr * C:(pr + 1) * C], vn[0:C, h, :], inT[0:C, h, 0:C], start=False, stop=True, r=[d_vn, d_inT], w=[dpo])
                K.mm(psS[r0:r0 + 64, pr * 64:(pr + 1) * 64], kw[0:C, h, :], vn[0:C, h, :], r=[d_kw, d_vn], w=[dpS])
            K.cp(self.oT[:, :, cs], pso[:, 0:4 * C].rearrange("p (a b) -> p a b", a=4), r=[dpo], w=[self.d_oT], eng="act")
            eglv = egl[:, :].rearrange("p (a b) -> p a b", b=2)
            K.tt(Sst[0:64], Sst[0:64], eglv[0:64, :, 0].unsqueeze(2).to_broadcast([64, 4, 64]), ALU.mult, r=[d_S, d_egl], w=[d_S])
            K.tt(Sst[64:128], Sst[64:128], eglv[64:128, :, 1].unsqueeze(2).to_broadcast([64, 4, 64]), ALU.mult, r=[d_S, d_egl], w=[d_S])
            K.tt(Sst[:], Sst[:], psS[:, 0:256].rearrange("p (a b) -> p a b", a=4), ALU.add, r=[d_S, dpS], w=[d_S])
            K.cp(Sbm[0:64, 0], Sst[0:64], r=[d_S], w=[d_S], eng="act")
            K.cp(Sbm[64:128, 1], Sst[64:128], r=[d_S], w=[d_S])
            if samp or (tc["last"] and n == nchunk - 1):
                self.dma_out(self.o_gdn_d[l, s0 + b], Sst[:], [d_S])
            yield
        self._gdn_tail(l, tc)

    def _gdn_tail(self, l, tc):
        K, par, d_par = self.K, self.par, self.d_par
        Tt = tc["T"]
        sq, rstd, tmpf = self.sq, self.rstd, self.tmpf
        for pr in range(4):
            K.act(sq[:, pr, 0:Tt], self.oT[:, pr, 0:Tt], AF.Square, r=[self.d_oT], w=[self.d_sq])
            ps, dp = self.ps_next()
            K.mm(ps[:, 0:Tt], self.bones_b[:], sq[:, pr, 0:Tt], r=[self.d_sq, self.d_cb], w=[dp])
            K.act(rstd[:, 0:Tt], ps[:, 0:Tt], AF.Sqrt, scale=1.0 / 64, bias=1e-6, r=[dp], w=[self.d_rstd])
            K.recip(rstd[:, 0:Tt], rstd[:, 0:Tt], r=[self.d_rstd], w=[self.d_rstd])
            K.tt(tmpf[:, 0:Tt], self.oT[:, pr, 0:Tt], rstd[:, 0:Tt], ALU.mult, r=[self.d_oT, self.d_rstd], w=[self.d_tmpf])
            K.stt(self.yb[2][:, pr, 0:Tt], tmpf[:, 0:Tt], par[:, l, P_GGDN:P_GGDN + 1], self.zs[:, pr, 0:Tt], ALU.mult, ALU.mult,
                  r=[self.d_tmpf, d_par, self.d_zs], w=[self.d_yb[2]])


def _blk(W, KC, NB):
    L, Kd, N = W.shape
    nb = N // NB
    return np.ascontiguousarray(W.reshape(L, KC, 128, nb, NB).transpose(0, 3, 2, 1, 4)).reshape(L, nb, 128, KC * NB)


def _fm(v, nch):
    lead = v.shape[:-1]
    a = v.reshape(lead + (nch, 128))
    return np.moveaxis(a, -1, 0)


def prep_shared(inp, cfg):
    L = cfg.depth
    f32 = np.float32
    sh = {}
    sh["wada"] = _blk(np.asarray(inp["w_ada"][:L], f32), 8, 512)
    win = np.asarray(inp["w_in"][:L], f32)
    wp = np.zeros((L, 1024, 44 * 128), f32)
    sc = win[:, :, 0:1536]
    bg_, cg_, xt_ = sc[:, :, 0:512], sc[:, :, 512:1024], sc[:, :, 1024:1536]
    for j in range(4):
        wp[:, :, (2 * j) * 128:(2 * j + 1) * 128] = cg_[:, :, j * 128:(j + 1) * 128]
        wp[:, :, (2 * j + 1) * 128:(2 * j + 2) * 128] = xt_[:, :, j * 128:(j + 1) * 128]
        wp[:, :, (8 + j) * 128:(9 + j) * 128] = bg_[:, :, j * 128:(j + 1) * 128]
    wp[:, :, 12 * 128:14 * 128] = win[:, :, 1536:1792]
    wp[:, :, 14 * 128:15 * 128] = win[:, :, 1792:1920]
    kpe = win[:, :, 1920:1952]
    kpes = np.concatenate([kpe[:, :, 16:32], kpe[:, :, 0:16]], axis=2)
    for rep in range(4):
        wp[:, :, 15 * 128 + rep * 32:15 * 128 + (rep + 1) * 32] = kpe
        wp[:, :, 16 * 128 + rep * 32:16 * 128 + (rep + 1) * 32] = kpes
    wp[:, :, 17 * 128:29 * 128] = win[:, :, 1952:3488]
    wp[:, :, 29 * 128:33 * 128] = win[:, :, 3488:4000]
    wp[:, :, 33 * 128:33 * 128 + 8] = win[:, :, 4000:4008]
    wp[:, :, 34 * 128:34 * 128 + 8] = win[:, :, 4008:4016]
    wp[:, :, 35 * 128:39 * 128] = win[:, :, 4016:4528]
    wp[:, :, 39 * 128:43 * 128] = win[:, :, 4528:5040]
    sh["win"] = _blk(wp, 8, 512)
    sh["wmg"] = _blk(np.asarray(inp["w_merge_gate"][:L], f32), 8, 512)
    wbo = np.asarray(inp["w_branch_out"][:L], f32)
    sh["wbo"] = np.ascontiguousarray(wbo.reshape(L, 4, 4, 128, 1024).transpose(0, 1, 3, 2, 4)).reshape(L, 4, 128, 4096)
    sh["wmo"] = _blk(np.asarray(inp["w_mix_out"][:L], f32), 8, 512)
    wfi = np.asarray(inp["w_ffn_in"][:L], f32)
    g = wfi[:, :, :FFN].reshape(L, 1024, 22, 1, 128)
    u = wfi[:, :, FFN:].reshape(L, 1024, 22, 1, 128)
    sh["wfi"] = _blk(np.concatenate([g, u], axis=3).reshape(L, 1024, 2 * FFN), 8, 512)
    sh["wfo"] = _blk(np.asarray(inp["w_ffn_out"][:L], f32), 22, 128)
    wm = np.zeros((L, 128, 4096), f32)
    wqb = np.asarray(inp["w_qb"][:L], f32).reshape(L, 256, 8, 96)
    nope = wqb[..., 0:64].reshape(L, 256, 512)
    pe = wqb[..., 64:96]
    peA = pe.reshape(L, 256, 256)
    peB = np.concatenate([pe[..., 16:32], pe[..., 0:16]], axis=-1).reshape(L, 256, 256)
    wq = np.concatenate([nope, peA, peB], axis=2)
    wm[:, :, M_WQ:M_WQ + 2048] = wq.reshape(L, 2, 128, 1024).transpose(0, 2, 1, 3).reshape(L, 128, 2048)
    wkvb = np.asarray(inp["w_kvb"][:L], f32)
    kb = wkvb[..., 0:64].reshape(L, 128, 4, 2, 64)
    wm[:, :, M_WKBT:M_WKBT + 512] = kb.transpose(0, 3, 4, 2, 1).reshape(L, 128, 512)
    wm[:, :, M_WVB:M_WVB + 512] = wkvb[..., 64:128].reshape(L, 128, 512)
    G = np.zeros((L, 128, 4, 2, 128), f32)
    for gi, nm in enumerate(("w_lru_gate_a", "w_lru_gate_x")):
        W = np.asarray(inp[nm][:L], f32)
        for c in range(4):
            for half in range(2):
                G[:, half * 64:(half + 1) * 64, c, gi, half * 64:(half + 1) * 64] = W[:, 2 * c + half]
    wm[:, :, M_LRUG:M_LRUG + 1024] = G.reshape(L, 128, 1024)
    sh["wmisc"] = wm
    par = np.zeros((128, L, NPAR), f32)

    def put(col, v, nch):
        par[:, :, col:col + nch] = _fm(np.asarray(v[:L], f32), nch)
    put(P_BADA, inp["b_ada"], 48)
    put(P_GMIX, inp["g_norm_mix"], 8)
    put(P_GFFN, inp["g_norm_ffn"], 8)
    w = np.asarray(inp["w_sc_conv"][:L], f32)
    for tap in range(3):
        par[:, :, P_SCW + tap * 4:P_SCW + tap * 4 + 4] = _fm(w[:, tap], 4)
    put(P_GQ, inp["g_q_norm"], 2)
    put(P_GKV, inp["g_kv_norm"], 1)
    w = np.asarray(inp["w_gdn_conv"][:L], f32)
    for tap in range(4):
        par[:, :, P_GDNW + tap * 12:P_GDNW + tap * 12 + 12] = _fm(w[:, tap], 12)
    w = np.asarray(inp["w_lru_conv"][:L], f32)
    for tap in range(4):
        par[:, :, P_LRUW + tap * 4:P_LRUW + tap * 4 + 4] = _fm(w[:, tap], 4)
    put(P_LRUB, inp["b_lru_conv"], 4)
    put(P_BA, inp["b_lru_gate_a"], 4)
    put(P_BX, inp["b_lru_gate_x"], 4)
    put(P_LAM, inp["lru_lambda"], 4)
    gg = np.asarray(inp["g_gdn_norm"][:L], f32)
    par[:, :, P_GGDN] = np.concatenate([gg, gg], axis=1).T
    for h in range(8):
        par[:, :, P_HM + h] = ((np.arange(128) // 32) == (h % 4)).astype(f32)[:, None]
    par[0:8, :, P_DTB] = np.asarray(inp["gdn_dt_bias"][:L], f32).T
    par[0:8, :, P_ALOG] = np.asarray(inp["gdn_a_log"][:L], f32).T
    sh["par"] = par
    sh["gfin"] = np.ascontiguousarray(_fm(np.asarray(inp["g_final"], f32), 8))
    cst = np.zeros((128, NCONST), f32)
    p = np.arange(128)[:, None]
    x = np.arange(128)[None, :]
    cst[:, C_ID:C_ID + 128] = (p == x)
    cst[:, C_BONES:C_BONES + 128] = ((p // 64) == (x // 64))
    cst[:, C_NBS:C_NBS + 128] = np.where(x < p, 0.0, 1e4)
    cst[:, C_NBT:C_NBT + 128] = np.where(x >= p, 0.0, -1e4)
    cst[:, C_OFFD:C_OFFD + 128] = (p != x)
    cst[:, C_TRIU:C_TRIU + 128] = (x >= p)
    ee = np.zeros((8, 4, 128), f32)
    for h in range(8):
        ee[h, h // 2, (h % 2) * 64:(h % 2) * 64 + 64] = 1.0
    cst[0:8, C_EEXP:C_EEXP + 512] = ee.reshape(8, 512)
    cst[0:8, C_I8:C_I8 + 8] = np.eye(8)
    cst[:, C_IOTA] = np.arange(128)
    sh["const"] = cst
    gm = np.zeros((64, 12, 64), f32)
    ii = np.arange(64)[:, None]
    jj = np.arange(64)[None, :]
    for li, sz in enumerate((1, 2, 4, 8, 16, 32)):
        m_ = ((ii // (2 * sz)) == (jj // (2 * sz))) & ((ii % (2 * sz)) >= sz) & ((jj % (2 * sz)) < sz)
        gm[:, li, :] = m_
        gm[:, 6 + li, :] = m_.T
    sh["gmask"] = gm
    return sh


def prep_core(inp, cfg, sh, pseqs, sseqs):
    f32 = np.float32
    L = cfg.depth
    m = dict(sh)
    xp = np.asarray(inp["x_prompt"], f32)[pseqs].reshape(-1, D)
    xs = np.asarray(inp["x_sample"], f32)[sseqs].reshape(-1, D)
    X = np.concatenate([xp, xs], axis=0)
    NTOK = X.shape[0]
    m["xin"] = np.ascontiguousarray(X.T).reshape(8, 128, NTOK)
    cc = np.concatenate([np.asarray(inp["c_prompt"], f32)[pseqs], np.asarray(inp["c_sample"], f32)[sseqs]], axis=0)
    m["cT"] = np.ascontiguousarray(cc.T.reshape(8, 128, -1).transpose(1, 0, 2))
    pos = np.concatenate([np.tile(np.arange(cfg.seq), len(pseqs)), np.tile(cfg.past + np.arange(cfg.ts), len(sseqs))]).astype(f32)
    inv = (np.float32(10000.0) ** (-np.arange(16, dtype=f32) / np.float32(16))).astype(f32)
    ang = (pos[:, None] * inv[None, :]).astype(f32)
    cos, sin = np.cos(ang).astype(f32), np.sin(ang).astype(f32)
    rope = np.zeros((128, 2, NTOK), f32)
    for p in range(128):
        f, half = p % 16, (p % 32) // 16
        rope[p, 0] = cos[:, f]
        rope[p, 1] = sin[:, f] if half == 1 else -sin[:, f]
    m["rope"] = rope
    ss = list(sseqs)
    m["st_sconv"] = np.ascontiguousarray(_fm(np.asarray(inp["state_sconv"], f32)[:L][:, ss], 4).transpose(0, 1, 4, 2, 3).transpose(1, 0, 2, 3, 4))
    m["st_gconv"] = np.ascontiguousarray(_fm(np.asarray(inp["state_gdn_conv"], f32)[:L][:, ss], 12).transpose(0, 1, 4, 2, 3).transpose(1, 0, 2, 3, 4))
    m["st_lconv"] = np.ascontiguousarray(_fm(np.asarray(inp["state_lru_conv"], f32)[:L][:, ss], 4).transpose(0, 1, 4, 2, 3).transpose(1, 0, 2, 3, 4))
    m["st_lru"] = np.ascontiguousarray(_fm(np.asarray(inp["state_lru"], f32)[:L][:, ss], 4).transpose(0, 1, 3, 2).transpose(1, 0, 2, 3))
    sg = np.asarray(inp["state_gdn"], f32)[:L][:, ss]
    m["st_gdn"] = np.ascontiguousarray(sg.reshape(L, len(ss), 4, 2, 64, 64).transpose(0, 1, 3, 4, 2, 5)).reshape(L, len(ss), 128, 4, 64)
    m["pt"] = np.ascontiguousarray(np.asarray(inp["page_table"], np.int32)[ss])
    m["cckv"] = np.asarray(inp["cache_mla_ckv"], f32)[:L].reshape(L, -1, 128)
    m["ckpe"] = np.asarray(inp["cache_mla_kpe"], f32)[:L].reshape(L, -1, 32)
    return m


def assemble(results, cfg, ncores):
    L, NSP, NSS, TS, SEQ = cfg.depth, cfg.nsp, cfg.nss, cfg.ts, cfg.seq
    ntp = cfg.ntokp

    def tokp(a):
        return a[:, :ntp].T.reshape(NSP, SEQ, -1)

    def toks(a):
        return a[:, ntp:].T.reshape(NSS, TS, -1)
    yp = np.concatenate([tokp(r["y"].reshape(D, -1)) for r in results], axis=0)
    ys = np.concatenate([toks(r["y"].reshape(D, -1)) for r in results], axis=0)
    outs_p, outs_s = {}, {}

    def both(key, fnp, fns):
        outs_p[key] = np.concatenate([fnp(r) for r in results], axis=1)
        outs_s[key] = np.concatenate([fns(r) for r in results], axis=1)
    both("ckv", lambda r: np.stack([tokp(r["o_ckv"][l]) for l in range(L)]), lambda r: np.stack([toks(r["o_ckv"][l]) for l in range(L)]))
    both("kpe", lambda r: np.stack([tokp(r["o_kpe"][l]) for l in range(L)]), lambda r: np.stack([toks(r["o_kpe"][l]) for l in range(L)]))

    def conv(a, nch):
        return a.transpose(0, 3, 4, 2, 1).reshape(a.shape[0], a.shape[3], a.shape[4], nch * 128)
    both("sconv", lambda r: conv(r["o_sconv"], 4)[:, :NSP], lambda r: conv(r["o_sconv"], 4)[:, NSP:])
    both("gdn_conv", lambda r: conv(r["o_gconv"], 12)[:, :NSP], lambda r: conv(r["o_gconv"], 12)[:, NSP:])
    both("lru_conv", lambda r: conv(r["o_lconv"], 4)[:, :NSP], lambda r: conv(r["o_lconv"], 4)[:, NSP:])

    def lru(a):
        return a.transpose(0, 3, 2, 1).reshape(a.shape[0], a.shape[3], 512)
    both("lru", lambda r: lru(r["o_lru"])[:, :NSP], lambda r: lru(r["o_lru"])[:, NSP:])

    def gdn(a):
        Ls, ns = a.shape[0], a.shape[1]
        return a.reshape(Ls, ns, 2, 64, 4, 64).transpose(0, 1, 4, 2, 3, 5).reshape(Ls, ns, 8, 64, 64)
    both("gdn", lambda r: gdn(r["o_gdn"])[:, :NSP], lambda r: gdn(r["o_gdn"])[:, NSP:])
    keys = ("ckv", "kpe", "sconv", "gdn_conv", "gdn", "lru_conv", "lru")
    out = [yp, ys] + [outs_p[k] for k in keys] + [outs_s[k] for k in keys]
    return tuple(np.ascontiguousarray(o, dtype=np.float32) for o in out)


def run(inp, cfg, ncores):
    sh = prep_shared(inp, cfg)
    in_maps = []
    for i in range(ncores):
        pseqs = list(range(i * cfg.nsp, (i + 1) * cfg.nsp))
        sseqs = list(range(i * cfg.nss, (i + 1) * cfg.nss))
        in_maps.append(prep_core(inp, cfg, sh, pseqs, sseqs))
    nc = build_program(cfg)
    res = run_bass_kernel_spmd(nc, in_maps, core_ids=list(range(ncores)))
    return assemble(res.results, cfg, ncores)


def kernel(**inputs):
    cfg = Cfg(depth=4, seq=2048, nsp=2, nss=16, ts=8, npages=64, npool=10240, T=256)
    return run(inputs, cfg, 8)
```

```python
import os
import numpy as np
from contextlib import ExitStack
import concourse.bass as bass
import concourse.mybir as mybir
from concourse.bass_utils import run_bass_kernel_spmd

F32 = mybir.dt.float32
BF16 = mybir.dt.bfloat16
I32 = mybir.dt.int32
AF = mybir.ActivationFunctionType
ALU = mybir.AluOpType

EPOCH = int(os.environ.get("MK_EPOCH", "24000"))


class Dep:
    __slots__ = ("w", "r", "chan")

    def __init__(self):
        self.w = None
        self.r = {}
        self.chan = None


class Chan:
    def __init__(self, sem):
        self.sem = sem
        self.count = 0


class Sched:
    ENGS = ("pe", "act", "dve", "pool", "sp")

    def __init__(self, nc, es):
        self.nc = nc
        self.es = es
        self.streams = {e: [] for e in self.ENGS}
        self.count = {e: 0 for e in self.ENGS}
        self.known = {e: {} for e in self.ENGS}
        self.esems = {}
        self.nsem = 0
        self.targets = {e: set() for e in self.ENGS}

    def new_sem(self, name):
        self.nsem += 1
        return self.es.enter_context(self.nc.semaphore(name))

    def chan(self, name):
        return Chan(self.new_sem("c_" + name))

    def _esem(self, eng, epoch):
        k = (eng, epoch)
        if k not in self.esems:
            self.esems[k] = self.new_sem("e_%s_%d" % k)
        return self.esems[k]

    def _key(self, ev):
        if ev[0] == "D":
            return ("D", id(ev[1]))
        return ("E", ev[1])

    def emit(self, eng, fn, reads=(), writes=(), chan=None):
        need = {}

        def add(ev, kind):
            if ev is None:
                return
            if ev[0] == "E" and ev[1] == eng:
                if eng == "pe" or kind != "raw":
                    return
            key = self._key(ev)
            val = ev[2]
            if self.known[eng].get(key, 0) >= val:
                return
            if key not in need or need[key][2] < val:
                need[key] = ev

        for d in reads:
            add(d.w, "raw")
        for d in writes:
            add(d.w, "waw")
            for ev in d.r.values():
                add(ev, "war")
        for key, ev in need.items():
            self.known[eng][key] = ev[2]
            if ev[0] == "E":
                self.targets[ev[1]].add(ev[2])
        if chan is not None:
            chan.count += 16
            ev = ("D", chan, chan.count)
            rk = ("D", id(chan))
        else:
            self.count[eng] += 1
            ev = ("E", eng, self.count[eng])
            rk = ("E", eng)
        for d in reads:
            d.r[rk] = ev
        for d in writes:
            d.w = ev
            d.r = {}
        self.streams[eng].append((list(need.values()), fn, ev))
        return ev

    def replay(self, final_waits):
        nc = self.nc
        rank = {}
        for e in self.ENGS:
            rank[e] = {c: i + 1 for i, c in enumerate(sorted(self.targets[e]))}
            for ep in range((max(len(rank[e]), 1) - 1) // EPOCH + 1):
                self._esem(e, ep)
        S = self

        def semval(ev):
            if ev[0] == "D":
                return ev[1].sem, ev[2]
            c = rank[ev[1]][ev[2]]
            ep = (c - 1) // EPOCH
            return S._esem(ev[1], ep), c - ep * EPOCH

        block = self.es.enter_context(nc.Block())

        def run(engname, engobj):
            for waits, fn, ev in S.streams[engname]:
                for wev in waits:
                    sem, val = semval(wev)
                    engobj.wait_ge(sem, val)
                ins = fn(engobj)
                if ev[0] == "D":
                    ins.then_inc(ev[1].sem, 16)
                elif ev[2] in rank[engname]:
                    sem, _ = semval(ev)
                    ins.then_inc(sem, 1)
            if engname == "sp":
                for ev in final_waits:
                    sem, val = semval(ev)
                    engobj.wait_ge(sem, val)

        @block.sync
        def _(e):
            run("sp", e)

        @block.gpsimd
        def _(e):
            run("pool", e)

        @block.vector
        def _(e):
            run("dve", e)

        @block.scalar
        def _(e):
            run("act", e)

        @block.tensor
        def _(e):
            run("pe", e)


D = 1024
NCH = 8
DEPTH_FULL = 4
H = 8
FFN = 2816
NPAR = 176
P_BADA, P_GMIX, P_GFFN, P_SCW, P_GQ, P_GKV, P_GDNW, P_LRUW, P_LRUB, P_BA, P_BX, P_LAM, P_GGDN, P_HM, P_DTB, P_ALOG = (
    0, 48, 56, 64, 76, 78, 79, 127, 143, 147, 151, 155, 159, 160, 168, 169)
INCH = {"sc": (0, 12, 128), "qa": (12, 2, 128), "ckv": (14, 1, 128), "kpeA": (15, 1, 128), "kpeB": (16, 1, 128),
        "gqkv": (17, 12, 128), "gz": (29, 4, 128), "ga": (33, 1, 8), "gb": (34, 1, 8), "lx": (35, 4, 128), "lg": (39, 4, 128)}
NINCH = 43


class Cfg:
    def __init__(self, depth=4, seq=2048, nsp=2, nss=16, ts=8, npages=64, npool=10240, T=512):
        self.depth, self.seq, self.nsp, self.nss, self.ts, self.npages, self.npool, self.T = depth, seq, nsp, nss, ts, npages, npool, T
        self.ntokp = nsp * seq
        self.ntoks = nss * ts
        self.nseq = nsp + nss
        self.past = npages * 128


class KB:
    def __init__(self, nc, es):
        self.nc, self.es = nc, es
        self.S = Sched(nc, es)
        self.psr = 0

    def sb(self, name, shape, dt=F32):
        return self.es.enter_context(self.nc.sbuf_tensor("s_" + name, list(shape), dt))

    def mm(self, out, lhsT, rhs, start=True, stop=True, r=(), w=()):
        return self.S.emit("pe", lambda e: e.matmul(out, lhsT=lhsT, rhs=rhs, start=start, stop=stop), r, w)

    def act(self, out, in_, func, r=(), w=(), bias=None, scale=None):
        kw = {}
        if bias is not None:
            kw["bias"] = bias
        if scale is not None:
            kw["scale"] = scale
        return self.S.emit("act", lambda e: e.activation(out=out, in_=in_, func=func, **kw), r, w)

    def tt(self, out, in0, in1, op, r=(), w=(), eng="dve"):
        return self.S.emit(eng, lambda e: e.tensor_tensor(out=out, in0=in0, in1=in1, op=op), r, w)

    def ts(self, out, in0, s1, op0, s2=None, op1=None, r=(), w=(), eng="dve"):
        if op1 is None:
            return self.S.emit(eng, lambda e: e.tensor_scalar(out=out, in0=in0, scalar1=s1, scalar2=None, op0=op0), r, w)
        return self.S.emit(eng, lambda e: e.tensor_scalar(out=out, in0=in0, scalar1=s1, scalar2=s2, op0=op0, op1=op1), r, w)

    def stt(self, out, in0, scalar, in1, op0, op1, r=(), w=()):
        return self.S.emit("dve", lambda e: e.scalar_tensor_tensor(out=out, in0=in0, scalar=scalar, in1=in1, op0=op0, op1=op1), r, w)

    def cp(self, out, in_, r=(), w=(), eng="dve"):
        if eng == "act":
            return self.S.emit("act", lambda e: e.activation(out=out, in_=in_, func=AF.Copy), r, w)
        return self.S.emit(eng, lambda e: e.tensor_copy(out=out, in_=in_), r, w)

    def recip(self, out, in_, r=(), w=()):
        return self.S.emit("dve", lambda e: e.reciprocal(out=out, in_=in_), r, w)

    def memset(self, ap, val, w=(), eng="pool"):
        return self.S.emit(eng, lambda e: e.memset(ap, val), (), w)

    def scan(self, out, d0, d1, init, r=(), w=()):
        return self.S.emit("dve", lambda e: e.tensor_tensor_scan(out=out, data0=d0, data1=d1, initial=init, op0=ALU.mult, op1=ALU.add), r, w)

    def dma(self, q, out, in_, owner, r=(), w=()):
        if owner.chan is None:
            owner.chan = self.S.chan("d%d" % self.S.nsem)
        chan = owner.chan
        return self.S.emit(q, lambda e: e.dma_start(out=out, in_=in_, allow_slow_non_contiguous=True), r, w, chan=chan)

    def gather(self, out, in_, idx_ap, owner, r=(), w=()):
        if owner.chan is None:
            owner.chan = self.S.chan("g%d" % self.S.nsem)
        chan = owner.chan
        return self.S.emit("pool", lambda e: e.indirect_dma_start(out=out, out_offset=None, in_=in_,
                                                                   in_offset=bass.IndirectOffsetOnAxis(ap=idx_ap, axis=0)), r, w, chan=chan)


C_ID, C_BONES, C_NBS, C_NBT, C_OFFD, C_TRIU, C_EEXP, C_I8, C_IOTA, NCONST = 0, 128, 256, 384, 512, 640, 768, 1280, 1288, 1296
M_WQ, M_WKBT, M_WVB, M_LRUG = 0, 2048, 2560, 3072


def build_program(cfg):
    nc = bass.Bass("TRN2", target_bir_lowering=False)
    L, T, NSEQ, NSP, NSS, TS = cfg.depth, cfg.T, cfg.nseq, cfg.nsp, cfg.nss, cfg.ts
    NTOK = cfg.ntokp + cfg.ntoks
    NPG = cfg.npages

    def din(name, shape, dt=F32):
        return nc.dram_tensor(name, list(shape), dt, kind="ExternalInput").ap()

    def dout(name, shape, dt=F32):
        return nc.dram_tensor(name, list(shape), dt, kind="ExternalOutput").ap()

    xin = din("xin", [8, 128, NTOK])
    cT_d = din("cT", [128, 8, NSEQ])
    par_d = din("par", [128, L, NPAR])
    const_d = din("const", [128, NCONST])
    rope_d = din("rope", [128, 2, NTOK])
    wada_d = din("wada", [L, 12, 128, 4096])
    win_d = din("win", [L, 11, 128, 4096])
    wmg_d = din("wmg", [L, 8, 128, 4096])
    wbo_d = din("wbo", [L, 4, 128, 4096])
    wmo_d = din("wmo", [L, 2, 128, 4096])
    wfi_d = din("wfi", [L, 11, 128, 4096])
    wfo_d = din("wfo", [L, 8, 128, 22 * 128])
    wmisc_d = din("wmisc", [L, 128, 4096])
    gfin_d = din("gfin", [128, 8])
    gmask_d = din("gmask", [64, 12, 64])
    st_sconv_d = din("st_sconv", [L, 128, 4, NSS, 2])
    st_gconv_d = din("st_gconv", [L, 128, 12, NSS, 3])
    st_lconv_d = din("st_lconv", [L, 128, 4, NSS, 3])
    st_lru_d = din("st_lru", [L, 128, 4, NSS])
    st_gdn_d = din("st_gdn", [L, NSS, 128, 4, 64])
    pt_d = din("pt", [NSS, NPG], I32)
    cckv_d = din("cckv", [L, cfg.npool * 128, 128])
    ckpe_d = din("ckpe", [L, cfg.npool * 128, 32])

    y_d = dout("y", [8, 128, NTOK])
    o_ckv_d = dout("o_ckv", [L, 128, NTOK])
    o_kpe_d = dout("o_kpe", [L, 32, NTOK])
    o_sconv_d = dout("o_sconv", [L, 128, 4, NSEQ, 2])
    o_gconv_d = dout("o_gconv", [L, 128, 12, NSEQ, 3])
    o_lconv_d = dout("o_lconv", [L, 128, 4, NSEQ, 3])
    o_lru_d = dout("o_lru", [L, 128, 4, NSEQ])
    o_gdn_d = dout("o_gdn", [L, NSEQ, 128, 4, 64])
    xs_d = nc.dram_tensor("xs", [8, 128, NTOK], F32, kind="Internal").ap()

    es = ExitStack()
    with es:
        K = KB(nc, es)
        S = K.S
        sb = K.sb
        out_deps = []

        def dma_out(dst, src, r):
            K.dma("pool", dst, src, r[0], r=r)
            if r[0] not in out_deps:
                out_deps.append(r[0])

        const = sb("const", [128, NCONST]); d_const = Dep()
        par = sb("par", [128, L, NPAR]); d_par = Dep()
        cTs = sb("cTs", [128, 8, NSEQ]); d_cT = Dep()
        gfin = sb("gfin", [128, 8]); d_gfin = Dep()
        K.dma("sp", const[:], const_d, d_const, w=[d_const])
        K.dma("sp", par[:], par_d, d_par, w=[d_par])
        K.dma("sp", cTs[:], cT_d, d_cT, w=[d_cT])
        K.dma("sp", gfin[:], gfin_d, d_gfin, w=[d_gfin])
        identf = const[:, C_ID:C_ID + 128]
        ident_b = sb("ident_b", [128, 128], BF16)
        bones_b = sb("bones_b", [128, 128], BF16)
        ones_b = sb("ones_b", [128, 128], BF16)
        ones_f = sb("ones_f", [128, 128], F32)
        d_cb = Dep()
        K.cp(ident_b[:], identf, r=[d_const], w=[d_cb])
        K.cp(bones_b[:], const[:, C_BONES:C_BONES + 128], r=[d_const], w=[d_cb])
        K.memset(ones_b[:], 1.0, w=[d_cb])
        K.memset(ones_f[:], 1.0, w=[d_cb])
        nbs = const[:, C_NBS:C_NBS + 128]
        nbt = const[:, C_NBT:C_NBT + 128]
        offd = const[:, C_OFFD:C_OFFD + 128]
        triu = const[:, C_TRIU:C_TRIU + 128]
        eexp = const[0:8, C_EEXP:C_EEXP + 512].rearrange("p (a b) -> p a b", a=4)
        i8 = const[0:8, C_I8:C_I8 + 8]
        iota_f = const[:, C_IOTA:C_IOTA + 1]
        csil = sb("csil", [128, 8, NSEQ], BF16); d_csil = Dep()
        K.act(csil[:], cTs[:], AF.Silu, r=[d_cT], w=[d_csil])

        psb = [es.enter_context(nc.psum_tensor("ps%d" % i, [128, 512], F32)) for i in range(8)]
        d_ps = [Dep() for _ in range(8)]

        psr2 = {"i": 0}

        def ps_next(grp=0):
            if grp == 0:
                i = K.psr % 3
                K.psr += 1
            else:
                i = 5 + psr2["i"] % 3
                psr2["i"] += 1
            return psb[i], d_ps[i]

        NSLOT = 2
        wring = [sb("wr%d" % i, [128, 4096], BF16) for i in range(NSLOT)]
        d_wr = [Dep() for _ in range(NSLOT)]
        wstate = {"i": 0}

        wsrc = {"wada": (wada_d, 12, 4096), "win": (win_d, 11, 4096), "wmg": (wmg_d, 8, 4096), "wbo": (wbo_d, 4, 4096),
                "wmo": (wmo_d, 2, 4096), "wfi": (wfi_d, 11, 4096), "wfo": (wfo_d, 8, 22 * 128), "wmisc": (wmisc_d, 1, 4096)}
        wbase = {}
        nblk = 0
        for nm, (_, nb, _) in wsrc.items():
            wbase[nm] = nblk
            nblk += nb
        wbf_d = nc.dram_tensor("wbf", [L * nblk, 128, 4096], BF16, kind="Internal").ap()
        d_wbf = {}
        for l in range(L):
            for nm, (src, nb, nel) in wsrc.items():
                for b in range(nb):
                    i = wstate["i"] % NSLOT
                    wstate["i"] += 1
                    sap = src[l] if nm == "wmisc" else src[l, b]
                    K.dma("pool", wring[i][:, 0:nel], sap, d_wr[i], w=[d_wr[i]])
                    idx = l * nblk + wbase[nm] + b
                    d_wbf[idx] = Dep()
                    K.dma("sp", wbf_d[idx][:, 0:nel], wring[i][:, 0:nel], d_wr[i], r=[d_wr[i]], w=[d_wbf[idx]])

        def wload(nm, l, b, nel=4096, kview=None, half=None):
            i = wstate["i"] % NSLOT
            wstate["i"] += 1
            idx = l * nblk + wbase[nm] + b
            src_ap = wbf_d[idx][:, 0:nel]
            dst = wring[i][:, 0:nel]
            if half is not None:
                src_ap = wbf_d[idx][:, 0:4096].rearrange("p (k n) -> p k n", k=kview)[:, :, half * 512:(half + 1) * 512]
                dst = wring[i][:, 0:nel].rearrange("p (k n) -> p k n", k=kview)
            K.dma("sp", dst, src_ap, d_wr[i], r=[d_wbf[idx]], w=[d_wr[i]])
            return wring[i], d_wr[i]

        modT = sb("modT", [128, 48, NSEQ]); d_mod = Dep()
        A1 = sb("A1", [128, 8, NSEQ]); A2 = sb("A2", [128, 8, NSEQ])
        negA = sb("negA", [8, 1]); lcl = sb("lcl", [128, 4]); lcl2 = sb("lcl2", [128, 4]); d_lp = Dep()
        hm_dummy = None

        def layer_setup(l):
            for b in range(12):
                wt, dw = wload("wada", l, b)
                wv = wt[:, :].rearrange("p (k n) -> p k n", k=8)
                for j4 in range(4):
                    j = b * 4 + j4
                    ps, dp = ps_next()
                    for kc in range(8):
                        K.mm(ps[:, 0:NSEQ], wv[:, kc, j4 * 128:(j4 + 1) * 128], csil[:, kc, :], start=(kc == 0), stop=(kc == 7),
                             r=[dw, d_csil], w=[dp])
                    K.act(modT[:, j, :], ps[:, 0:NSEQ], AF.Identity, bias=par[:, l, P_BADA + j:P_BADA + j + 1], r=[dp, d_par], w=[d_mod])
            for oc in range(8):
                K.ts(A1[:, oc, :], modT[:, 8 + oc, :], 1.0, ALU.add, par[:, l, P_GMIX + oc:P_GMIX + oc + 1], ALU.mult, r=[d_mod, d_par], w=[d_mod])
                K.ts(A2[:, oc, :], modT[:, 32 + oc, :], 1.0, ALU.add, par[:, l, P_GFFN + oc:P_GFFN + oc + 1], ALU.mult, r=[d_mod, d_par], w=[d_mod])
            K.act(negA[:], par[0:8, l, P_ALOG:P_ALOG + 1], AF.Exp, r=[d_par], w=[d_lp])
            K.ts(negA[:], negA[:], -1.0, ALU.mult, r=[d_lp], w=[d_lp])
            K.act(lcl[:], par[:, l, P_LAM:P_LAM + 4], AF.Exp, scale=-1.0, r=[d_par], w=[d_lp])
            K.act(lcl[:], lcl[:], AF.Ln, bias=1.0, r=[d_lp], w=[d_lp])
            K.ts(lcl2[:], lcl[:], -16.0, ALU.mult, r=[d_lp], w=[d_lp])
            K.ts(lcl[:], lcl[:], -8.0, ALU.mult, r=[d_lp], w=[d_lp])

        TM = T
        xt = sb("xt", [128, 8, TM]); d_x = Dep()
        hb = sb("hb", [128, 8, TM], BF16); d_h = Dep()
        sq = sb("sq", [128, 8, TM], BF16); d_sq = Dep()
        rstd = sb("rstd", [128, TM]); d_rstd = Dep()
        tmpf = sb("tmpf", [128, TM]); d_tmpf = Dep()
        tmpf2 = sb("tmpf2", [128, TM]); d_tmpf2 = Dep()
        ropet = sb("ropet", [128, 2, TM]); d_rope = Dep()
        d_xs_tiles = {}
        yb = [sb("yb%d" % n, [128, 4, TM], BF16) for n in range(4)]
        d_yb = [Dep() for _ in range(4)]
        qaqm = sb("qaqm", [128, 16, TM], BF16)
        qkzs = sb("qkzs", [128, 12, TM], F32)
        d_qabs, d_qm, d_qk, d_zs = Dep(), Dep(), Dep(), Dep()
        macc = qaqm[:, :, :].bitcast(F32).rearrange("p a t -> p (a t)").rearrange("p (c t) -> p c t", c=8)
        d_macc = Dep()
        maccb, d_maccb = sq, d_sq
        gsb = sb("gsb", [128, TM]); d_gsb = Dep()
        hid = qkzs[:, :, :].bitcast(BF16).rearrange("p a t -> p (a t)").rearrange("p (c t) -> p c t", c=24)
        d_hid = Dep()

        def bc(ap2, nseq, tps):
            return ap2.unsqueeze(2).to_broadcast([128, nseq, tps])

        def v3(ap2, nseq, tps):
            return ap2.rearrange("p (a b) -> p a b", a=nseq)

        def rmsnorm_mod(l, tc, Amod, shbase, gcol):
            Tt, nseq, tps, s0 = tc["T"], tc["nseq"], tc["tps"], tc["s0"]
            for c in range(8):
                K.act(sq[:, c, 0:Tt], xt[:, c, 0:Tt], AF.Square, r=[d_x], w=[d_sq])
            ps, dp = ps_next()
            for c in range(8):
                K.mm(ps[:, 0:Tt], ones_b[:], sq[:, c, 0:Tt], start=(c == 0), stop=(c == 7), r=[d_sq, d_cb], w=[dp])
            K.act(rstd[:, 0:Tt], ps[:, 0:Tt], AF.Sqrt, scale=1.0 / D, bias=1e-6, r=[dp], w=[d_rstd])
            K.recip(rstd[:, 0:Tt], rstd[:, 0:Tt], r=[d_rstd], w=[d_rstd])
            for c in range(8):
                K.tt(tmpf[:, 0:Tt], xt[:, c, 0:Tt], rstd[:, 0:Tt], ALU.mult, r=[d_x, d_rstd], w=[d_tmpf])
                if nseq == 1:
                    K.act(hb[:, c, 0:Tt], tmpf[:, 0:Tt], AF.Identity, scale=Amod[:, c, s0:s0 + 1], bias=modT[:, shbase + c, s0:s0 + 1],
                          r=[d_tmpf, d_mod], w=[d_h])
                else:
                    K.tt(v3(tmpf[:, 0:Tt], nseq, tps), v3(tmpf[:, 0:Tt], nseq, tps), bc(Amod[:, c, s0:s0 + nseq], nseq, tps), ALU.mult,
                         r=[d_tmpf, d_mod], w=[d_tmpf])
                    K.tt(v3(hb[:, c, 0:Tt], nseq, tps), v3(tmpf[:, 0:Tt], nseq, tps), bc(modT[:, shbase + c, s0:s0 + nseq], nseq, tps), ALU.add,
                         r=[d_tmpf, d_mod], w=[d_h])

        def resid_add(tc, ps, dp, oc, gtbase):
            Tt, nseq, tps, s0 = tc["T"], tc["nseq"], tc["tps"], tc["s0"]
            if nseq == 1:
                K.stt(xt[:, oc, 0:Tt], ps[:, 0:Tt], modT[:, gtbase + oc, s0:s0 + 1], xt[:, oc, 0:Tt], ALU.mult, ALU.add,
                      r=[dp, d_mod, d_x], w=[d_x])
            else:
                K.tt(v3(tmpf[:, 0:Tt], nseq, tps), v3(ps[:, 0:Tt], nseq, tps), bc(modT[:, gtbase + oc, s0:s0 + nseq], nseq, tps), ALU.mult,
                     r=[dp, d_mod], w=[d_tmpf])
                K.tt(xt[:, oc, 0:Tt], xt[:, oc, 0:Tt], tmpf[:, 0:Tt], ALU.add, r=[d_tmpf, d_x], w=[d_x])

        ctx = dict(nc=nc, K=K, S=S, sb=sb, cfg=cfg, par=par, d_par=d_par, const=const, d_const=d_const, psb=psb, d_ps=d_ps, ps_next=ps_next,
                   ident_b=ident_b, bones_b=bones_b, ones_b=ones_b, ones_f=ones_f, d_cb=d_cb, identf=identf, nbs=nbs, nbt=nbt, offd=offd,
                   triu=triu, eexp=eexp, i8=i8, iota_f=iota_f, hb=hb, d_h=d_h, yb=yb, d_yb=d_yb, ropet=ropet, d_rope=d_rope,
                   tmpf=tmpf, d_tmpf=d_tmpf, tmpf2=tmpf2, d_tmpf2=d_tmpf2, sq=sq, d_sq=d_sq, rstd=rstd, d_rstd=d_rstd,
                   negA=negA, lcl=lcl, lcl2=lcl2, d_lp=d_lp, dma_out=dma_out, wload=wload, v3=v3, bc=bc,
                   o_ckv_d=o_ckv_d, o_kpe_d=o_kpe_d, o_sconv_d=o_sconv_d, o_gconv_d=o_gconv_d, o_lconv_d=o_lconv_d, o_lru_d=o_lru_d, o_gdn_d=o_gdn_d,
                   st_sconv_d=st_sconv_d, st_gconv_d=st_gconv_d, st_lconv_d=st_lconv_d, st_lru_d=st_lru_d, st_gdn_d=st_gdn_d,
                   qaqm=qaqm, qkzs=qkzs, d_qabs=d_qabs, d_qm=d_qm, d_qk=d_qk, d_zs=d_zs,
                   gmask_d=gmask_d, pt_d=pt_d, cckv_d=cckv_d, ckpe_d=ckpe_d, wmisc_d=wmisc_d, win_d=win_d)
        mix = Mixers(ctx)
        if NSS > 0:
            mix.setup_pages()

        tiles = []
        for s in range(NSP):
            for t0 in range(0, cfg.seq, T):
                tiles.append(dict(kind="p", T=T, nseq=1, tps=T, s0=s, col0=s * cfg.seq + t0, pos0=t0, first=(t0 == 0), last=(t0 + T >= cfg.seq)))
        if NSS > 0:
            tiles.append(dict(kind="s", T=NSS * TS, nseq=NSS, tps=TS, s0=NSP, col0=cfg.ntokp, pos0=cfg.past, first=True, last=True))

        for l in range(L):
            layer_setup(l)
            for ti, tc in enumerate(tiles):
                Tt, c0 = tc["T"], tc["col0"]
                src = xin if l == 0 else xs_d
                dxs = d_xs_tiles.setdefault(ti, Dep())
                K.dma("pool", xt[:, :, 0:Tt], src[:, :, c0:c0 + Tt].rearrange("c p t -> p c t"), d_x, r=[dxs], w=[d_x])
                K.dma("pool", ropet[:, :, 0:Tt], rope_d[:, :, c0:c0 + Tt], d_rope, w=[d_rope])
                rmsnorm_mod(l, tc, A1, 0, P_GMIX)
                mix.run_layer_tile(l, tc)
                for n in range(4):
                    for half in range(2):
                        wbo, dwbo = wload("wbo", l, n, nel=2048, kview=4, half=half)
                        wbov = wbo[:, 0:2048].rearrange("p (k n) -> p k n", k=4)
                        wg, dwg = wload("wmg", l, n * 2 + half)
                        wgv = wg[:, :].rearrange("p (k n) -> p k n", k=8)
                        for o4 in range(4):
                            oc = half * 4 + o4
                            psg, dpg = ps_next()
                            for kc in range(8):
                                K.mm(psg[:, 0:Tt], wgv[:, kc, o4 * 128:(o4 + 1) * 128], hb[:, kc, 0:Tt], start=(kc == 0), stop=(kc == 7),
                                     r=[dwg, d_h], w=[dpg])
                            K.act(gsb[:, 0:Tt], psg[:, 0:Tt], AF.Sigmoid, r=[dpg], w=[d_gsb])
                            psp, dpp = ps_next()
                            for kc in range(4):
                                K.mm(psp[:, 0:Tt], wbov[:, kc, o4 * 128:(o4 + 1) * 128], yb[n][:, kc, 0:Tt], start=(kc == 0), stop=(kc == 3),
                                     r=[dwbo, d_yb[n]], w=[dpp])
                            if n == 0:
                                K.tt(macc[:, oc, 0:Tt], gsb[:, 0:Tt], psp[:, 0:Tt], ALU.mult, r=[d_gsb, dpp], w=[d_macc, d_qabs, d_qm])
                            else:
                                K.tt(gsb[:, 0:Tt], gsb[:, 0:Tt], psp[:, 0:Tt], ALU.mult, r=[d_gsb, dpp], w=[d_gsb])
                                if n < 3:
                                    K.tt(macc[:, oc, 0:Tt], macc[:, oc, 0:Tt], gsb[:, 0:Tt], ALU.add, r=[d_gsb, d_macc, d_qabs, d_qm], w=[d_macc, d_qabs, d_qm])
                                else:
                                    K.tt(maccb[:, oc, 0:Tt], macc[:, oc, 0:Tt], gsb[:, 0:Tt], ALU.add, r=[d_gsb, d_macc, d_qabs, d_qm], w=[d_maccb])
                for b in range(2):
                    wm, dwm = wload("wmo", l, b)
                    wmv = wm[:, :].rearrange("p (k n) -> p k n", k=8)
                    for o4 in range(4):
                        oc = b * 4 + o4
                        ps, dp = ps_next()
                        for kc in range(8):
                            K.mm(ps[:, 0:Tt], wmv[:, kc, o4 * 128:(o4 + 1) * 128], maccb[:, kc, 0:Tt], start=(kc == 0), stop=(kc == 7),
                                 r=[dwm, d_maccb], w=[dp])
                        resid_add(tc, ps, dp, oc, 16)
                rmsnorm_mod(l, tc, A2, 24, P_GFFN)
                for b in range(11):
                    wf, dwf = wload("wfi", l, b)
                    wfv = wf[:, :].rearrange("p (k n) -> p k n", k=8)
                    for jj in range(2):
                        j = b * 2 + jj
                        psg, dpg = ps_next()
                        for kc in range(8):
                            K.mm(psg[:, 0:Tt], wfv[:, kc, (2 * jj) * 128:(2 * jj + 1) * 128], hb[:, kc, 0:Tt], start=(kc == 0), stop=(kc == 7),
                                 r=[dwf, d_h], w=[dpg])
                        psu, dpu = ps_next()
                        for kc in range(8):
                            K.mm(psu[:, 0:Tt], wfv[:, kc, (2 * jj + 1) * 128:(2 * jj + 2) * 128], hb[:, kc, 0:Tt], start=(kc == 0), stop=(kc == 7),
                                 r=[dwf, d_h], w=[dpu])
                        K.act(gsb[:, 0:Tt], psg[:, 0:Tt], AF.Silu, r=[dpg], w=[d_gsb])
                        K.tt(hid[:, j, 0:Tt], gsb[:, 0:Tt], psu[:, 0:Tt], ALU.mult, r=[d_gsb, dpu], w=[d_hid, d_qk, d_zs])
                for oc in range(8):
                    wo, dwo = wload("wfo", l, oc, nel=22 * 128)
                    wov = wo[:, 0:22 * 128].rearrange("p (k n) -> p k n", k=22)
                    ps, dp = ps_next()
                    for kc in range(22):
                        K.mm(ps[:, 0:Tt], wov[:, kc, :], hid[:, kc, 0:Tt], start=(kc == 0), stop=(kc == 21), r=[dwo, d_hid, d_qk, d_zs], w=[dp])
                    resid_add(tc, ps, dp, oc, 40)
                if l < L - 1:
                    K.dma("pool", xs_d[:, :, c0:c0 + Tt].rearrange("c p t -> p c t"), xt[:, :, 0:Tt], d_x, r=[d_x], w=[dxs])
                else:
                    for c in range(8):
                        K.act(sq[:, c, 0:Tt], xt[:, c, 0:Tt], AF.Square, r=[d_x], w=[d_sq])
                    ps, dp = ps_next()
                    for c in range(8):
                        K.mm(ps[:, 0:Tt], ones_b[:], sq[:, c, 0:Tt], start=(c == 0), stop=(c == 7), r=[d_sq, d_cb], w=[dp])
                    K.act(rstd[:, 0:Tt], ps[:, 0:Tt], AF.Sqrt, scale=1.0 / D, bias=1e-6, r=[dp], w=[d_rstd])
                    K.recip(rstd[:, 0:Tt], rstd[:, 0:Tt], r=[d_rstd], w=[d_rstd])
                    for c in range(8):
                        K.stt(macc[:, c, 0:Tt], xt[:, c, 0:Tt], gfin[:, c:c + 1], rstd[:, 0:Tt], ALU.mult, ALU.mult,
                              r=[d_x, d_gfin, d_rstd], w=[d_macc, d_qabs, d_qm])
                    dma_out(y_d[:, :, c0:c0 + Tt].rearrange("c p t -> p c t"), macc[:, :, 0:Tt], [d_macc, d_qabs, d_qm])
        S.replay([("D", d.chan, d.chan.count) for d in out_deps])
    return nc


class Mixers:
    def __init__(self, ctx):
        self.__dict__.update(ctx)
        cfg, sb = self.cfg, self.sb
        T = cfg.T
        TM = max(T, cfg.nss * cfg.ts)
        self.TM = TM
        S = self.S
        self.uext = sb("uext", [128, 4, TM + 2 * max(1, cfg.nss)]); self.d_uext = Dep()
        self.gext = sb("gext", [128, 12, TM + 3 * max(1, cfg.nss)]); self.d_gext = Dep()
        self.lext = sb("lext", [128, 4, TM + 3 * max(1, cfg.nss)]); self.d_lext = Dep()
        self.cg = sb("cg", [128, TM]); self.d_cg = Dep()
        self.qa = sb("qa", [128, 2, TM]); self.d_qa = Dep()
        self.ckvr = sb("ckvr", [128, TM]); self.d_ckvr = Dep()
        self.kr = sb("kr", [128, TM]); self.d_kr = Dep()
        self.qk = self.qkzs[:, 0:8, :]
        self.vT = sb("vT", [128, 4, TM], BF16); self.d_vT = Dep()
        self.zs = self.qkzs[:, 8:12, :]
        self.ga = sb("ga", [8, TM]); self.gb = sb("gb", [8, TM]); self.d_gab = Dep()
        self.lgel = sb("lgel", [128, 4, TM], BF16); self.d_lgel = Dep()
        self.cq = sb("cq", [128, 2, TM], BF16); self.d_cq = Dep()
        self.qn2 = sb("qn2", [128, 4, TM], BF16); self.d_qn2 = Dep()
        self.qabs = self.qaqm[:, 0:8, :]
        self.qrot = sb("qrot", [128, 2, TM]); self.d_qrot = Dep()
        self.qm = self.qaqm[:, 8:16, :]
        NK = cfg.seq
        self.ckvT = sb("ckvT", [128, NK], BF16); self.d_ckvT = Dep()
        self.kpeR = sb("kpeR", [128, NK], BF16); self.d_kpeR = Dep()
        self.ckvtok = sb("ckvtok", [128, NK // 128, 128], BF16); self.d_ckvtok = Dep()
        self.ckvf = sb("ckvf", [128, TM]); self.d_ckvf = Dep()
        self.pT = [sb("pT%d" % i, [128, TM], BF16) for i in range(2)]; self.d_pT = [Dep(), Dep()]
        self.olat = sb("olat", [128, TM], BF16); self.d_olat = Dep()
        self.rsum = sb("rsum", [128, TM]); self.d_rsum = Dep()
        self.lu = sb("lu", [128, TM]); self.d_lu = Dep()
        self.lub = sb("lub", [128, TM], BF16); self.d_lub = Dep()
        self.la = sb("la", [128, TM]); self.d_la = Dep()
        self.lb = sb("lb", [128, TM]); self.d_lb = Dep()
        self.lhs_ = sb("lhs_", [128, TM]); self.d_lhs = Dep()
        self.lstate = sb("lstate", [128, 4, max(1, cfg.nss)]); self.d_lstate = Dep()
        self.qnT = sb("qnT", [128, 4, TM], BF16); self.knT = sb("knT", [128, 4, TM], BF16)
        self.kbT = sb("kbT", [128, 4, TM], BF16); self.d_gT = Dep()
        self.knTm = sb("knTm", [128, 2, 4, TM], BF16); self.qgT = sb("qgT", [128, 4, TM], BF16)
        self.wT = sb("wT", [128, 4, 64], BF16); self.d_wTm = Dep()
        self.Sbm = sb("Sbm", [128, 2, 4, 64], BF16)
        self.K.memset(self.knTm[:, :, :, :], 0.0, w=[self.d_gT])
        self.betaf = sb("betaf", [8, TM]); self.gcf = sb("gcf", [8, TM]); self.egcf = sb("egcf", [8, TM]); self.d_scal = Dep()
        self.gfm = sb("gfm", [8, TM]); self.d_gfm = Dep()
        self.oT = sb("oT", [128, 4, TM]); self.d_oT = Dep()
        self.Sst = sb("Sst", [128, 4, 64]); self.d_S = Dep()
        C = 64
        self.gd = {}
        for nm, shp, dt in [("tok", [64, 96], F32), ("rhsR", [8, 8, C], F32), ("d1", [64, 8, C], F32), ("d2", [64, 8, C], F32),
                            ("Ds", [64, 8, C], BF16), ("DTi", [64, 8, C], BF16), ("DTs", [64, 8, C], BF16),
                            ("Q0", [64, 8, C], BF16), ("P0", [64, 8, C], BF16), ("inT", [64, 8, C], BF16),
                            ("U0", [64, 8, C], BF16), ("U1", [64, 8, C], BF16), ("V0", [64, 8, C], BF16), ("V1", [64, 8, C], BF16),
                            ("O", [64, 8, C], BF16), ("OT", [64, 8, C], BF16), ("W1", [64, 8, C], BF16), ("W2", [64, 8, C], BF16),
                            ("vb", [64, 8, 64], BF16), ("kbg", [64, 8, 64], BF16), ("kw", [64, 8, 64], BF16),
                            ("bg", [64, 8], F32), ("ew", [64, 8], F32), ("egl", [128, 8], F32),
                            ("u", [64, 8, 64], F32), ("vn", [64, 8, 64], BF16)]:
            self.gd[nm] = (sb("g_" + nm, shp, dt), Dep())
        self.gmask = sb("gmask", [64, 12, 64], BF16); self.d_gmask = Dep()
        self.K.dma("pool", self.gmask[:, :, :], self.gmask_d, self.d_gmask, w=[self.d_gmask])
        if cfg.nss > 0:
            self.ptb = sb("ptb", [128, cfg.npages], I32); self.d_ptb = Dep()
            NG = 2
            self.pgc = [sb("pgc%d" % i, [128, 4, 128]) for i in range(NG)]
            self.pgk = [sb("pgk%d" % i, [128, 4, 32]) for i in range(NG)]
            self.d_pgc = [[Dep() for _ in range(4)] for _ in range(NG)]; self.d_pgk = [[Dep() for _ in range(4)] for _ in range(NG)]
            self.pgcb = [sb("pgcb%d" % i, [128, 4, 129], BF16) for i in range(NG)]; self.d_pgcb = [Dep() for _ in range(NG)]
            self.pgkb = [sb("pgkb%d" % i, [128, 4, 128], BF16) for i in range(NG)]; self.d_pgkb = [Dep() for _ in range(NG)]
            self.pcT = [sb("pcT%d" % i, [128, 512], BF16) for i in range(NG)]; self.d_pcT = [Dep() for _ in range(NG)]
            self.pkT = [sb("pkT%d" % i, [128, 512], BF16) for i in range(NG)]; self.d_pkT = [Dep() for _ in range(NG)]
            self.spT = [sb("spT%d" % i, [128, 256], BF16) for i in range(NG)]; self.d_spT = [Dep() for _ in range(NG)]
            for i in range(NG):
                self.K.memset(self.pgcb[i][:, :, 128:129], 1.0, w=[self.d_pgcb[i]])
            self.newtok = sb("newtok", [8, 129], BF16); self.d_newtok = Dep()
            self.K.memset(self.newtok[:, 128:129], 1.0, w=[self.d_newtok])
            self.so = sb("so", [64, 129]); self.d_so = Dep()
            self.sob = sb("sob", [64, 128], BF16); self.d_sob = Dep()
            self.solT = sb("solT", [128, 64], BF16); self.d_solT = Dep()
            self.pgi = 0

    def run_layer_tile(self, l, tc):
        K, par, d_par = self.K, self.par, self.d_par
        Tt, nseq, tps, s0 = tc["T"], tc["nseq"], tc["tps"], tc["s0"]
        hb, d_h = self.hb, self.d_h
        samp = tc["kind"] == "s"
        hist2 = 2 * nseq if samp else 2
        hist3 = 3 * nseq if samp else 3
        def extv(buf, c, h):
            return buf[:, c, 0:nseq * (h + tps)].rearrange("p (a b) -> p a b", a=nseq)
        self.extv = extv
        if samp:
            for (buf, dd, src, h, nchk) in ((self.uext, self.d_uext, self.st_sconv_d, 2, 4), (self.gext, self.d_gext, self.st_gconv_d, 3, 12),
                                            (self.lext, self.d_lext, self.st_lconv_d, 3, 4)):
                for c in range(nchk):
                    K.dma("pool", extv(buf, c, h)[:, :, 0:h], src[l, :, c, :, :], dd, w=[dd])
            K.dma("pool", self.lstate[:, :, 0:nseq], self.st_lru_d[l], self.d_lstate, w=[self.d_lstate])
        elif tc["first"]:
            for (buf, dd, h, nchk) in ((self.uext, self.d_uext, 2, 4), (self.gext, self.d_gext, 3, 12), (self.lext, self.d_lext, 3, 4)):
                K.memset(buf[:, :, 0:h], 0.0, w=[dd])
            K.memset(self.lstate[:, :, 0:1], 0.0, w=[self.d_lstate])
        else:
            for (buf, dd, h, nchk) in ((self.uext, self.d_uext, 2, 4), (self.gext, self.d_gext, 3, 12), (self.lext, self.d_lext, 3, 4)):
                K.cp(buf[:, :, 0:h], buf[:, :, tps:tps + h], r=[dd], w=[dd], eng="pool")

        wcur = {"b": -1, "wt": None, "dw": None}

        def proj(ci, M):
            b = ci // 4
            if b != wcur["b"]:
                wcur["wt"], wcur["dw"] = self.wload("win", l, b)
                wcur["b"] = b
            wv = wcur["wt"][:, :].rearrange("p (k n) -> p k n", k=8)
            ps, dp = self.ps_next()
            o = (ci % 4) * 128
            for kc in range(8):
                K.mm(ps[0:M, 0:Tt], wv[:, kc, o:o + M], hb[:, kc, 0:Tt], start=(kc == 0), stop=(kc == 7), r=[wcur["dw"], d_h], w=[dp])
            return ps, dp

        v3 = self.v3
        for j in range(4):
            ps, dp = proj(2 * j, 128)
            K.cp(self.cg[:, 0:Tt], ps[:, 0:Tt], r=[dp], w=[self.d_cg], eng="act")
            ps, dp = proj(2 * j + 1, 128)
            K.tt(extv(self.uext, j, 2)[:, :, 2:2 + tps], v3(self.cg[:, 0:Tt], nseq, tps), v3(ps[:, 0:Tt], nseq, tps), ALU.mult,
                 r=[self.d_cg, dp], w=[self.d_uext])
        for j in range(4):
            ps, dp = proj(8 + j, 128)
            e = extv(self.uext, j, 2)
            tv = v3(self.tmpf[:, 0:Tt], nseq, tps)
            K.ts(tv, e[:, :, 0:tps], par[:, l, P_SCW + j:P_SCW + j + 1], ALU.mult, r=[self.d_uext, d_par], w=[self.d_tmpf])
            for tap in (1, 2):
                K.stt(tv, e[:, :, tap:tap + tps], par[:, l, P_SCW + tap * 4 + j:P_SCW + tap * 4 + j + 1], tv, ALU.mult, ALU.add,
                      r=[self.d_uext, d_par, self.d_tmpf], w=[self.d_tmpf])
            K.tt(self.yb[0][:, j, 0:Tt], self.tmpf[:, 0:Tt], ps[:, 0:Tt], ALU.mult, r=[self.d_tmpf, dp], w=[self.d_yb[0]])
        for j in range(2):
            ps, dp = proj(12 + j, 128)
            K.cp(self.qa[:, j, 0:Tt], ps[:, 0:Tt], r=[dp], w=[self.d_qa], eng="act")
        ps, dp = proj(14, 128)
        K.cp(self.ckvr[:, 0:Tt], ps[:, 0:Tt], r=[dp], w=[self.d_ckvr], eng="act")
        ps, dp = proj(15, 128)
        K.tt(self.tmpf[:, 0:Tt], ps[:, 0:Tt], self.ropet[:, 0, 0:Tt], ALU.mult, r=[dp, self.d_rope], w=[self.d_tmpf])
        ps, dp = proj(16, 128)
        K.tt(self.tmpf2[:, 0:Tt], ps[:, 0:Tt], self.ropet[:, 1, 0:Tt], ALU.mult, r=[dp, self.d_rope], w=[self.d_tmpf2])
        K.tt(self.kr[:, 0:Tt], self.tmpf[:, 0:Tt], self.tmpf2[:, 0:Tt], ALU.add, r=[self.d_tmpf, self.d_tmpf2], w=[self.d_kr])
        for j in range(12):
            ps, dp = proj(17 + j, 128)
            e = extv(self.gext, j, 3)
            K.cp(e[:, :, 3:3 + tps], v3(ps[:, 0:Tt], nseq, tps), r=[dp], w=[self.d_gext], eng="act")
            tv = v3(self.tmpf[:, 0:Tt], nseq, tps)
            K.ts(tv, e[:, :, 0:tps], par[:, l, P_GDNW + j:P_GDNW + j + 1], ALU.mult, r=[self.d_gext, d_par], w=[self.d_tmpf])
            for tap in (1, 2, 3):
                K.stt(tv, e[:, :, tap:tap + tps], par[:, l, P_GDNW + tap * 12 + j:P_GDNW + tap * 12 + j + 1], tv, ALU.mult, ALU.add,
                      r=[self.d_gext, d_par, self.d_tmpf], w=[self.d_tmpf])
            if j < 8:
                K.act(self.qk[:, j, 0:Tt], self.tmpf[:, 0:Tt], AF.Silu, r=[self.d_tmpf], w=[self.d_qk])
            else:
                K.act(self.vT[:, j - 8, 0:Tt], self.tmpf[:, 0:Tt], AF.Silu, r=[self.d_tmpf], w=[self.d_vT])
        for j in range(4):
            ps, dp = proj(29 + j, 128)
            K.act(self.zs[:, j, 0:Tt], ps[:, 0:Tt], AF.Silu, r=[dp], w=[self.d_zs])
        ps, dp = proj(33, 8)
        K.cp(self.ga[:, 0:Tt], ps[0:8, 0:Tt], r=[dp], w=[self.d_gab], eng="act")
        ps, dp = proj(34, 8)
        K.cp(self.gb[:, 0:Tt], ps[0:8, 0:Tt], r=[dp], w=[self.d_gab], eng="act")
        for j in range(4):
            ps, dp = proj(35 + j, 128)
            K.cp(extv(self.lext, j, 3)[:, :, 3:3 + tps], v3(ps[:, 0:Tt], nseq, tps), r=[dp], w=[self.d_lext], eng="act")
        for j in range(4):
            ps, dp = proj(39 + j, 128)
            K.act(self.lgel[:, j, 0:Tt], ps[:, 0:Tt], AF.Gelu_apprx_tanh, r=[dp], w=[self.d_lgel])
        self.wm, self.dwm = self.wload("wmisc", l, 0)
        if tc["last"]:
            sl = slice(s0, s0 + nseq)
            for c in range(4):
                self.dma_out(self.o_sconv_d[l, :, c, sl, :], extv(self.uext, c, 2)[:, :, tps:tps + 2], [self.d_uext])
                self.dma_out(self.o_lconv_d[l, :, c, sl, :], extv(self.lext, c, 3)[:, :, tps:tps + 3], [self.d_lext])
            for c in range(12):
                self.dma_out(self.o_gconv_d[l, :, c, sl, :], extv(self.gext, c, 3)[:, :, tps:tps + 3], [self.d_gext])
        skip = getattr(self.cfg, "skip", ())
        gens = []
        for nm, fn, ybi in (("mla", self.mla, 1), ("gdn", self.gdn, 2), ("lru", self.lru, 3)):
            if nm in skip:
                K.memset(self.yb[ybi][:, :, 0:Tt], 0.0, w=[self.d_yb[ybi]])
            else:
                gens.append(fn(l, tc))
        while gens:
            for g in list(gens):
                try:
                    next(g)
                except StopIteration:
                    gens.remove(g)

    def lru(self, l, tc):
        K, par, d_par, v3 = self.K, self.par, self.d_par, self.v3
        Tt, nseq, tps, s0 = tc["T"], tc["nseq"], tc["tps"], tc["s0"]
        lu, lub, la, lb, lhs_, tmpf = self.lu, self.lub, self.la, self.lb, self.lhs_, self.tmpf
        wg = self.wm[:, M_LRUG:M_LRUG + 1024].rearrange("p (c g m) -> p c g m", c=4, g=2)
        for c in range(4):
            e = self.extv(self.lext, c, 3)
            uv = v3(lu[:, 0:Tt], nseq, tps)
            K.ts(uv, e[:, :, 0:tps], par[:, l, P_LRUW + c:P_LRUW + c + 1], ALU.mult, par[:, l, P_LRUB + c:P_LRUB + c + 1], ALU.add,
                 r=[self.d_lext, d_par], w=[self.d_lu])
            for tap in (1, 2, 3):
                K.stt(uv, e[:, :, tap:tap + tps], par[:, l, P_LRUW + tap * 4 + c:P_LRUW + tap * 4 + c + 1], uv, ALU.mult, ALU.add,
                      r=[self.d_lext, d_par, self.d_lu], w=[self.d_lu])
            K.cp(lub[:, 0:Tt], lu[:, 0:Tt], r=[self.d_lu], w=[self.d_lub], eng="act")
            ps, dp = self.ps_next()
            K.mm(ps[:, 0:Tt], wg[:, c, 0, :], lub[:, 0:Tt], r=[self.dwm, self.d_lub], w=[dp])
            K.act(la[:, 0:Tt], ps[:, 0:Tt], AF.Sigmoid, bias=par[:, l, P_BA + c:P_BA + c + 1], r=[dp, d_par], w=[self.d_la])
            ps, dp = self.ps_next()
            K.mm(ps[:, 0:Tt], wg[:, c, 1, :], lub[:, 0:Tt], r=[self.dwm, self.d_lub], w=[dp])
            K.act(lb[:, 0:Tt], ps[:, 0:Tt], AF.Sigmoid, bias=par[:, l, P_BX + c:P_BX + c + 1], r=[dp, d_par], w=[self.d_lb])
            K.act(tmpf[:, 0:Tt], la[:, 0:Tt], AF.Exp, scale=self.lcl2[:, c:c + 1], r=[self.d_la, self.d_lp], w=[self.d_tmpf])
            K.act(la[:, 0:Tt], la[:, 0:Tt], AF.Exp, scale=self.lcl[:, c:c + 1], r=[self.d_la, self.d_lp], w=[self.d_la])
            K.ts(tmpf[:, 0:Tt], tmpf[:, 0:Tt], 1.0, ALU.min, r=[self.d_tmpf], w=[self.d_tmpf])
            K.act(tmpf[:, 0:Tt], tmpf[:, 0:Tt], AF.Sqrt, scale=-1.0, bias=1.0, r=[self.d_tmpf], w=[self.d_tmpf])
            if tc["kind"] == "p" and tc["first"]:
                K.memset(tmpf[:, 0:1], 1.0, w=[self.d_tmpf], eng="dve")
            K.tt(lb[:, 0:Tt], lb[:, 0:Tt], lu[:, 0:Tt], ALU.mult, r=[self.d_lb, self.d_lu], w=[self.d_lb])
            K.tt(lb[:, 0:Tt], lb[:, 0:Tt], tmpf[:, 0:Tt], ALU.mult, r=[self.d_lb, self.d_tmpf], w=[self.d_lb])
            for b in range(nseq):
                seg = slice(b * tps, (b + 1) * tps)
                K.scan(lhs_[:, seg], la[:, seg], lb[:, seg], self.lstate[:, c, b:b + 1], r=[self.d_la, self.d_lb, self.d_lstate], w=[self.d_lhs])
            K.cp(self.lstate[:, c, 0:nseq], v3(lhs_[:, 0:Tt], nseq, tps)[:, :, tps - 1], r=[self.d_lhs], w=[self.d_lstate])
            K.tt(self.yb[3][:, c, 0:Tt], lhs_[:, 0:Tt], self.lgel[:, c, 0:Tt], ALU.mult, r=[self.d_lhs, self.d_lgel], w=[self.d_yb[3]])
            yield
        if tc["last"]:
            self.dma_out(self.o_lru_d[l, :, :, s0:s0 + nseq], self.lstate[:, :, 0:nseq], [self.d_lstate])

    def mla(self, l, tc):
        K, par, d_par = self.K, self.par, self.d_par
        Tt, nseq, tps, s0, c0 = tc["T"], tc["nseq"], tc["tps"], tc["s0"], tc["col0"]
        sq, rstd, tmpf, tmpf2 = self.sq, self.rstd, self.tmpf, self.tmpf2
        wm, dwm = self.wm, self.dwm
        for j in range(2):
            K.act(sq[:, j, 0:Tt], self.qa[:, j, 0:Tt], AF.Square, r=[self.d_qa], w=[self.d_sq])
        ps, dp = self.ps_next()
        for j in range(2):
            K.mm(ps[:, 0:Tt], self.ones_b[:], sq[:, j, 0:Tt], start=(j == 0), stop=(j == 1), r=[self.d_sq, self.d_cb], w=[dp])
        K.act(rstd[:, 0:Tt], ps[:, 0:Tt], AF.Sqrt, scale=1.0 / 256, bias=1e-6, r=[dp], w=[self.d_rstd])
        K.recip(rstd[:, 0:Tt], rstd[:, 0:Tt], r=[self.d_rstd], w=[self.d_rstd])
        for j in range(2):
            K.stt(self.cq[:, j, 0:Tt], self.qa[:, j, 0:Tt], par[:, l, P_GQ + j:P_GQ + j + 1], rstd[:, 0:Tt], ALU.mult, ALU.mult,
                  r=[self.d_qa, d_par, self.d_rstd], w=[self.d_cq])
        K.act(sq[:, 2, 0:Tt], self.ckvr[:, 0:Tt], AF.Square, r=[self.d_ckvr], w=[self.d_sq])
        ps, dp = self.ps_next()
        K.mm(ps[:, 0:Tt], self.ones_b[:], sq[:, 2, 0:Tt], r=[self.d_sq, self.d_cb], w=[dp])
        K.act(rstd[:, 0:Tt], ps[:, 0:Tt], AF.Sqrt, scale=1.0 / 128, bias=1e-6, r=[dp], w=[self.d_rstd])
        K.recip(rstd[:, 0:Tt], rstd[:, 0:Tt], r=[self.d_rstd], w=[self.d_rstd])
        K.stt(self.ckvf[:, 0:Tt], self.ckvr[:, 0:Tt], par[:, l, P_GKV:P_GKV + 1], rstd[:, 0:Tt], ALU.mult, ALU.mult,
              r=[self.d_ckvr, d_par, self.d_rstd], w=[self.d_ckvf])
        self.dma_out(self.o_ckv_d[l, :, c0:c0 + Tt], self.ckvf[:, 0:Tt], [self.d_ckvf])
        self.dma_out(self.o_kpe_d[l, :, c0:c0 + Tt], self.kr[0:32, 0:Tt], [self.d_kr])
        wq = wm[:, M_WQ:M_WQ + 2048].rearrange("p (k n) -> p k n", k=2)
        wkbT = wm[:, M_WKBT:M_WKBT + 512].rearrange("p (a c) -> p a c", a=4)
        for pr in range(4):
            ps, dp = self.ps_next()
            for kc in range(2):
                K.mm(ps[:, 0:Tt], wq[:, kc, pr * 128:(pr + 1) * 128], self.cq[:, kc, 0:Tt], start=(kc == 0), stop=(kc == 1), r=[dwm, self.d_cq], w=[dp])
            K.cp(self.qn2[:, pr, 0:Tt], ps[:, 0:Tt], r=[dp], w=[self.d_qn2], eng="act")
        for h in range(8):
            pr, r0 = h // 2, 64 * (h % 2)
            ps, dp = self.ps_next()
            K.mm(ps[:, 0:Tt], wkbT[r0:r0 + 64, pr, :], self.qn2[r0:r0 + 64, pr, 0:Tt], r=[dwm, self.d_qn2], w=[dp])
            K.cp(self.qabs[:, h, 0:Tt], ps[:, 0:Tt], r=[dp], w=[self.d_qabs], eng=("act" if h % 2 else "dve"))
        for g in range(2):
            ps, dp = self.ps_next()
            for kc in range(2):
                K.mm(ps[:, 0:Tt], wq[:, kc, 512 + g * 128:512 + (g + 1) * 128], self.cq[:, kc, 0:Tt], start=(kc == 0), stop=(kc == 1),
                     r=[dwm, self.d_cq], w=[dp])
            K.tt(tmpf[:, 0:Tt], ps[:, 0:Tt], self.ropet[:, 0, 0:Tt], ALU.mult, r=[dp, self.d_rope], w=[self.d_tmpf])
            ps, dp = self.ps_next()
            for kc in range(2):
                K.mm(ps[:, 0:Tt], wq[:, kc, 768 + g * 128:768 + (g + 1) * 128], self.cq[:, kc, 0:Tt], start=(kc == 0), stop=(kc == 1),
                     r=[dwm, self.d_cq], w=[dp])
            K.tt(tmpf2[:, 0:Tt], ps[:, 0:Tt], self.ropet[:, 1, 0:Tt], ALU.mult, r=[dp, self.d_rope], w=[self.d_tmpf2])
            K.tt(self.qrot[:, g, 0:Tt], tmpf[:, 0:Tt], tmpf2[:, 0:Tt], ALU.add, r=[self.d_tmpf, self.d_tmpf2], w=[self.d_qrot])
            for h4 in range(4):
                h = 4 * g + h4
                K.ts(self.qm[:, h, 0:Tt], self.qrot[:, g, 0:Tt], par[:, l, P_HM + h:P_HM + h + 1], ALU.mult, r=[self.d_qrot, d_par], w=[self.d_qm])
        skip = getattr(self.cfg, "skip", ())
        yield
        if tc["kind"] == "p" and "attnp" not in skip:
            yield from self.attn_prompt(l, tc)
        elif tc["kind"] == "s" and "attns" not in skip:
            yield from self.attn_sample(l, tc)
        else:
            K.memset(self.yb[1][:, :, 0:Tt], 0.0, w=[self.d_yb[1]])

    def attn_prompt(self, l, tc):
        K = self.K
        Tt, pos0 = tc["T"], tc["pos0"]
        kt0 = pos0 // 128
        wvb = self.wm[:, M_WVB:M_WVB + 512]
        K.cp(self.ckvT[:, pos0:pos0 + Tt], self.ckvf[:, 0:Tt], r=[self.d_ckvf], w=[self.d_ckvT], eng="act")
        K.cp(self.kpeR[:, pos0:pos0 + Tt], self.kr[:, 0:Tt], r=[self.d_kr], w=[self.d_kpeR], eng="act")
        for i in range(Tt // 128):
            ps, dp = self.ps_next()
            K.mm(ps[:, 0:128], self.ckvT[:, pos0 + i * 128:pos0 + (i + 1) * 128], self.ident_b[:], r=[self.d_ckvT, self.d_cb], w=[dp])
            K.cp(self.ckvtok[:, kt0 + i, :], ps[:, 0:128], r=[dp], w=[self.d_ckvtok])
        scale = 96.0 ** -0.5
        nkt = (pos0 + Tt) // 128
        pi = 0
        for h in range(8):
            accO, dO = self.psb[3], self.d_ps[3]
            accS, dS = self.psb[4], self.d_ps[4]
            for kt in range(nkt):
                off = kt * 128 - pos0
                q0 = max(off, 0)
                ps, dp = self.ps_next()
                K.mm(ps[:, q0:Tt], self.ckvT[:, kt * 128:(kt + 1) * 128], self.qabs[:, h, q0:Tt], start=True, stop=False,
                     r=[self.d_ckvT, self.d_qabs], w=[dp])
                K.mm(ps[:, q0:Tt], self.kpeR[:, kt * 128:(kt + 1) * 128], self.qm[:, h, q0:Tt], start=False, stop=True,
                     r=[self.d_kpeR, self.d_qm], w=[dp])
                pt, dpt = self.pT[pi % 2], self.d_pT[pi % 2]
                pi += 1
                K.act(pt[:, q0:Tt], ps[:, q0:Tt], AF.Exp, scale=scale, r=[dp], w=[dpt])
                if off >= 0:
                    K.tt(pt[:, q0:q0 + 128], pt[:, q0:q0 + 128], self.triu, ALU.mult, r=[dpt, self.d_const], w=[dpt], eng="pool")
                K.mm(accO[:, q0:Tt], self.ckvtok[:, kt, :], pt[:, q0:Tt], start=(kt == 0), stop=(kt == nkt - 1), r=[self.d_ckvtok, dpt], w=[dO])
                K.mm(accS[:, q0:Tt], self.ones_b[:], pt[:, q0:Tt], start=(kt == 0), stop=(kt == nkt - 1), r=[self.d_cb, dpt], w=[dS])
            K.recip(self.rsum[:, 0:Tt], accS[:, 0:Tt], r=[dS], w=[self.d_rsum])
            K.tt(self.olat[:, 0:Tt], accO[:, 0:Tt], self.rsum[:, 0:Tt], ALU.mult, r=[dO, self.d_rsum], w=[self.d_olat])
            pr, r0 = h // 2, 64 * (h % 2)
            ps, dp = self.ps_next()
            K.mm(ps[r0:r0 + 64, 0:Tt], wvb[:, h * 64:(h + 1) * 64], self.olat[:, 0:Tt], r=[self.dwm, self.d_olat], w=[dp])
            K.cp(self.yb[1][r0:r0 + 64, pr, 0:Tt], ps[r0:r0 + 64, 0:Tt], r=[dp], w=[self.d_yb[1]], eng="act")
            yield

    def setup_pages(self):
        K, cfg = self.K, self.cfg
        self.idx_seq = [self.sb("idx_seq%d" % i, [128, cfg.npages], I32) for i in range(2)]
        self.idxf_seq = self.sb("idxf_seq", [128, cfg.npages])
        self.iota_l = self.sb("iota_l", [128, cfg.depth])
        self.d_idxseq = [Dep(), Dep()]
        self.d_idxf = Dep()
        self.d_iotal = Dep()
        for l in range(cfg.depth):
            K.ts(self.iota_l[:, l:l + 1], self.iota_f, float(l * cfg.npool * 128), ALU.add, r=[self.d_const], w=[self.d_iotal])

    def seq_pages(self, l, b):
        K = self.K
        i = b % 2
        K.dma("pool", self.ptb[:], self.pt_d[b:b + 1, :].partition_broadcast(128), self.d_ptb, w=[self.d_ptb])
        K.cp(self.idxf_seq[:], self.ptb[:], r=[self.d_ptb], w=[self.d_idxf])
        K.ts(self.idxf_seq[:], self.idxf_seq[:], 128.0, ALU.mult, self.iota_l[:, l:l + 1], ALU.add, r=[self.d_idxf, self.d_iotal], w=[self.d_idxf])
        K.cp(self.idx_seq[i][:], self.idxf_seq[:], r=[self.d_idxf], w=[self.d_idxseq[i]])
        return self.idx_seq[i], self.d_idxseq[i]

    def attn_sample(self, l, tc):
        K, cfg = self.K, self.cfg
        nseq, tps = tc["nseq"], tc["tps"]
        NPG = cfg.npages
        wvb = self.wm[:, M_WVB:M_WVB + 512]
        scale = 96.0 ** -0.5
        accO, dO = self.psb[3], self.d_ps[3]
        cflat = self.cckv_d.rearrange("l n c -> (l n) c")
        kflat = self.ckpe_d.rearrange("l n c -> (l n) c")
        ckvb, krb = self.pT[0], self.pT[1]
        Tt = tc["T"]
        K.cp(ckvb[:, 0:Tt], self.ckvf[:, 0:Tt], r=[self.d_ckvf], w=[self.d_pT[0]], eng="act")
        K.cp(krb[:, 0:Tt], self.kr[:, 0:Tt], r=[self.d_kr], w=[self.d_pT[1]], eng="act")
        for b in range(nseq):
            cs = slice(b * tps, (b + 1) * tps)
            qa_b = self.qabs[:, :, cs]
            qm_b = self.qm[:, :, cs]
            ps, dp = self.ps_next()
            K.mm(ps[0:tps, 0:128], ckvb[:, cs], self.ident_b[:], r=[self.d_pT[0], self.d_cb], w=[dp])
            K.cp(self.newtok[0:tps, 0:128], ps[0:tps, 0:128], r=[dp], w=[self.d_newtok])
            first = True
            idxs, d_idxs = self.seq_pages(l, b)
            for g0 in range(0, NPG, 4):
                i = self.pgi % 2
                self.pgi += 1
                ng = min(4, NPG - g0)
                for j in range(ng):
                    K.gather(self.pgc[i][:, j, :], cflat, idxs[:, g0 + j:g0 + j + 1], self.d_pgc[i][j], r=[d_idxs], w=[self.d_pgc[i][j]])
                    K.gather(self.pgk[i][:, j, :], kflat, idxs[:, g0 + j:g0 + j + 1], self.d_pgk[i][j], r=[d_idxs], w=[self.d_pgk[i][j]])
                K.cp(self.pgcb[i][:, 0:ng, 0:128], self.pgc[i][:, 0:ng, :], r=self.d_pgc[i][0:ng], w=[self.d_pgcb[i]])
                for j in range(ng):
                    K.cp(self.pgkb[i][:, j, :].rearrange("p (a b) -> p a b", a=4), self.pgk[i][:, j, :].unsqueeze(1).to_broadcast([128, 4, 32]),
                         r=[self.d_pgk[i][j]], w=[self.d_pgkb[i]], eng="act")
                psT, dpT_ = self.ps_next()
                for j in range(ng):
                    K.mm(psT[:, j * 128:(j + 1) * 128], self.pgcb[i][:, j, 0:128], self.ident_b[:], r=[self.d_pgcb[i], self.d_cb], w=[dpT_])
                K.cp(self.pcT[i][:, 0:ng * 128], psT[:, 0:ng * 128], r=[dpT_], w=[self.d_pcT[i]])
                psK, dpK = self.ps_next()
                for j in range(ng):
                    K.mm(psK[:, j * 128:(j + 1) * 128], self.pgkb[i][:, j, :], self.ident_b[:], r=[self.d_pgkb[i], self.d_cb], w=[dpK])
                K.cp(self.pkT[i][:, 0:ng * 128], psK[:, 0:ng * 128], r=[dpK], w=[self.d_pkT[i]], eng="act")
                psS, dpS = self.ps_next()
                for j in range(ng):
                    K.mm(psS[:, j * 64:(j + 1) * 64], self.pcT[i][:, j * 128:(j + 1) * 128], qa_b, start=True, stop=False,
                         r=[self.d_pcT[i], self.d_qabs], w=[dpS])
                    K.mm(psS[:, j * 64:(j + 1) * 64], self.pkT[i][:, j * 128:(j + 1) * 128], qm_b, start=False, stop=True,
                         r=[self.d_pkT[i], self.d_qm], w=[dpS])
                K.act(self.spT[i][:, 0:ng * 64], psS[:, 0:ng * 64], AF.Exp, scale=scale, r=[dpS], w=[self.d_spT[i]])
                for j in range(ng):
                    K.mm(accO[0:64, 0:129], self.spT[i][:, j * 64:(j + 1) * 64], self.pgcb[i][:, j, :], start=first, stop=False,
                         r=[self.d_spT[i], self.d_pgcb[i]], w=[dO])
                    first = False
            i = self.pgi % 2
            self.pgi += 1
            psS, dpS = self.ps_next()
            K.mm(psS[0:tps, 0:64], ckvb[:, cs], qa_b, start=True, stop=False, r=[self.d_pT[0], self.d_qabs], w=[dpS])
            K.mm(psS[0:tps, 0:64], krb[:, cs], qm_b, start=False, stop=True, r=[self.d_pT[1], self.d_qm], w=[dpS])
            K.act(self.spT[i][0:tps, 0:64], psS[0:tps, 0:64], AF.Exp, scale=scale, r=[dpS], w=[self.d_spT[i]])
            spv = self.spT[i][0:tps, 0:64].rearrange("p (a b) -> p a b", a=8)
            K.tt(spv, spv, self.triu[0:tps, 0:tps].unsqueeze(1).to_broadcast([tps, 8, tps]), ALU.mult, r=[self.d_spT[i], self.d_const], w=[self.d_spT[i]])
            K.mm(accO[0:64, 0:129], self.spT[i][0:tps, 0:64], self.newtok[0:tps, :], start=first, stop=True,
                 r=[self.d_spT[i], self.d_newtok], w=[dO])
            K.cp(self.so[:, :], accO[0:64, 0:129], r=[dO], w=[self.d_so])
            K.recip(self.so[:, 128:129], self.so[:, 128:129], r=[self.d_so], w=[self.d_so])
            K.ts(self.sob[:, :], self.so[:, 0:128], self.so[:, 128:129], ALU.mult, r=[self.d_so], w=[self.d_sob])
            ps, dp = self.ps_next()
            K.mm(ps[:, 0:64], self.sob[:, :], self.ident_b[0:64, 0:64], r=[self.d_sob, self.d_cb], w=[dp])
            K.cp(self.solT[:, :], ps[:, 0:64], r=[dp], w=[self.d_solT], eng="act")
            for h in range(8):
                pr, r0 = h // 2, 64 * (h % 2)
                ps, dp = self.ps_next()
                K.mm(ps[r0:r0 + 64, 0:tps], wvb[:, h * 64:(h + 1) * 64], self.solT[:, h * tps:(h + 1) * tps], r=[self.dwm, self.d_solT], w=[dp])
                K.cp(self.yb[1][r0:r0 + 64, pr, cs], ps[r0:r0 + 64, 0:tps], r=[dp], w=[self.d_yb[1]], eng="act")
            yield

    def gdn(self, l, tc):
        K, par, d_par, v3 = self.K, self.par, self.d_par, self.v3
        Tt, nseq, tps, s0 = tc["T"], tc["nseq"], tc["tps"], tc["s0"]
        samp = tc["kind"] == "s"
        C = tps if samp else 64
        NL = {64: 6, 8: 3}[C]
        sq, rstd, tmpf = self.sq, self.rstd, self.tmpf
        gd = self.gd
        gstop = getattr(self.cfg, 'gstop', 0)
        if gstop:
            K.memset(self.oT[:, :, 0:Tt], 0.0, w=[self.d_oT])
        for j in range(8):
            K.act(sq[:, j, 0:Tt], self.qk[:, j, 0:Tt], AF.Square, r=[self.d_qk], w=[self.d_sq])
            ps, dp = self.ps_next()
            K.mm(ps[:, 0:Tt], self.bones_b[:], sq[:, j, 0:Tt], r=[self.d_sq, self.d_cb], w=[dp])
            K.act(rstd[:, 0:Tt], ps[:, 0:Tt], AF.Sqrt, bias=1e-6, r=[dp], w=[self.d_rstd])
            K.recip(rstd[:, 0:Tt], rstd[:, 0:Tt], r=[self.d_rstd], w=[self.d_rstd])
            dst = self.qnT[:, j, 0:Tt] if j < 4 else self.knT[:, j - 4, 0:Tt]
            K.stt(dst, self.qk[:, j, 0:Tt], (0.125 if j < 4 else 1.0), rstd[:, 0:Tt], ALU.mult, ALU.mult,
                  r=[self.d_qk, self.d_rstd], w=[self.d_gT])
            if j >= 4:
                K.cp(self.knTm[0:64, 0, j - 4, 0:Tt], self.knT[0:64, j - 4, 0:Tt], r=[self.d_gT], w=[self.d_gT], eng="act")
                K.cp(self.knTm[64:128, 1, j - 4, 0:Tt], self.knT[64:128, j - 4, 0:Tt], r=[self.d_gT], w=[self.d_gT], eng="act")
        if gstop and gstop <= 1:
            self._gdn_tail(l, tc)
            return
        betaf, gcf, egcf, gfm = self.betaf, self.gcf, self.egcf, self.gfm
        d_scal = self.d_scal
        K.act(betaf[:, 0:Tt], self.gb[:, 0:Tt], AF.Sigmoid, r=[self.d_gab], w=[d_scal])
        K.act(gfm[:, 0:Tt], self.ga[:, 0:Tt], AF.Exp, bias=par[0:8, l, P_DTB:P_DTB + 1], r=[self.d_gab, d_par], w=[self.d_gfm])
        K.act(gfm[:, 0:Tt], gfm[:, 0:Tt], AF.Ln, bias=1.0, r=[self.d_gfm], w=[self.d_gfm])
        K.ts(gfm[:, 0:Tt], gfm[:, 0:Tt], self.negA[:, 0:1], ALU.mult, r=[self.d_gfm, self.d_lp], w=[self.d_gfm])
        nchunk = Tt // C
        for n in range(nchunk):
            cs = slice(n * C, (n + 1) * C)
            K.scan(gcf[:, cs], self.ones_f[0:8, 0:C], gfm[:, cs], 0.0, r=[self.d_gfm, self.d_cb], w=[d_scal])
        K.act(egcf[:, 0:Tt], gcf[:, 0:Tt], AF.Exp, r=[d_scal], w=[d_scal])
        if gstop and gstop <= 2:
            self._gdn_tail(l, tc)
            return
        for pr in range(4):
            ps, dp = self.ps_next()
            K.mm(ps[:, 0:Tt], self.eexp[:, pr, :], betaf[:, 0:Tt], r=[self.d_const, d_scal], w=[dp])
            K.tt(self.kbT[:, pr, 0:Tt], self.knT[:, pr, 0:Tt], ps[:, 0:Tt], ALU.mult, r=[self.d_gT, dp], w=[self.d_gT])
            ps, dp = self.ps_next()
            K.mm(ps[:, 0:Tt], self.eexp[:, pr, :], egcf[:, 0:Tt], r=[self.d_const, d_scal], w=[dp])
            K.tt(self.qgT[:, pr, 0:Tt], self.qnT[:, pr, 0:Tt], ps[:, 0:Tt], ALU.mult, r=[self.d_gT, dp], w=[self.d_gT])
        if gstop and gstop <= 3:
            self._gdn_tail(l, tc)
            return
        Sst, Sbm, d_S = self.Sst, self.Sbm, self.d_S
        ident_b, d_cb = self.ident_b, self.d_cb

        def G(nm):
            return gd[nm]

        yield
        for n in range(nchunk):
            cs = slice(n * C, (n + 1) * C)
            b = n if samp else 0
            if samp:
                K.dma("pool", Sst[:], self.st_gdn_d[l, b], d_S, w=[d_S])
                K.cp(Sbm[0:64, 0], Sst[0:64], r=[d_S], w=[d_S])
                K.cp(Sbm[64:128, 1], Sst[64:128], r=[d_S], w=[d_S], eng="act")
            elif tc["first"] and n == 0:
                K.memset(Sst[:], 0.0, w=[d_S], eng="dve")
                K.memset(Sbm[:, :, :, :], 0.0, w=[d_S], eng="dve")
            tok, d_tok = G("tok")
            ps, dp = self.ps_next(1)
            for qi, srcf in enumerate((betaf, gcf, egcf)):
                K.mm(ps[0:C, qi * 8:(qi + 1) * 8], srcf[:, cs], self.identf[0:8, 0:8], r=[d_scal, self.d_const], w=[dp])
            K.cp(tok[0:C, 0:24], ps[0:C, 0:24], r=[dp], w=[d_tok], eng="act")
            beta_t, gc_t, egc_t = tok[0:C, 0:8], tok[0:C, 8:16], tok[0:C, 16:24]
            rhsR, d_rhsR = G("rhsR")
            K.tt(rhsR[:, :, 0:C], gcf[:, cs].unsqueeze(1).to_broadcast([8, 8, C]), self.i8.unsqueeze(2).to_broadcast([8, 8, C]), ALU.mult,
                 r=[d_scal, self.d_const], w=[d_rhsR])
            psR, dpR = self.ps_next(1)
            K.mm(psR[:, 0:8 * C], self.ones_f[0:8, :], rhsR[:, :, 0:C], r=[d_cb, d_rhsR], w=[dpR])
            Rv = psR[:, 0:8 * C].rearrange("p (a b) -> p a b", a=8)
            d1, d_d1 = G("d1"); d2, d_d2 = G("d2"); Ds, d_Ds = G("Ds"); DTi, d_DTi = G("DTi"); DTs, d_DTs = G("DTs")
            K.tt(d1[0:C, :, 0:C], Rv[0:C], gc_t.unsqueeze(2).to_broadcast([C, 8, C]), ALU.subtract, r=[dpR, d_tok], w=[d_d1])
            K.tt(d2[0:C, :, 0:C], d1[0:C, :, 0:C], self.nbs[0:C, 0:C].unsqueeze(1).to_broadcast([C, 8, C]), ALU.max, r=[d_d1, self.d_const], w=[d_d2])
            K.act(Ds[0:C, :, 0:C], d2[0:C, :, 0:C], AF.Exp, scale=-1.0, r=[d_d2], w=[d_Ds])
            K.tt(d2[0:C, :, 0:C], d1[0:C, :, 0:C], self.nbt[0:C, 0:C].unsqueeze(1).to_broadcast([C, 8, C]), ALU.min, r=[d_d1, self.d_const, d_Ds], w=[d_d2])
            K.act(DTi[0:C, :, 0:C], d2[0:C, :, 0:C], AF.Exp, r=[d_d2], w=[d_DTi])
            K.tt(DTs[0:C, :, 0:C], DTi[0:C, :, 0:C], self.offd[0:C, 0:C].unsqueeze(1).to_broadcast([C, 8, C]), ALU.mult, r=[d_DTi, self.d_const], w=[d_DTs])
            egl, d_egl = G("egl")
            K.act(egl[:, :], Rv[:, :, C - 1], AF.Exp, r=[dpR], w=[d_egl])
            ew, d_ew = G("ew"); bg, d_bg = G("bg")
            K.tt(ew[0:C, :], Rv[0:C, :, C - 1], gc_t, ALU.subtract, r=[dpR, d_tok], w=[d_ew])
            K.act(ew[0:C, :], ew[0:C, :], AF.Exp, r=[d_ew], w=[d_ew])
            K.tt(bg[0:C, :], beta_t, egc_t, ALU.mult, r=[d_tok], w=[d_bg])
            if gstop and gstop <= 4:
                continue
            psA, dpA = self.ps_next(1)
            psAT, dpAT = self.ps_next(1)
            psQ, dpQ = self.ps_next(1)
            for h in range(8):
                pr, r0 = h // 2, 64 * (h % 2)
                hf = h % 2
                K.mm(psA[0:C, h * C:(h + 1) * C], self.kbT[:, pr, cs], self.knTm[:, hf, pr, cs], r=[self.d_gT], w=[dpA])
                K.mm(psAT[0:C, h * C:(h + 1) * C], self.knTm[:, hf, pr, cs], self.kbT[:, pr, cs], r=[self.d_gT], w=[dpAT])
                K.mm(psQ[0:C, h * C:(h + 1) * C], self.knTm[:, hf, pr, cs], self.qnT[:, pr, cs], r=[self.d_gT], w=[dpQ])
            if gstop and gstop <= 5:
                continue
            (Q0, dQ0), (P0, dP0) = G("Q0"), G("P0")
            inT, d_inT = G("inT")

            def pv(ps):
                return ps[0:C, 0:8 * C].rearrange("p (a b) -> p a b", a=8)

            def mk(li, tr):
                return self.gmask[0:C, (6 if tr else 0) + li, 0:C].unsqueeze(1).to_broadcast([C, 8, C])
            K.tt(Q0[0:C, :, 0:C], pv(psA), Ds[0:C, :, 0:C], ALU.mult, r=[dpA, d_Ds], w=[dQ0])
            K.tt(P0[0:C, :, 0:C], pv(psAT), DTs[0:C, :, 0:C], ALU.mult, r=[dpAT, d_DTs], w=[dP0])
            K.tt(inT[0:C, :, 0:C], pv(psQ), DTi[0:C, :, 0:C], ALU.mult, r=[dpQ, d_DTi], w=[d_inT])
            if gstop and gstop <= 6:
                continue
            U = [G("U0"), G("U1")]; V = [G("V0"), G("V1")]
            (O, dO_), (OT, dOT) = G("O"), G("OT")
            (W1, dW1), (W2, dW2) = G("W1"), G("W2")
            idb = self.identf[0:C, 0:C].unsqueeze(1).to_broadcast([C, 8, C])
            K.tt(O[0:C, :, 0:C], Q0[0:C, :, 0:C], mk(0, False), ALU.mult, r=[dQ0, self.d_gmask], w=[dO_])
            K.stt(U[0][0][0:C, :, 0:C], O[0:C, :, 0:C], -1.0, idb, ALU.mult, ALU.add, r=[dO_, self.d_const], w=[U[0][1]])
            K.tt(OT[0:C, :, 0:C], P0[0:C, :, 0:C], mk(0, True), ALU.mult, r=[dP0, self.d_gmask], w=[dOT])
            K.stt(V[0][0][0:C, :, 0:C], OT[0:C, :, 0:C], -1.0, idb, ALU.mult, ALU.add, r=[dOT, self.d_const], w=[V[0][1]])
            cur = 0
            levels = [sz for sz in (2, 4, 8, 16, 32) if sz < C]
            for li, sz in enumerate(levels):
                lastl = (li == len(levels) - 1)
                nxt = 1 - cur
                (Uc, dUc), (Vc, dVc) = U[cur], V[cur]
                (Un, dUn), (Vn, dVn) = U[nxt], V[nxt]
                K.tt(O[0:C, :, 0:C], Q0[0:C, :, 0:C], mk(li + 1, False), ALU.mult, r=[dQ0, self.d_gmask], w=[dO_])
                psw2, dpw2 = self.ps_next(1)
                for h in range(8):
                    K.mm(psw2[0:C, h * C:(h + 1) * C], O[0:C, h, 0:C], Vc[0:C, h, 0:C], r=[dO_, dVc], w=[dpw2])
                K.cp(W2[0:C, :, 0:C], pv(psw2), r=[dpw2], w=[dW2], eng="act")
                psv_, dpv_ = self.ps_next(1)
                for h in range(8):
                    K.mm(psv_[0:C, h * C:(h + 1) * C], Uc[0:C, h, 0:C], W2[0:C, h, 0:C], r=[dUc, dW2], w=[dpv_])
                K.tt(Vn[0:C, :, 0:C], Vc[0:C, :, 0:C], pv(psv_), ALU.subtract, r=[dVc, dpv_], w=[dVn])
                if not lastl:
                    K.tt(OT[0:C, :, 0:C], P0[0:C, :, 0:C], mk(li + 1, True), ALU.mult, r=[dP0, self.d_gmask], w=[dOT])
                    psw1, dpw1 = self.ps_next(1)
                    for h in range(8):
                        K.mm(psw1[0:C, h * C:(h + 1) * C], OT[0:C, h, 0:C], Uc[0:C, h, 0:C], r=[dOT, dUc], w=[dpw1])
                    K.cp(W1[0:C, :, 0:C], pv(psw1), r=[dpw1], w=[dW1], eng="act")
                    psu_, dpu_ = self.ps_next(1)
                    for h in range(8):
                        K.mm(psu_[0:C, h * C:(h + 1) * C], Vc[0:C, h, 0:C], W1[0:C, h, 0:C], r=[dVc, dW1], w=[dpu_])
                    K.tt(Un[0:C, :, 0:C], Uc[0:C, :, 0:C], pv(psu_), ALU.subtract, r=[dUc, dpu_], w=[dUn])
                cur = nxt
            TT, dTT = V[cur]
            if gstop and gstop <= 7:
                continue
            psk, dpk = self.ps_next(1)
            psv, dpv = self.ps_next(1)
            for pr in range(4):
                K.mm(psk[0:C, pr * 128:(pr + 1) * 128], self.knT[:, pr, cs], ident_b[:], r=[self.d_gT, d_cb], w=[dpk])
                K.mm(psv[0:C, pr * 128:(pr + 1) * 128], self.vT[:, pr, cs], ident_b[:], r=[self.d_vT, d_cb], w=[dpv])
            vb, d_vb = G("vb"); kbg, d_kbg = G("kbg"); kw, d_kw = G("kw")

            def p64(ps):
                return ps[0:C, 0:512].rearrange("p (a b) -> p a b", a=8)
            K.tt(vb[0:C], p64(psv), beta_t.unsqueeze(2).to_broadcast([C, 8, 64]), ALU.mult, r=[dpv, d_tok], w=[d_vb])
            K.tt(kbg[0:C], p64(psk), bg[0:C, :].unsqueeze(2).to_broadcast([C, 8, 64]), ALU.mult, r=[dpk, d_bg], w=[d_kbg])
            K.tt(kw[0:C], p64(psk), ew[0:C, :].unsqueeze(2).to_broadcast([C, 8, 64]), ALU.mult, r=[dpk, d_ew], w=[d_kw])
            if gstop and gstop <= 8:
                continue
            psu, dpu = self.ps_next(1)
            psw, dpw = self.ps_next(1)
            for h in range(8):
                pr, r0 = h // 2, 64 * (h % 2)
                K.mm(psu[0:C, h * 64:(h + 1) * 64], TT[0:C, h, 0:C], vb[0:C, h, :], r=[dTT, d_vb], w=[dpu])
                K.mm(psw[r0:r0 + 64, pr * C:(pr + 1) * C], kbg[0:C, h, :], TT[0:C, h, 0:C], r=[dTT, d_kbg], w=[dpw])
            u, d_u = G("u"); vn, d_vn = G("vn")
            wT, d_wT = self.wT, self.d_wTm
            K.cp(u[0:C], p64(psu), r=[dpu], w=[d_u], eng="act")
            K.cp(wT[:, :, 0:C], psw[:, 0:4 * C].rearrange("p (a b) -> p a b", a=4), r=[dpw], w=[d_wT])
            if gstop and gstop <= 9:
                continue
            pss, dps = self.ps_next(1)
            for h in range(8):
                pr, r0 = h // 2, 64 * (h % 2)
                K.mm(pss[0:C, h * 64:(h + 1) * 64], wT[:, pr, 0:C], Sbm[:, h % 2, pr, :], r=[d_wT, d_S], w=[dps])
            K.tt(vn[0:C], u[0:C], p64(pss), ALU.subtract, r=[d_u, dps], w=[d_vn])
            if gstop and gstop <= 10:
                continue
            pso, dpo = self.ps_next(1)
            psS, dpS = self.ps_next(1)
            for h in range(8):
                pr, r0 = h // 2, 64 * (h % 2)
                K.mm(pso[r0:r0 + 64, pr * C:(pr + 1) * C], Sbm[:, h % 2, pr, :], self.qgT[:, pr, cs], start=True, stop=False,
                     r=[d_S, self.d_gT], w=[dpo])
                K.mm(pso[r0:r0 + 64, pr * C:(pr + 1) * C], vn[0:C, h, :], inT[0:C, h, 0:C], start=False, stop=True, r=[d_vn, d_inT], w=[dpo])
                K.mm(psS[r0:r0 + 64, pr * 64:(pr + 1) * 64], kw[0:C, h, :], vn[0:C, h, :], r=[d_kw, d_vn], w=[dpS])
            K.cp(self.oT[:, :, cs], pso[:, 0:4 * C].rearrange("p (a b) -> p a b", a=4), r=[dpo], w=[self.d_oT], eng="act")
            eglv = egl[:, :].rearrange("p (a b) -> p a b", b=2)
            K.tt(Sst[0:64], Sst[0:64], eglv[0:64, :, 0].unsqueeze(2).to_broadcast([64, 4, 64]), ALU.mult, r=[d_S, d_egl], w=[d_S])
            K.tt(Sst[64:128], Sst[64:128], eglv[64:128, :, 1].unsqueeze(2).to_broadcast([64, 4, 64]), ALU.mult, r=[d_S, d_egl], w=[d_S])
            K.tt(Sst[:], Sst[:], psS[:, 0:256].rearrange("p (a b) -> p a b", a=4), ALU.add, r=[d_S, dpS], w=[d_S])
            K.cp(Sbm[0:64, 0], Sst[0:64], r=[d_S], w=[d_S], eng="act")
            K.cp(Sbm[64:128, 1], Sst[64:128], r=[d_S], w=[d_S])
            if samp or (tc["last"] and n == nchunk - 1):
                self.dma_out(self.o_gdn_d[l, s0 + b], Sst[:], [d_S])
            yield
        self._gdn_tail(l, tc)

    def _gdn_tail(self, l, tc):
        K, par, d_par = self.K, self.par, self.d_par
        Tt = tc["T"]
        sq, rstd, tmpf = self.sq, self.rstd, self.tmpf
        for pr in range(4):
            K.act(sq[:, pr, 0:Tt], self.oT[:, pr, 0:Tt], AF.Square, r=[self.d_oT], w=[self.d_sq])
            ps, dp = self.ps_next()
            K.mm(ps[:, 0:Tt], self.bones_b[:], sq[:, pr, 0:Tt], r=[self.d_sq, self.d_cb], w=[dp])
            K.act(rstd[:, 0:Tt], ps[:, 0:Tt], AF.Sqrt, scale=1.0 / 64, bias=1e-6, r=[dp], w=[self.d_rstd])
            K.recip(rstd[:, 0:Tt], rstd[:, 0:Tt], r=[self.d_rstd], w=[self.d_rstd])
            K.tt(tmpf[:, 0:Tt], self.oT[:, pr, 0:Tt], rstd[:, 0:Tt], ALU.mult, r=[self.d_oT, self.d_rstd], w=[self.d_tmpf])
            K.stt(self.yb[2][:, pr, 0:Tt], tmpf[:, 0:Tt], par[:, l, P_GGDN:P_GGDN + 1], self.zs[:, pr, 0:Tt], ALU.mult, ALU.mult,
                  r=[self.d_tmpf, d_par, self.d_zs], w=[self.d_yb[2]])


def _blk(W, KC, NB):
    L, Kd, N = W.shape
    nb = N // NB
    return np.ascontiguousarray(W.reshape(L, KC, 128, nb, NB).transpose(0, 3, 2, 1, 4)).reshape(L, nb, 128, KC * NB)


def _fm(v, nch):
    lead = v.shape[:-1]
    a = v.reshape(lead + (nch, 128))
    return np.moveaxis(a, -1, 0)


def prep_shared(inp, cfg):
    L = cfg.depth
    f32 = np.float32
    sh = {}
    sh["wada"] = _blk(np.asarray(inp["w_ada"][:L], f32), 8, 512)
    win = np.asarray(inp["w_in"][:L], f32)
    wp = np.zeros((L, 1024, 44 * 128), f32)
    sc = win[:, :, 0:1536]
    bg_, cg_, xt_ = sc[:, :, 0:512], sc[:, :, 512:1024], sc[:, :, 1024:1536]
    for j in range(4):
        wp[:, :, (2 * j) * 128:(2 * j + 1) * 128] = cg_[:, :, j * 128:(j + 1) * 128]
        wp[:, :, (2 * j + 1) * 128:(2 * j + 2) * 128] = xt_[:, :, j * 128:(j + 1) * 128]
        wp[:, :, (8 + j) * 128:(9 + j) * 128] = bg_[:, :, j * 128:(j + 1) * 128]
    wp[:, :, 12 * 128:14 * 128] = win[:, :, 1536:1792]
    wp[:, :, 14 * 128:15 * 128] = win[:, :, 1792:1920]
    kpe = win[:, :, 1920:1952]
    kpes = np.concatenate([kpe[:, :, 16:32], kpe[:, :, 0:16]], axis=2)
    for rep in range(4):
        wp[:, :, 15 * 128 + rep * 32:15 * 128 + (rep + 1) * 32] = kpe
        wp[:, :, 16 * 128 + rep * 32:16 * 128 + (rep + 1) * 32] = kpes
    wp[:, :, 17 * 128:29 * 128] = win[:, :, 1952:3488]
    wp[:, :, 29 * 128:33 * 128] = win[:, :, 3488:4000]
    wp[:, :, 33 * 128:33 * 128 + 8] = win[:, :, 4000:4008]
    wp[:, :, 34 * 128:34 * 128 + 8] = win[:, :, 4008:4016]
    wp[:, :, 35 * 128:39 * 128] = win[:, :, 4016:4528]
    wp[:, :, 39 * 128:43 * 128] = win[:, :, 4528:5040]
    sh["win"] = _blk(wp, 8, 512)
    sh["wmg"] = _blk(np.asarray(inp["w_merge_gate"][:L], f32), 8, 512)
    wbo = np.asarray(inp["w_branch_out"][:L], f32)
    sh["wbo"] = np.ascontiguousarray(wbo.reshape(L, 4, 4, 128, 1024).transpose(0, 1, 3, 2, 4)).reshape(L, 4, 128, 4096)
    sh["wmo"] = _blk(np.asarray(inp["w_mix_out"][:L], f32), 8, 512)
    wfi = np.asarray(inp["w_ffn_in"][:L], f32)
    g = wfi[:, :, :FFN].reshape(L, 1024, 22, 1, 128)
    u = wfi[:, :, FFN:].reshape(L, 1024, 22, 1, 128)
    sh["wfi"] = _blk(np.concatenate([g, u], axis=3).reshape(L, 1024, 2 * FFN), 8, 512)
    sh["wfo"] = _blk(np.asarray(inp["w_ffn_out"][:L], f32), 22, 128)
    wm = np.zeros((L, 128, 4096), f32)
    wqb = np.asarray(inp["w_qb"][:L], f32).reshape(L, 256, 8, 96)
    nope = wqb[..., 0:64].reshape(L, 256, 512)
    pe = wqb[..., 64:96]
    peA = pe.reshape(L, 256, 256)
    peB = np.concatenate([pe[..., 16:32], pe[..., 0:16]], axis=-1).reshape(L, 256, 256)
    wq = np.concatenate([nope, peA, peB], axis=2)
    wm[:, :, M_WQ:M_WQ + 2048] = wq.reshape(L, 2, 128, 1024).transpose(0, 2, 1, 3).reshape(L, 128, 2048)
    wkvb = np.asarray(inp["w_kvb"][:L], f32)
    kb = wkvb[..., 0:64].reshape(L, 128, 4, 2, 64)
    wm[:, :, M_WKBT:M_WKBT + 512] = kb.transpose(0, 3, 4, 2, 1).reshape(L, 128, 512)
    wm[:, :, M_WVB:M_WVB + 512] = wkvb[..., 64:128].reshape(L, 128, 512)
    G = np.zeros((L, 128, 4, 2, 128), f32)
    for gi, nm in enumerate(("w_lru_gate_a", "w_lru_gate_x")):
        W = np.asarray(inp[nm][:L], f32)
        for c in range(4):
            for half in range(2):
                G[:, half * 64:(half + 1) * 64, c, gi, half * 64:(half + 1) * 64] = W[:, 2 * c + half]
    wm[:, :, M_LRUG:M_LRUG + 1024] = G.reshape(L, 128, 1024)
    sh["wmisc"] = wm
    par = np.zeros((128, L, NPAR), f32)

    def put(col, v, nch):
        par[:, :, col:col + nch] = _fm(np.asarray(v[:L], f32), nch)
    put(P_BADA, inp["b_ada"], 48)
    put(P_GMIX, inp["g_norm_mix"], 8)
    put(P_GFFN, inp["g_norm_ffn"], 8)
    w = np.asarray(inp["w_sc_conv"][:L], f32)
    for tap in range(3):
        par[:, :, P_SCW + tap * 4:P_SCW + tap * 4 + 4] = _fm(w[:, tap], 4)
    put(P_GQ, inp["g_q_norm"], 2)
    put(P_GKV, inp["g_kv_norm"], 1)
    w = np.asarray(inp["w_gdn_conv"][:L], f32)
    for tap in range(4):
        par[:, :, P_GDNW + tap * 12:P_GDNW + tap * 12 + 12] = _fm(w[:, tap], 12)
    w = np.asarray(inp["w_lru_conv"][:L], f32)
    for tap in range(4):
        par[:, :, P_LRUW + tap * 4:P_LRUW + tap * 4 + 4] = _fm(w[:, tap], 4)
    put(P_LRUB, inp["b_lru_conv"], 4)
    put(P_BA, inp["b_lru_gate_a"], 4)
    put(P_BX, inp["b_lru_gate_x"], 4)
    put(P_LAM, inp["lru_lambda"], 4)
    gg = np.asarray(inp["g_gdn_norm"][:L], f32)
    par[:, :, P_GGDN] = np.concatenate([gg, gg], axis=1).T
    for h in range(8):
        par[:, :, P_HM + h] = ((np.arange(128) // 32) == (h % 4)).astype(f32)[:, None]
    par[0:8, :, P_DTB] = np.asarray(inp["gdn_dt_bias"][:L], f32).T
    par[0:8, :, P_ALOG] = np.asarray(inp["gdn_a_log"][:L], f32).T
    sh["par"] = par
    sh["gfin"] = np.ascontiguousarray(_fm(np.asarray(inp["g_final"], f32), 8))
    cst = np.zeros((128, NCONST), f32)
    p = np.arange(128)[:, None]
    x = np.arange(128)[None, :]
    cst[:, C_ID:C_ID + 128] = (p == x)
    cst[:, C_BONES:C_BONES + 128] = ((p // 64) == (x // 64))
    cst[:, C_NBS:C_NBS + 128] = np.where(x < p, 0.0, 1e4)
    cst[:, C_NBT:C_NBT + 128] = np.where(x >= p, 0.0, -1e4)
    cst[:, C_OFFD:C_OFFD + 128] = (p != x)
    cst[:, C_TRIU:C_TRIU + 128] = (x >= p)
    ee = np.zeros((8, 4, 128), f32)
    for h in range(8):
        ee[h, h // 2, (h % 2) * 64:(h % 2) * 64 + 64] = 1.0
    cst[0:8, C_EEXP:C_EEXP + 512] = ee.reshape(8, 512)
    cst[0:8, C_I8:C_I8 + 8] = np.eye(8)
    cst[:, C_IOTA] = np.arange(128)
    sh["const"] = cst
    gm = np.zeros((64, 12, 64), f32)
    ii = np.arange(64)[:, None]
    jj = np.arange(64)[None, :]
    for li, sz in enumerate((1, 2, 4, 8, 16, 32)):
        m_ = ((ii // (2 * sz)) == (jj // (2 * sz))) & ((ii % (2 * sz)) >= sz) & ((jj % (2 * sz)) < sz)
        gm[:, li, :] = m_
        gm[:, 6 + li, :] = m_.T
    sh["gmask"] = gm
    return sh


def prep_core(inp, cfg, sh, pseqs, sseqs):
    f32 = np.float32
    L = cfg.depth
    m = dict(sh)
    xp = np.asarray(inp["x_prompt"], f32)[pseqs].reshape(-1, D)
    xs = np.asarray(inp["x_sample"], f32)[sseqs].reshape(-1, D)
    X = np.concatenate([xp, xs], axis=0)
    NTOK = X.shape[0]
    m["xin"] = np.ascontiguousarray(X.T).reshape(8, 128, NTOK)
    cc = np.concatenate([np.asarray(inp["c_prompt"], f32)[pseqs], np.asarray(inp["c_sample"], f32)[sseqs]], axis=0)
    m["cT"] = np.ascontiguousarray(cc.T.reshape(8, 128, -1).transpose(1, 0, 2))
    pos = np.concatenate([np.tile(np.arange(cfg.seq), len(pseqs)), np.tile(cfg.past + np.arange(cfg.ts), len(sseqs))]).astype(f32)
    inv = (np.float32(10000.0) ** (-np.arange(16, dtype=f32) / np.float32(16))).astype(f32)
    ang = (pos[:, None] * inv[None, :]).astype(f32)
    cos, sin = np.cos(ang).astype(f32), np.sin(ang).astype(f32)
    rope = np.zeros((128, 2, NTOK), f32)
    for p in range(128):
        f, half = p % 16, (p % 32) // 16
        rope[p, 0] = cos[:, f]
        rope[p, 1] = sin[:, f] if half == 1 else -sin[:, f]
    m["rope"] = rope
    ss = list(sseqs)
    m["st_sconv"] = np.ascontiguousarray(_fm(np.asarray(inp["state_sconv"], f32)[:L][:, ss], 4).transpose(0, 1, 4, 2, 3).transpose(1, 0, 2, 3, 4))
    m["st_gconv"] = np.ascontiguousarray(_fm(np.asarray(inp["state_gdn_conv"], f32)[:L][:, ss], 12).transpose(0, 1, 4, 2, 3).transpose(1, 0, 2, 3, 4))
    m["st_lconv"] = np.ascontiguousarray(_fm(np.asarray(inp["state_lru_conv"], f32)[:L][:, ss], 4).transpose(0, 1, 4, 2, 3).transpose(1, 0, 2, 3, 4))
    m["st_lru"] = np.ascontiguousarray(_fm(np.asarray(inp["state_lru"], f32)[:L][:, ss], 4).transpose(0, 1, 3, 2).transpose(1, 0, 2, 3))
    sg = np.asarray(inp["state_gdn"], f32)[:L][:, ss]
    m["st_gdn"] = np.ascontiguousarray(sg.reshape(L, len(ss), 4, 2, 64, 64).transpose(0, 1, 3, 4, 2, 5)).reshape(L, len(ss), 128, 4, 64)
    m["pt"] = np.ascontiguousarray(np.asarray(inp["page_table"], np.int32)[ss])
    m["cckv"] = np.asarray(inp["cache_mla_ckv"], f32)[:L].reshape(L, -1, 128)
    m["ckpe"] = np.asarray(inp["cache_mla_kpe"], f32)[:L].reshape(L, -1, 32)
    return m


def assemble(results, cfg, ncores):
    L, NSP, NSS, TS, SEQ = cfg.depth, cfg.nsp, cfg.nss, cfg.ts, cfg.seq
    ntp = cfg.ntokp

    def tokp(a):
        return a[:, :ntp].T.reshape(NSP, SEQ, -1)

    def toks(a):
        return a[:, ntp:].T.reshape(NSS, TS, -1)
    yp = np.concatenate([tokp(r["y"].reshape(D, -1)) for r in results], axis=0)
    ys = np.concatenate([toks(r["y"].reshape(D, -1)) for r in results], axis=0)
    outs_p, outs_s = {}, {}

    def both(key, fnp, fns):
        outs_p[key] = np.concatenate([fnp(r) for r in results], axis=1)
        outs_s[key] = np.concatenate([fns(r) for r in results], axis=1)
    both("ckv", lambda r: np.stack([tokp(r["o_ckv"][l]) for l in range(L)]), lambda r: np.stack([toks(r["o_ckv"][l]) for l in range(L)]))
    both("kpe", lambda r: np.stack([tokp(r["o_kpe"][l]) for l in range(L)]), lambda r: np.stack([toks(r["o_kpe"][l]) for l in range(L)]))

    def conv(a, nch):
        return a.transpose(0, 3, 4, 2, 1).reshape(a.shape[0], a.shape[3], a.shape[4], nch * 128)
    both("sconv", lambda r: conv(r["o_sconv"], 4)[:, :NSP], lambda r: conv(r["o_sconv"], 4)[:, NSP:])
    both("gdn_conv", lambda r: conv(r["o_gconv"], 12)[:, :NSP], lambda r: conv(r["o_gconv"], 12)[:, NSP:])
    both("lru_conv", lambda r: conv(r["o_lconv"], 4)[:, :NSP], lambda r: conv(r["o_lconv"], 4)[:, NSP:])

    def lru(a):
        return a.transpose(0, 3, 2, 1).reshape(a.shape[0], a.shape[3], 512)
    both("lru", lambda r: lru(r["o_lru"])[:, :NSP], lambda r: lru(r["o_lru"])[:, NSP:])

    def gdn(a):
        Ls, ns = a.shape[0], a.shape[1]
        return a.reshape(Ls, ns, 2, 64, 4, 64).transpose(0, 1, 4, 2, 3, 5).reshape(Ls, ns, 8, 64, 64)
    both("gdn", lambda r: gdn(r["o_gdn"])[:, :NSP], lambda r: gdn(r["o_gdn"])[:, NSP:])
    keys = ("ckv", "kpe", "sconv", "gdn_conv", "gdn", "lru_conv", "lru")
    out = [yp, ys] + [outs_p[k] for k in keys] + [outs_s[k] for k in keys]
    return tuple(np.ascontiguousarray(o, dtype=np.float32) for o in out)


def run(inp, cfg, ncores):
    sh = prep_shared(inp, cfg)
    in_maps = []
    for i in range(ncores):
        pseqs = list(range(i * cfg.nsp, (i + 1) * cfg.nsp))
        sseqs = list(range(i * cfg.nss, (i + 1) * cfg.nss))
        in_maps.append(prep_core(inp, cfg, sh, pseqs, sseqs))
    nc = build_program(cfg)
    res = run_bass_kernel_spmd(nc, in_maps, core_ids=list(range(ncores)))
    return assemble(res.results, cfg, ncores)


def kernel(**inputs):
    cfg = Cfg(depth=4, seq=2048, nsp=2, nss=16, ts=8, npages=64, npool=10240, T=256)
    return run(inputs, cfg, 8)
```
